# Optimizing a Trainium2 kernel written in Bass

```python
import jax, jax.numpy as jnp
from jax import lax
import numpy as np

D_MODEL = 1024
BATCH = 8
SEQ = 2048
DEPTH = 2
DEC_BATCH = 8
DEC_SEQ = 32
PAST_LEN = 1024

CHUNK = 64
MIX_WIDTH = D_MODEL
SGU_WIDTH = MIX_WIDTH // 2
SGU_GROUPS = 4
SGU_GROUP_DIM = SGU_WIDTH // SGU_GROUPS
SGU_CHUNK = 128
DN_WIDTH = MIX_WIDTH - SGU_WIDTH
DN_HEADS = 4
DN_HEAD_DIM = DN_WIDTH // DN_HEADS
CONV_W = 4
CONV_CH = 3 * DN_WIDTH
D_FF = 4 * D_MODEL
IN_WIDTH = 2 * SGU_WIDTH + CONV_CH + DN_WIDTH + 2 * DN_HEADS
EPS = 1e-6

kernel_name = "hybrid_sgu_gdn_stream_step"


def rmsnorm(x, g):
    xf = x.astype(jnp.float32)
    y = xf * lax.rsqrt(jnp.mean(xf * xf, axis=-1, keepdims=True) + EPS)
    return (y * g.astype(jnp.float32)).astype(x.dtype)


def l2norm(x):
    xf = x.astype(jnp.float32)
    return xf * lax.rsqrt(jnp.sum(xf * xf, axis=-1, keepdims=True) + EPS)


def spatial_gating(u, v, w_s, b_s):
    B, L, _ = u.shape
    P = SGU_CHUNK if L % SGU_CHUNK == 0 else L
    n = L // P
    blk = jnp.arange(P) // CHUNK
    mask = blk[None, :] <= blk[:, None]
    w = jnp.where(mask[None], w_s[:, :P, :P], 0).astype(v.dtype)
    vb = v.reshape(B, n, P, SGU_GROUPS, SGU_GROUP_DIM)
    bias = jnp.swapaxes(b_s[:, :P], 0, 1)[None, None, :, :, None]
    s = jnp.einsum('gpq,bnqgc->bnpgc', w, vb) + bias.astype(v.dtype)
    return u * s.reshape(B, L, SGU_WIDTH)


def causal_conv(x, buf, w):
    L = x.shape[1]
    xp = jnp.concatenate([buf.astype(x.dtype), x], axis=1)
    y = xp[:, 0:L] * w[0]
    for i in range(1, CONV_W):
        y = y + xp[:, i:i + L] * w[i]
    return y, xp[:, -(CONV_W - 1):]


def gated_delta_rule(q, k, v, beta, g, s0):
    B, L, H, Dk = q.shape
    Dv = v.shape[-1]
    C = CHUNK if L % CHUNK == 0 else L
    n = L // C

    def blk(t):
        t = t.reshape((B, n, C, H) + t.shape[3:])
        return jnp.moveaxis(t, 3, 1)

    q, k, v, beta, g = blk(q), blk(k), blk(v), blk(beta), blk(g)
    gc = jnp.cumsum(g, axis=-1)
    diff = gc[..., :, None] - gc[..., None, :]
    idx = jnp.arange(C)
    incl = idx[:, None] >= idx[None, :]
    strict = idx[:, None] > idx[None, :]
    dec_incl = jnp.exp(jnp.where(incl, diff, -jnp.inf))
    dec_strict = jnp.where(strict, dec_incl, 0.0)
    kb = k * beta[..., None]
    a = jnp.einsum('bhnid,bhnjd->bhnij', kb, k) * dec_strict
    lmat = a + jnp.eye(C, dtype=a.dtype)
    rhs = jnp.concatenate([v * beta[..., None], kb * jnp.exp(gc)[..., None]], axis=-1)
    sol = lax.linalg.triangular_solve(lmat, rhs, left_side=True, lower=True, unit_diagonal=True)
    u_c, w_c = sol[..., :Dv], sol[..., Dv:]
    attn = jnp.einsum('bhnid,bhnjd->bhnij', q, k) * dec_incl
    qg = q * jnp.exp(gc)[..., None]
    kg = k * jnp.exp(gc[..., -1:] - gc)[..., None]
    glast = jnp.exp(gc[..., -1])
    xs = (jnp.moveaxis(u_c, 2, 0), jnp.moveaxis(w_c, 2, 0), jnp.moveaxis(attn, 2, 0),
          jnp.moveaxis(qg, 2, 0), jnp.moveaxis(kg, 2, 0), jnp.moveaxis(glast, 2, 0))

    def step(s, inp):
        u_i, w_i, attn_i, qg_i, kg_i, gl_i = inp
        v_new = u_i - jnp.einsum('bhck,bhkv->bhcv', w_i, s)
        o = jnp.einsum('bhck,bhkv->bhcv', qg_i, s) + jnp.einsum('bhij,bhjv->bhiv', attn_i, v_new)
        s = s * gl_i[..., None, None] + jnp.einsum('bhck,bhcv->bhkv', kg_i, v_new)
        return s, o

    s_fin, o = lax.scan(step, s0, xs)
    o = jnp.transpose(o, (1, 0, 3, 2, 4)).reshape(B, L, H, Dv)
    return o, s_fin


def mixer(h, conv_buf, s0, w_in, sgu_norm_g, sgu_w, sgu_b, conv_w, dt_bias, a_log, dn_norm_g, w_out):
    B, L, _ = h.shape
    p = h @ w_in
    o1 = 2 * SGU_WIDTH
    o2 = o1 + CONV_CH
    o3 = o2 + DN_WIDTH
    uv, qkv, z, ab = p[..., :o1], p[..., o1:o2], p[..., o2:o3], p[..., o3:]
    uv = jax.nn.gelu(uv)
    u, v = uv[..., :SGU_WIDTH], uv[..., SGU_WIDTH:]
    v = rmsnorm(v, sgu_norm_g)
    a_out = spatial_gating(u, v, sgu_w, sgu_b)
    qkv, new_buf = causal_conv(qkv, conv_buf, conv_w)
    qkv = jax.nn.silu(qkv).reshape(B, L, 3 * DN_HEADS, DN_HEAD_DIM)
    q = l2norm(qkv[:, :, :DN_HEADS]) * (DN_HEAD_DIM ** -0.5)
    k = l2norm(qkv[:, :, DN_HEADS:2 * DN_HEADS])
    vv = qkv[:, :, 2 * DN_HEADS:].astype(jnp.float32)
    abf = ab.astype(jnp.float32)
    beta = jax.nn.sigmoid(abf[..., :DN_HEADS])
    g = -jnp.exp(a_log.astype(jnp.float32)) * jax.nn.softplus(abf[..., DN_HEADS:] + dt_bias.astype(jnp.float32))
    o, s_new = gated_delta_rule(q, k, vv, beta, g, s0.astype(jnp.float32))
    zf = jax.nn.silu(z.astype(jnp.float32)).reshape(B, L, DN_HEADS, DN_HEAD_DIM)
    b_out = (rmsnorm(o, dn_norm_g) * zf).astype(h.dtype).reshape(B, L, DN_WIDTH)
    out = jnp.concatenate([a_out, b_out], axis=-1) @ w_out
    return out, new_buf, s_new, v


def trunk(x, c, conv_state, delta_state, ada_w, ada_b, norm_mix_g, norm_ffn_g, w_in, sgu_norm_g,
          sgu_w, sgu_b, conv_w, dt_bias, a_log, dn_norm_g, w_out, w_up, w_down, final_norm_g):
    convs, deltas, vrows = [], [], []
    cs = jax.nn.silu(c)
    for l in range(DEPTH):
        mod = (cs @ ada_w[l] + ada_b[l])[:, None, :]
        sh1, sc1, gt1, sh2, sc2, gt2 = jnp.split(mod, 6, axis=-1)
        h = rmsnorm(x, norm_mix_g[l]) * (1 + sc1) + sh1
        m, nb, ns, vr = mixer(h, conv_state[l], delta_state[l], w_in[l], sgu_norm_g[l], sgu_w[l], sgu_b[l],
                              conv_w[l], dt_bias[l], a_log[l], dn_norm_g[l], w_out[l])
        x = x + gt1 * m
        h = rmsnorm(x, norm_ffn_g[l]) * (1 + sc2) + sh2
        x = x + gt2 * (jnp.square(jax.nn.relu(h @ w_up[l])) @ w_down[l])
        convs.append(nb)
        deltas.append(ns.astype(x.dtype))
        vrows.append(vr)
    y = rmsnorm(x, final_norm_g)
    return y, jnp.stack(convs), jnp.stack(deltas), jnp.stack(vrows)


def setup_inputs(seed: int = 0) -> dict:
    key = jax.random.key(seed)
    ks = jax.random.split(key, 24)
    f32 = jnp.float32
    nrm = lambda k, s, sc: jax.random.normal(k, s, f32) * sc
    dt = jnp.exp(jax.random.uniform(ks[15], (DEPTH, DN_HEADS), f32, np.log(1e-3), np.log(1e-1)))
    return {
        "x_prompt": nrm(ks[0], (BATCH, SEQ, D_MODEL), 1.0),
        "x_sample": nrm(ks[1], (DEC_BATCH, DEC_SEQ, D_MODEL), 1.0),
        "c_prompt": nrm(ks[2], (BATCH, D_MODEL), 1.0),
        "c_sample": nrm(ks[3], (DEC_BATCH, D_MODEL), 1.0),
        "state_conv": nrm(ks[4], (DEPTH, DEC_BATCH, CONV_W - 1, CONV_CH), 1.0),
        "state_delta": nrm(ks[5], (DEPTH, DEC_BATCH, DN_HEADS, DN_HEAD_DIM, DN_HEAD_DIM), 0.1),
        "ada_w": nrm(ks[6], (DEPTH, D_MODEL, 6 * D_MODEL), D_MODEL ** -0.5),
        "ada_b": nrm(ks[7], (DEPTH, 6 * D_MODEL), 0.02),
        "norm_mix_g": 1.0 + nrm(ks[8], (DEPTH, D_MODEL), 0.02),
        "norm_ffn_g": 1.0 + nrm(ks[9], (DEPTH, D_MODEL), 0.02),
        "w_in": nrm(ks[10], (DEPTH, D_MODEL, IN_WIDTH), D_MODEL ** -0.5),
        "sgu_norm_g": 1.0 + nrm(ks[11], (DEPTH, SGU_WIDTH), 0.02),
        "sgu_w": nrm(ks[12], (DEPTH, SGU_GROUPS, SGU_CHUNK, SGU_CHUNK), SGU_CHUNK ** -0.5),
        "sgu_b": nrm(ks[13], (DEPTH, SGU_GROUPS, SGU_CHUNK), 0.02),
        "conv_w": nrm(ks[14], (DEPTH, CONV_W, CONV_CH), CONV_W ** -0.5),
        "dt_bias": dt + jnp.log(-jnp.expm1(-dt)),
        "a_log": jnp.log(jax.random.uniform(ks[16], (DEPTH, DN_HEADS), f32, 1.0, 16.0)),
        "dn_norm_g": 1.0 + nrm(ks[17], (DEPTH, DN_HEAD_DIM), 0.02),
        "w_out": nrm(ks[18], (DEPTH, MIX_WIDTH, D_MODEL), MIX_WIDTH ** -0.5),
        "w_up": nrm(ks[19], (DEPTH, D_MODEL, D_FF), D_MODEL ** -0.5),
        "w_down": nrm(ks[20], (DEPTH, D_FF, D_MODEL), D_FF ** -0.5),
        "final_norm_g": 1.0 + nrm(ks[21], (D_MODEL,), 0.02),
    }


def reference(x_prompt, x_sample, c_prompt, c_sample, state_conv, state_delta, ada_w, ada_b, norm_mix_g,
              norm_ffn_g, w_in, sgu_norm_g, sgu_w, sgu_b, conv_w, dt_bias, a_log, dn_norm_g, w_out, w_up,
              w_down, final_norm_g):
    conv0 = jnp.zeros((DEPTH, x_prompt.shape[0], CONV_W - 1, CONV_CH), x_prompt.dtype)
    delta0 = jnp.zeros((DEPTH, x_prompt.shape[0], DN_HEADS, DN_HEAD_DIM, DN_HEAD_DIM), jnp.float32)
    y_prompt, prompt_conv, prompt_delta, _ = trunk(
        x_prompt, c_prompt, conv0, delta0, ada_w, ada_b, norm_mix_g, norm_ffn_g, w_in, sgu_norm_g,
        sgu_w, sgu_b, conv_w, dt_bias, a_log, dn_norm_g, w_out, w_up, w_down, final_norm_g)
    y_sample, sample_conv, sample_delta, sample_sgu_v = trunk(
        x_sample, c_sample, state_conv, state_delta, ada_w, ada_b, norm_mix_g, norm_ffn_g, w_in, sgu_norm_g,
        sgu_w, sgu_b, conv_w, dt_bias, a_log, dn_norm_g, w_out, w_up, w_down, final_norm_g)
    return (y_prompt, y_sample, prompt_conv, prompt_delta, sample_conv, sample_delta, sample_sgu_v)
```

```python
import os
import numpy as np
import ml_dtypes
from contextlib import ExitStack
import concourse.bass as bass
import concourse.mybir as mybir
from concourse.bass_utils import run_bass_kernel_spmd

F32, BF16 = mybir.dt.float32, mybir.dt.bfloat16
AF = mybir.ActivationFunctionType
ALU = mybir.AluOpType

D = 1024
LS = 32
DEPTH = 2
INW = 3080
DFF = 4096
EPS = 1e-6
BIG = 1.0e30
KDMA = 6
SAME_ENGINE_SYNC = True
FSTEPS = 1
NO_REORDER = False


class Buf:
    __slots__ = ("name", "w", "r", "rw", "rr")

    def __init__(self, name):
        self.name = name
        self.w = {}
        self.r = {}
        self.rw = []
        self.rr = []


class _Proxy:
    def __init__(self):
        self.call = None

    def __getattr__(self, name):
        def f(*a, **kw):
            self.call = (name, a, kw)
            return self
        return f


def _free_size(call):
    name, a, kw = call
    ap = kw.get("out", a[0] if a else None)
    try:
        shp = ap.shape
        n = 1
        for d in shp[1:]:
            n *= int(d)
        return max(n, 1)
    except Exception:
        return 128


def _cost(E, call):
    n = _free_size(call)
    if E == "pe":
        return (50.0 + 0.4 * n) * float(os.environ.get("SC_PE", 1.0))
    if E == "dve":
        return (n + 151) / 0.96 * float(os.environ.get("SC_DVE", 1.0))
    if E == "act":
        return (130.0 + 0.75 * n) * float(os.environ.get("SC_ACT", 1.0))
    if E == "pool":
        return 290.0 + 1.23 * n
    return 100.0


class Ctx:
    def __init__(self, nc, es):
        self.nc = nc
        self.es = es
        self.sems = {}
        self.eng = {}
        for name, h in (("pe", nc.tensor), ("act", nc.scalar), ("dve", nc.vector),
                        ("pool", nc.gpsimd), ("sp", nc.sync)):
            self.sems["e_" + name] = es.enter_context(nc.semaphore("se_" + name))
            self.eng[name] = dict(h=h, cnt=0, seen={})
        self.dq = {}
        for q in ("sp", "pool"):
            for i in range(KDMA):
                self.sems[f"d_{q}{i}"] = es.enter_context(nc.semaphore(f"sd_{q}{i}"))
            self.dq[q] = dict(cnts=[0] * KDMA, n=0)
        self.nbuf = 0
        self.recording = False
        self.rec = []

    def rec_begin(self):
        self.recording = True
        self.rec = []

    def rec_flush(self):
        rec = self.rec
        self.recording = False
        n = len(rec)
        if n == 0:
            return
        unit_of = [0] * n
        units = []
        cur = None
        for i, nd in enumerate(rec):
            if nd["E"] == "pe" and nd["kind"] == "op":
                if cur is None:
                    cur = len(units)
                    units.append(dict(E="pe", ops=[], dur=0.0, dma=False))
                units[cur]["ops"].append(i)
                units[cur]["dur"] += nd["dur"]
                unit_of[i] = cur
                if nd["sig"]:
                    cur = None
            else:
                u = len(units)
                units.append(dict(E=nd["E"], ops=[i], dur=nd["dur"], dma=(nd["kind"] == "dma")))
                unit_of[i] = u
        assert cur is None, "segment ends inside an unsignaled PE group"
        nu = len(units)
        udeps = [set() for _ in range(nu)]
        for i, nd in enumerate(rec):
            u = unit_of[i]
            for d in nd["deps"]:
                du = unit_of[d]
                if du != u:
                    udeps[u].add(du)
        succ = [[] for _ in range(nu)]
        indeg = [0] * nu
        for u in range(nu):
            indeg[u] = len(udeps[u])
            for d in udeps[u]:
                succ[d].append(u)
        prio = [0.0] * nu
        for u in range(nu - 1, -1, -1):
            m = 0.0
            for v in succ[u]:
                if prio[v] > m:
                    m = prio[v]
            prio[u] = units[u]["dur"] + m
        LAT, SLAT, WIN = 150.0, 60.0, 250.0
        etime = {k: 0.0 for k in self.eng}
        fin = [0.0] * nu
        ready = [u for u in range(nu) if indeg[u] == 0]
        order = []
        while ready:
            best = None
            ests = []
            mn = None
            for u in ready:
                E = units[u]["E"]
                st = etime[E]
                for d in udeps[u]:
                    t = fin[d] + (SLAT if units[d]["E"] == E else LAT)
                    if t > st:
                        st = t
                ests.append((u, st))
                if mn is None or st < mn:
                    mn = st
            for u, st in ests:
                if st <= mn + WIN:
                    key = (prio[u], -u)
                    if best is None or key > best[0]:
                        best = (key, u, st)
            _, u, st = best
            E = units[u]["E"]
            if units[u]["dma"]:
                etime[E] = st + units[u]["dur"]
                fin[u] = st + 2000.0
            else:
                etime[E] = st + units[u]["dur"]
                fin[u] = etime[E]
            order.append(u)
            ready.remove(u)
            for v in succ[u]:
                indeg[v] -= 1
                if indeg[v] == 0:
                    ready.append(v)
        assert len(order) == nu
        if NO_REORDER:
            order = list(range(nu))
        self.sim_time = max(fin) if fin else 0.0
        ticket = [None] * n
        for u in order:
            for i in units[u]["ops"]:
                nd = rec[i]
                E = nd["E"]
                e = self.eng[E]
                deps = self._deps(nd["reads"], nd["writes"])
                for d in nd["deps"]:
                    deps.append(ticket[d])
                if nd["kind"] == "dma":
                    q = self.dq[E]
                    idx = q["n"] % KDMA
                    q["n"] += 1
                    key = f"d_{E}{idx}"
                    if q["cnts"][idx] > 0:
                        deps.append((key, q["cnts"][idx]))
                    self._wait(E, deps)
                    out, in_ = nd["call"]
                    ins = e["h"].dma_start(out=out, in_=in_)
                    q["cnts"][idx] += 16
                    ins.then_inc(self.sems[key], 16)
                    ticket[i] = (key, q["cnts"][idx])
                else:
                    if E == "pe" or not SAME_ENGINE_SYNC:
                        deps = [d for d in deps if d[0] != "e_" + E]
                    self._wait(E, deps)
                    name, a, kw = nd["call"]
                    ins = getattr(e["h"], name)(*a, **kw)
                    ticket[i] = ("e_" + E, e["cnt"] + 1)
                    if nd["sig"]:
                        ins.then_inc(self.sems["e_" + E], 1)
                        e["cnt"] += 1
        for i, nd in enumerate(rec):
            self._reg(ticket[i], nd["reads"], nd["writes"])
            for b in nd["reads"]:
                b.rw = []
                b.rr = []
            for b in nd["writes"]:
                b.rw = []
                b.rr = []
        self.rec = []

    def buf(self, name=None):
        self.nbuf += 1
        return Buf(name or f"b{self.nbuf}")

    def sb(self, name, shape, dt):
        t = self.es.enter_context(self.nc.sbuf_tensor(name, list(shape), dt))
        return t

    def _wait(self, E, deps):
        e = self.eng[E]
        need = {}
        for k, v in deps:
            if v > need.get(k, 0):
                need[k] = v
        for k, v in need.items():
            if e["seen"].get(k, 0) >= v:
                continue
            e["h"].wait_ge(self.sems[k], v)
            e["seen"][k] = v

    @staticmethod
    def _deps(reads, writes):
        deps = []
        for b in reads:
            deps.extend(b.w.items())
        for b in writes:
            deps.extend(b.w.items())
            deps.extend(b.r.items())
        return deps

    @staticmethod
    def _reg(tk, reads, writes):
        k, v = tk
        for b in reads:
            if b.r.get(k, 0) < v:
                b.r[k] = v
        for b in writes:
            if b.w.get(k, 0) < v:
                b.w[k] = v

    def _rec_add(self, node, reads, writes):
        i = len(self.rec)
        deps = set()
        for b in reads:
            deps.update(b.rw)
        nw = os.environ.get("SIM_NOWAR", "")
        for b in writes:
            deps.update(b.rw)
            if not (nw == "1" or (nw and any(b.name.startswith(x) for x in nw.split(",") if x))):
                deps.update(b.rr)
        node["deps"] = deps
        node["reads"] = list(reads)
        node["writes"] = list(writes)
        self.rec.append(node)
        for b in reads:
            b.rr.append(i)
        for b in writes:
            b.rw = [i]
            b.rr = []

    def op(self, E, fn, reads=(), writes=(), sig=True):
        if self.recording:
            px = _Proxy()
            fn(px)
            self._rec_add(dict(E=E, kind="op", call=px.call, sig=sig, dur=_cost(E, px.call)), reads, writes)
            return None
        e = self.eng[E]
        deps = self._deps(reads, writes)
        if E == "pe" or not SAME_ENGINE_SYNC:
            deps = [d for d in deps if d[0] != "e_" + E]
        self._wait(E, deps)
        ins = fn(e["h"])
        tk = ("e_" + E, e["cnt"] + 1)
        if sig:
            ins.then_inc(self.sems["e_" + E], 1)
            e["cnt"] += 1
        self._reg(tk, reads, writes)
        return ins

    def dma(self, Q, out, in_, reads=(), writes=()):
        if self.recording:
            self._rec_add(dict(E=Q, kind="dma", call=(out, in_), sig=True, dur=100.0), reads, writes)
            return
        q = self.dq[Q]
        e = self.eng[Q]
        idx = q["n"] % KDMA
        q["n"] += 1
        key = f"d_{Q}{idx}"
        deps = self._deps(reads, writes)
        if q["cnts"][idx] > 0:
            deps.append((key, q["cnts"][idx]))
        self._wait(Q, deps)
        ins = e["h"].dma_start(out=out, in_=in_)
        q["cnts"][idx] += 16
        ins.then_inc(self.sems[key], 16)
        self._reg((key, q["cnts"][idx]), reads, writes)

    def barrier(self):
        deps = []
        for i in range(KDMA):
            c = self.dq["sp"]["cnts"][i]
            if c:
                deps.append((f"d_sp{i}", c))
        for name in ("pe", "act", "dve", "pool", "sp"):
            c = self.eng[name]["cnt"]
            if c:
                deps.append(("e_" + name, c))
        for name in ("pe", "act", "dve", "pool", "sp"):
            self._wait(name, [d for d in deps if d[0] != "e_" + name])

    def finish(self):
        deps = []
        for q in ("sp", "pool"):
            for i in range(KDMA):
                c = self.dq[q]["cnts"][i]
                if c:
                    deps.append((f"d_{q}{i}", c))
        for name in ("pe", "act", "dve", "pool"):
            c = self.eng[name]["cnt"]
            if c:
                deps.append(("e_" + name, c))
        self._wait("sp", deps)


def build(L=2048, dbg=None):
    NTOK = L + LS
    NB = L // 128
    nc = bass.Bass("TRN2", target_bir_lowering=False)
    dbg = dbg or {}

    def din(name, shape, dt=F32):
        return nc.dram_tensor(name, list(shape), dt, kind="ExternalInput").ap()

    def dout(name, shape, dt=F32):
        return nc.dram_tensor(name, list(shape), dt, kind="ExternalOutput").ap()

    x_p = din("x_p", [L, D]); x_s = din("x_s", [LS, D])
    c_ps = din("c_ps", [16, 128])
    st_conv = din("st_conv", [DEPTH, 36, 128])
    st_delta = din("st_delta", [DEPTH, 4, 128, 128])
    ada_w = din("ada_w", [DEPTH, D, 6 * D])
    rowsA = din("rowsA", [DEPTH, 113, 128])
    w_in = din("w_in", [DEPTH, D, INW])
    sgu_norm_g = din("sgu_norm_g", [DEPTH, 512])
    sgu_w = din("sgu_w", [DEPTH, 4, 128, 128])
    sgu_b = din("sgu_b", [DEPTH, 512])
    dt_bias = din("dt_bias", [DEPTH, 4]); a_log = din("a_log", [DEPTH, 4])
    w_out = din("w_out", [DEPTH, D, D]); w_up = din("w_up", [DEPTH, D, DFF]); w_down = din("w_down", [DEPTH, DFF, D])
    fin_g = din("fin_g", [1, D])
    k_ident = din("k_ident", [128, 128]); k_masks = din("k_masks", [128, 3, 128])
    k_sgum = din("k_sgum", [128, 128])
    k_cm = din("k_cm", [128, 6, 128], BF16)

    y_p = dout("y_p", [L, D]); y_s = dout("y_s", [LS, D])
    o_pconv = dout("o_pconv", [DEPTH, 3, 1536]); o_pdelta = dout("o_pdelta", [DEPTH, 4, 128, 128])
    o_sconv = dout("o_sconv", [DEPTH, 3, 1536]); o_sdelta = dout("o_sdelta", [DEPTH, 4, 128, 128])
    o_sgu = dout("o_sgu", [DEPTH, LS, 512])
    dbg_outs = {k: dout("dbg_" + k, shp) for k, shp in dbg.items()}

    es = ExitStack()
    with es:
        cx = Ctx(nc, es)
        sb = cx.sb
        B = cx.buf

        psf = es.enter_context(nc.psum_tensor("psf", [128, 7, 512], F32))
        psb = es.enter_context(nc.psum_tensor("psb", [128, 1024], BF16))
        pbank = [B(f"psbank{i}") for i in range(7)]
        b_psb = B("psb")
        pstate = dict(n=0)

        rings = {"X": (0, 7), "F": (0, 3), "B": (3, 4)}
        rpos_ps = {"X": 0, "F": 0, "B": 0}

        def PS(nb=1, ring="X"):
            base, size = rings[ring]
            i = rpos_ps[ring] % size
            if i + nb > size:
                i = 0
            rpos_ps[ring] = i + nb
            return base + i, pbank[base + i:base + i + nb]

        ident_f = sb("ident_f", [128, 128], F32); b_ident = B()
        ident_b = sb("ident_b", [128, 128], BF16)
        masks = sb("masks", [128, 3, 128], F32); b_masks = B()
        sgum = sb("sgum", [128, 128], F32)
        cmask = sb("cmask", [128, 6, 128], BF16)
        ones_b = sb("ones_b", [128, 128], BF16)
        ones_f = sb("ones_f", [128, 128], F32)
        b_const = B("const")
        cx.dma("sp", ident_f[:], k_ident, writes=[b_const])
        cx.dma("sp", masks[:], k_masks, writes=[b_const])
        cx.dma("sp", sgum[:], k_sgum, writes=[b_const])
        cx.dma("sp", cmask[:], k_cm, writes=[b_const])
        cx.op("dve", lambda h: h.memset(ones_b[:], 1.0), writes=[b_const])
        cx.op("dve", lambda h: h.memset(ones_f[:], 1.0), writes=[b_const])
        cx.op("dve", lambda h: h.tensor_copy(out=ident_b[:], in_=ident_f[:]), reads=[b_const], writes=[b_const])
        NUI = masks[:, 0, :]; NUS = masks[:, 1, :]; LSB = masks[:, 2, :]

        def bc4(ap2d, n=4, w=128, p=128):
            return ap2d.unsqueeze(1).to_broadcast([p, n, w])

        xT = sb("xT", [128, 8, NTOK], F32)
        b_x = [B(f"x{i}") for i in range(NB + 1)]
        WA = sb("WA", [128, 8 * INW + 8 * D], BF16)
        win = WA[:, 0:8 * INW].rearrange("p (k n) -> p k n", k=8)
        wout = WA[:, 8 * INW:8 * INW + 8 * D].rearrange("p (k n) -> p k n", k=8)
        SLOT = 8192
        b_slot = [B(f"slot{i}") for i in range(4)]
        slot_off = [8 * INW, 0, SLOT, 2 * SLOT]
        slot_buf = [[b_slot[0]], [b_slot[1]], [b_slot[2]], [b_slot[3]]]
        L_win = [b_slot[1], b_slot[2], b_slot[3]]; L_wout = [b_slot[0]]

        RBYTES = int(os.environ.get('RKB', 61)) * 1024
        R = sb("R", [128, RBYTES // 4], F32)
        rpos = dict(a=0, b=0)

        def RA(name, free_shape, dt, which="a"):
            n = int(np.prod(free_shape))
            words = (n * (2 if dt == BF16 else 4) + 3) // 4
            words = (words + 7) // 8 * 8
            o = rpos[which]
            rpos[which] = o + words
            assert rpos[which] * 4 <= RBYTES, (name, rpos[which] * 4)
            ap = R[:, o:o + words]
            if dt == BF16:
                ap = ap.bitcast(BF16)[:, 0:n]
            else:
                ap = ap[:, 0:n]
            if len(free_shape) == 2:
                ap = ap.rearrange("p (a b) -> p a b", a=free_shape[0])
            elif len(free_shape) == 3:
                ap = ap.rearrange("p (a b c) -> p a b c", a=free_shape[0], b=free_shape[1])
            return ap

        pcols = sb("pcols", [128, 113], F32); b_pcols = B("pcols")
        rowsA_sb = sb("rowsA_sb", [113, 128], F32); b_rowsA = B()
        rowsB_sb = sb("rowsB_sb", [52, 128], F32); b_rowsB = B()
        colsB = sb("colsB", [128, 52], F32); b_colsB = B("colsB")
        csT = sb("csT", [128, 2, 8], BF16); b_cs = B("cs")
        modT = sb("modT", [128, 48, 2], F32); b_mod = B("mod")
        lay = sb("lay", [128, 6, 2, 8], F32); b_lay = B("lay")
        WmT = sb("WmT", [128, 4, 128], BF16); b_wm = B("wm")
        BSRB = sb("BSRB", [128, 4, 128], BF16)
        SGBC = sb("SGBC", [128, 512], F32)
        hd = sb("hd", [128, 16], F32)
        b_lp = B("layerparams")
        FING = WA[:, 0:2 * D].bitcast(F32); b_fing = B()
        S = sb("S", [128, 4, 128], F32); S_bf = sb("S_bf", [128, 4, 128], BF16); b_S = B("S"); b_Sbf = B("Sbf")
        XP = sb("XP", [128, 12, 131], BF16); b_XP = B("XP")

        ACT = lambda fn, **kw: cx.op("act", fn, **kw)
        DVE = lambda fn, **kw: cx.op("dve", fn, **kw)
        POOL = lambda fn, **kw: cx.op("pool", fn, **kw)
        PE = lambda fn, **kw: cx.op("pe", fn, **kw)

        def mm(out, lhsT, rhs, start, stop, reads, writes, sig):
            PE(lambda h: h.matmul(out, lhsT=lhsT, rhs=rhs, start=start, stop=stop),
               reads=reads, writes=writes, sig=sig)

        def tr(out, in_, K, reads, writes, sig, dt=F32):
            idn = (ident_f if dt == F32 else ident_b)[0:K, 0:K]
            PE(lambda h: h.transpose(out, in_, idn), reads=list(reads) + [b_const], writes=writes, sig=sig)

        def bmerge(dst, src):
            for k_, v_ in src.w.items():
                if dst.w.get(k_, 0) < v_:
                    dst.w[k_] = v_
            for k_, v_ in src.r.items():
                if dst.r.get(k_, 0) < v_:
                    dst.r[k_] = v_
            dst.rw = list(set(dst.rw) | set(src.rw))
            dst.rr = list(set(dst.rr) | set(src.rr))

        def dump(name, ap, reads):
            if name in dbg_outs:
                cx.dma("sp", dbg_outs[name], ap, reads=reads)

        hT = RA("hT", [8, 128], BF16); b_hT = B("hT")
        QA = RA("QA", [12, 128], F32); b_QA = B("QA")
        b_QAg = [B("QAg0"), B("QAg1"), B("QAg2")]
        QKN = RA("QKN", [8, 128], BF16); b_QKN = B("QKN")
        vTb = RA("vTb", [4, 128], BF16); b_vT = B("vT")
        gcRB = RA("gcRB", [4, 128], F32); b_gc = B("gcRB")
        BMt = RA("BMt", [4, 128], BF16); b_BM = B("BMt")
        qgT_l = [RA("qgT" + str(i), [4, 128], BF16) for i in range(2)]; b_qg_l = [B("qgT" + str(i)) for i in range(2)]
        TM = RA("TM", [16], F32); b_TM = B("TM")
        TM2_l = [RA("TM2" + str(i), [24], F32) for i in range(2)]; b_TM2_l = [B("TM2" + str(i)) for i in range(2)]
        absb = RA("absb", [128], F32); b_absb = B("absb")
        PQ = RA("PQ", [4, 2, 128], BF16); b_PQ = B("PQ")
        PTQ = RA("PTQ", [4, 2, 128], BF16); b_PTQ = B("PTQ")
        NM_l = [RA("NM" + str(i), [4, 2, 128], BF16) for i in range(2)]; b_NM_l = [B("NM" + str(i)) for i in range(2)]
        attnT_l = [RA("attnT" + str(i), [4, 128], BF16) for i in range(2)]; b_attn_l = [B("attnT" + str(i)) for i in range(2)]
        kbg_l = [RA("kbg" + str(i), [4, 128], BF16) for i in range(2)]; b_kbg_l = [B("kbg" + str(i)) for i in range(2)]
        kg_l = [RA("kg" + str(i), [4, 128], BF16) for i in range(2)]; b_kg_l = [B("kg" + str(i)) for i in range(2)]
        vb_l = [RA("vb" + str(i), [4, 128], BF16) for i in range(2)]; b_vb_l = [B("vb" + str(i)) for i in range(2)]
        wTb = RA("wTb", [4, 128], BF16); b_wT = B("wT")
        vnew = RA("vnew", [4, 128], BF16); b_vn = B("vnew")
        zs_l = [RA("zs" + str(i), [4, 128], BF16) for i in range(2)]; b_zs_l = [B("zs" + str(i)) for i in range(2)]
        uT = RA("uT", [4, 128], BF16); b_uT = B("uT")
        vnb = RA("vnb", [512], BF16); b_vnb = B("vnb")
        ABT_l = [RA("ABT" + str(i), [8, 128], BF16) for i in range(2)]; b_AB_l = [B("ABT" + str(i)) for i in range(2)]
        sml = RA("sml", [8], F32); b_sml = B("sml")
        NTF = int(os.environ.get('NTF', 7)); NTB = int(os.environ.get('NTB', 2)); NTMP = NTF + NTB
        tmp_o = rpos["a"]
        tmpF = [RA(f"tmpF{i}", [512], F32) for i in range(NTMP)]
        b_tmp = [B(f"tmp{i}") for i in range(NTMP)]
        xstage = R[:, tmp_o:tmp_o + 1024]
        adaSt = [R[:, i * 1024:(i + 1) * 1024].bitcast(BF16).rearrange("p (k n) -> p k n", k=8) for i in range(2)]
        b_adaSt = [B("adaSt0"), B("adaSt1")]
        adaSt8 = [R[:, i * 1024:(i + 1) * 1024].bitcast(BF16).rearrange("p (k n) -> p k n", k=8) for i in range(8)]
        b_adaSt8 = [B(f"adaSt8_{i}") for i in range(8)]
        assert 8 * 1024 <= tmp_o
        tstate = dict(n=0)
        b_xst = [b_tmp[0], b_tmp[1]]

        tpos = {"F": 0, "B": 0, "b": 0}

        def TMP(pool="F"):
            if tstate.get("mode", "a") == "b":
                pool_b = tstate.get("bpool")
                if pool_b:
                    i = tpos["b"] % len(pool_b)
                    tpos["b"] += 1
                    return pool_b[i]
                i = tpos["b"] % 4
                tpos["b"] += 1
                return tmpFb[i], b_tmpb[i]
            if pool == "F":
                i = tpos["F"] % NTF
            else:
                i = NTF + tpos["B"] % NTB
            tpos[pool] += 1
            return tmpF[i], b_tmp[i]

        hTall = RA("hTall", [8, NTOK], BF16, "b"); b_hall_l = [B(f"hTall{i}") for i in range(NB + 1)]
        hidf = [RA(f"hidf{i}", [512], F32, "b") for i in range(2)]; b_hidf = [B(), B()]
        hidb = [RA(f"hidb{i}", [4, 512], BF16, "b") for i in range(2)]; b_hidb = [B(), B()]
        adaStB = RA("adaStB", [8, 256], BF16, "b"); b_adaStB = B("adaStB")
        tmpFb = [RA(f"tmpFb{i}", [512], F32, "b") for i in range(4)]
        b_tmpb = [B(f"tmpb{i}") for i in range(4)]

        def load_x(bi):
            tw = 128 if bi < NB else LS
            src = x_p[bi * 128:(bi + 1) * 128, :] if bi < NB else x_s
            xs_, bxs_ = (xstage, b_xst) if bi % 2 == 0 else (xstage2, b_xst2)
            cx.dma("sp", xs_[0:tw, :], src, writes=bxs_)
            for half in range(2):
                bk, pb = PS(1)
                for c in range(4):
                    cc = half * 4 + c
                    tr(psf[:, bk, c * 128:c * 128 + tw], xs_[0:tw, cc * 128:(cc + 1) * 128], tw,
                       reads=bxs_, writes=pb, sig=(c == 3))
                o = psf[:, bk, :].rearrange("p (c t) -> p c t", c=4)[:, :, 0:tw]
                ACT(lambda h: h.copy(out=xT[:, half * 4:half * 4 + 4, bi * 128:bi * 128 + tw], in_=o),
                    reads=pb, writes=[b_x[bi]])

        cx.dma("sp", rowsB_sb[36:52, :], c_ps, writes=[b_rowsB])

        def layer_params(l):
            cx.dma("sp", rowsA_sb[:], rowsA[l], writes=[b_rowsA])
            cx.dma("sp", rowsB_sb[0:36, :], st_conv[l], writes=[b_rowsB])
            bk, pb = PS(1)
            tr(psf[:, bk, 0:113], rowsA_sb[:], 113, reads=[b_rowsA], writes=pb, sig=True)
            DVE(lambda h: h.tensor_copy(out=pcols[:], in_=psf[:, bk, 0:113]), reads=pb, writes=[b_pcols])
            bk, pb = PS(1)
            tr(psf[:, bk, 0:52], rowsB_sb[:], 52, reads=[b_rowsB], writes=pb, sig=True)
            DVE(lambda h: h.tensor_copy(out=colsB[:], in_=psf[:, bk, 0:52]), reads=pb, writes=[b_colsB])
            if l == 0:
                ACT(lambda h: h.activation(out=csT[:].rearrange("p s k -> p (s k)"), in_=colsB[:, 36:52], func=AF.Silu),
                    reads=[b_colsB], writes=[b_cs])
            cx.dma("sp", hd[:, 0:4], dt_bias[l:l + 1, :].partition_broadcast(128), writes=[b_lp])
            cx.dma("sp", hd[:, 4:8], a_log[l:l + 1, :].partition_broadcast(128), writes=[b_lp])
            ACT(lambda h: h.activation(out=hd[:, 8:12], in_=hd[:, 4:8], func=AF.Exp), reads=[b_lp], writes=[b_lp])
            DVE(lambda h: h.tensor_scalar(out=hd[:, 8:12], in0=hd[:, 8:12], scalar1=-1.0, scalar2=None, op0=ALU.mult),
                reads=[b_lp], writes=[b_lp])
            cx.dma("sp", SGBC[:], sgu_norm_g[l:l + 1, :].partition_broadcast(128), writes=[b_lp])
            cx.dma("pool", BSRB[:].rearrange("p g q -> p (g q)"), sgu_b[l:l + 1, :].partition_broadcast(128), writes=[b_lp])
            t0, bt0 = TMP()
            cx.dma("sp", t0.rearrange("p (g q) -> p g q", g=4), sgu_w[l].rearrange("g p q -> p g q"), writes=[bt0])
            bk, pb = PS(1)
            for g in range(4):
                tr(psf[:, bk, g * 128:(g + 1) * 128], t0[:, g * 128:(g + 1) * 128], 128, reads=[bt0], writes=pb, sig=(g == 3))
            DVE(lambda h: h.tensor_tensor(out=WmT[:], in0=psf[:, bk, :].rearrange("p (g q) -> p g q", g=4),
                                          in1=bc4(sgum[:]), op=ALU.mult), reads=pb + [b_const], writes=[b_wm])

        def ada_chunk(l, ch, st, bst):
            cx.dma("pool", st, ada_w[l].rearrange("(k p) n -> p k n", p=128)[:, :, ch * 256:(ch + 1) * 256], writes=[bst])
            bk, pb = PS(1)
            for j in range(2):
                for k in range(8):
                    mm(psf[:, bk, j * 2:j * 2 + 2], st[:, k, j * 128:(j + 1) * 128], csT[:, :, k],
                       k == 0, k == 7, reads=[bst, b_cs], writes=pb, sig=(k == 7 and j == 1))
            DVE(lambda h: h.tensor_tensor(out=modT[:, ch * 2:ch * 2 + 2, :],
                                          in0=psf[:, bk, 0:4].rearrange("p (j s) -> p j s", s=2),
                                          in1=pcols[:, 16 + ch * 2:16 + ch * 2 + 2].unsqueeze(2).to_broadcast([128, 2, 2]),
                                          op=ALU.add), reads=pb + [b_pcols], writes=[b_mod])

        def ada(l):
            for ch in range(24):
                ada_chunk(l, ch, adaSt8[ch % 8], b_adaSt8[ch % 8])

        def layer_cols(l):
            for s in range(2):
                for (dst, jsc, jsh, jgt, gof) in ((0, 8, 0, 16, 0), (3, 32, 24, 40, 8)):
                    DVE(lambda h: h.scalar_tensor_tensor(out=lay[:, dst, s, :], in0=modT[:, jsc:jsc + 8, s], scalar=1.0,
                                                         in1=pcols[:, gof:gof + 8], op0=ALU.add, op1=ALU.mult),
                        reads=[b_mod, b_pcols], writes=[b_lay])
                    DVE(lambda h: h.tensor_copy(out=lay[:, dst + 1, s, :], in_=modT[:, jsh:jsh + 8, s]), reads=[b_mod], writes=[b_lay])
                    DVE(lambda h: h.tensor_copy(out=lay[:, dst + 2, s, :], in_=modT[:, jgt:jgt + 8, s]), reads=[b_mod], writes=[b_lay])

        def win_chunk_slots(k):
            lo, hi = k * INW, (k + 1) * INW
            sl = []
            for j in (1, 2, 3):
                a0, a1 = (j - 1) * SLOT, j * SLOT
                if lo < a1 and hi > a0:
                    sl.append(j)
            if hi > 3 * SLOT and 3 not in sl:
                sl.append(3)
            return sl

        def load_win_chunks(l, ks):
            for k in ks:
                cx.dma("pool", win[:, k:k + 1, :], w_in[l].rearrange("(k p) n -> p k n", p=128)[:, k:k + 1, :],
                       writes=[b_slot[j] for j in win_chunk_slots(k)])

        def load_wout(l):
            for k0 in range(0, 8, 4):
                cx.dma("pool", wout[:, k0:k0 + 4, :], w_out[l].rearrange("(k p) n -> p k n", p=128)[:, k0:k0 + 4, :],
                       writes=L_wout)

        def load_wA(l):
            load_win_chunks(l, range(8))
            load_wout(l)

        def rstd_from_ps(ps_ap, width, scale, bias_ln, dst, rd, wr):
            ACT(lambda h: h.activation(out=dst, in_=ps_ap, func=AF.Ln, bias=EPS, scale=scale), reads=rd, writes=wr)
            ACT(lambda h: h.activation(out=dst, in_=dst, func=AF.Exp, scale=-0.5, bias=bias_ln), reads=wr, writes=wr)

        def norm_block(col0, tw, s, a_idx, dst, b_dst, xbufs, ring="X"):
            sq, bsq = TMP(); sq2, bsq2 = TMP()
            sqv = sq.bitcast(BF16).rearrange("p (c t) -> p c t", c=8)
            ACT(lambda h: h.activation(out=sqv[:, :, 0:tw], in_=xT[:, :, col0:col0 + tw], func=AF.Square),
                reads=xbufs, writes=[bsq])
            bk, pb = PS(1, ring)
            for c in range(8):
                mm(psf[:, bk, 0:tw], ones_b[:], sqv[:, c, 0:tw], c == 0, c == 7, reads=[bsq, b_const], writes=pb, sig=(c == 7))
            rstd_from_ps(psf[:, bk, 0:tw], tw, 1.0 / D, 0.0, sq2[:, 0:tw], pb, [bsq2])
            t3, bt3 = TMP(); t4, bt4 = TMP()
            rbc = sq2[:, 0:tw].unsqueeze(1).to_broadcast([128, 4, tw])
            for hf, (tq, bq) in enumerate(((t3, bt3), (t4, bt4))):
                tqv = tq.rearrange("p (c t) -> p c t", c=4)[:, :, 0:tw]
                DVE(lambda h: h.tensor_tensor(out=tqv, in0=xT[:, hf * 4:hf * 4 + 4, col0:col0 + tw], in1=rbc, op=ALU.mult),
                    reads=xbufs + [bsq2], writes=[bq])
            for c in range(8):
                tt = (t3 if c < 4 else t4)[:, (c % 4) * 128:(c % 4) * 128 + tw]
                bb = bt3 if c < 4 else bt4
                if c % 2 == 0:
                    ACT(lambda h: h.activation(out=dst[:, c, 0:tw], in_=tt, func=AF.Identity, bias=lay[:, a_idx + 1, s, c:c + 1],
                                               scale=lay[:, a_idx, s, c:c + 1]), reads=[bb, b_lay], writes=[b_dst])
                else:
                    DVE(lambda h: h.tensor_scalar(out=dst[:, c, 0:tw], in0=tt, scalar1=lay[:, a_idx, s, c:c + 1],
                                                  scalar2=lay[:, a_idx + 1, s, c:c + 1], op0=ALU.mult, op1=ALU.add),
                        reads=[bb, b_lay], writes=[b_dst])

        def phaseA_block(l, bi):
            smp = bi == NB
            tw = LS if smp else 128
            s = 1 if smp else 0
            col0 = bi * 128
            xb = [b_x[bi]]
            rdw = L_win + [b_hT]
            par = bi % 2
            NM, b_NM = NM_l[par], b_NM_l[par]; attnT, b_attn = attnT_l[par], b_attn_l[par]
            kbg, b_kbg = kbg_l[par], b_kbg_l[par]; kg, b_kg = kg_l[par], b_kg_l[par]; vb, b_vb = vb_l[par], b_vb_l[par]
            qgT, b_qg = qgT_l[par], b_qg_l[par]; zs, b_zs = zs_l[par], b_zs_l[par]; ABT, b_AB = ABT_l[par], b_AB_l[par]
            TM2, b_TM2 = TM2_l[par], b_TM2_l[par]
            RG = "F"
            pres = {}

            norm_block(col0, tw, s, 0, hT, b_hT, xb, ring="F")
            yield 1
            if l == 0 and bi == 0:
                dump("hT", hT, [b_hT])

            def proj(c0, nch, M=128):
                bk, pb = PS(1, RG)
                for j in range(nch):
                    for k in range(8):
                        mm(psf[0:M, bk, j * 128:j * 128 + tw], win[:, k, c0 + j * 128:c0 + j * 128 + M], hT[:, k, 0:tw],
                           k == 0, k == 7, reads=rdw, writes=pb, sig=(k == 7))
                    if j < nch - 1:
                        yield 1
                pres["v"] = (bk, pb)

            def pv(bk, nch=4):
                return psf[:, bk, :].rearrange("p (c t) -> p c t", c=4)[:, 0:nch, 0:tw]

            if smp:
                DVE(lambda h: h.tensor_copy(out=XP[:, :, 0:3], in_=colsB[:, 0:36].rearrange("p (i c) -> p c i", i=3)),
                    reads=[b_colsB], writes=[b_XP])
            elif bi == 0:
                DVE(lambda h: h.memset(XP[:, :, 0:3], 0.0), writes=[b_XP])
            for g3 in range(3):
                yield from proj(1024 + g3 * 512, 4)
                bk, pb = pres["v"]
                yield 1
                eng = ACT if g3 != 1 else DVE
                if eng is ACT:
                    ACT(lambda h: h.copy(out=XP[:, g3 * 4:g3 * 4 + 4, 3:3 + tw], in_=pv(bk)), reads=pb, writes=[b_XP])
                else:
                    DVE(lambda h: h.tensor_copy(out=XP[:, g3 * 4:g3 * 4 + 4, 3:3 + tw], in_=pv(bk)), reads=pb, writes=[b_XP])
            if smp or bi == NB - 1:
                dst = o_sconv if smp else o_pconv
                for g3 in range(3):
                    for c in range(4):
                        tr(psb[0:3, c * 128:(c + 1) * 128], XP[:, g3 * 4 + c, tw:tw + 3], 128, reads=[b_XP], writes=[b_psb], sig=(c == 3), dt=BF16)
                    t0, bt0 = TMP(RG)
                    DVE(lambda h: h.tensor_copy(out=t0[0:3, :], in_=psb[0:3, 0:512]), reads=[b_psb], writes=[bt0])
                    cx.dma("sp", dst[l][:, g3 * 512:(g3 + 1) * 512], t0[0:3, :], reads=[bt0])
            yield 1
            def cwb(g, i):
                return pcols[:, 64 + i * 12 + g * 4:64 + i * 12 + g * 4 + 4].unsqueeze(2).to_broadcast([128, 4, tw])
            def CE(g):
                return DVE if g == 2 else POOL
            for g in range(3):
                bmerge(b_QAg[g], b_QA)
            for g in range(3):
                CE(g)(lambda h: h.tensor_tensor(out=QA[:, g * 4:g * 4 + 4, 0:tw], in0=XP[:, g * 4:g * 4 + 4, 0:tw], in1=cwb(g, 0), op=ALU.mult),
                      reads=[b_XP, b_pcols], writes=[b_QAg[g]])
            for i in range(1, 4):
                tl = []
                for g in range(3):
                    t_, bt_ = TMP(RG)
                    tv_ = t_.rearrange("p (c t) -> p c t", c=4)[:, :, 0:tw]
                    CE(g)(lambda h: h.tensor_tensor(out=tv_, in0=XP[:, g * 4:g * 4 + 4, i:i + tw], in1=cwb(g, i), op=ALU.mult),
                          reads=[b_XP, b_pcols], writes=[bt_])
                    tl.append((tv_, bt_))
                for g in range(3):
                    tv_, bt_ = tl[g]
                    CE(g)(lambda h: h.tensor_tensor(out=QA[:, g * 4:g * 4 + 4, 0:tw], in0=QA[:, g * 4:g * 4 + 4, 0:tw], in1=tv_, op=ALU.add),
                          reads=[bt_, b_QAg[g]], writes=[b_QAg[g]])
                yield 1
            for g in range(3):
                bmerge(b_QA, b_QAg[g])
            if not smp:
                POOL(lambda h: h.tensor_copy(out=XP[:, :, 0:3], in_=XP[:, :, 128:131]), reads=[b_XP, b_QA], writes=[b_XP])
            yield 1
            yield from proj(0, 4)
            bk, pb = pres["v"]
            ACT(lambda h: h.activation(out=uT[:, :, 0:tw], in_=pv(bk), func=AF.Gelu), reads=pb, writes=[b_uT])
            yield 1
            bk, pb = PS(1, RG)
            for k in range(8):
                mm(psf[0:tw, bk, :], hT[:, k, 0:tw], win[:, k, 512:1024], k == 0, k == 7, reads=rdw, writes=pb, sig=(k == 7))
            vg, bvg = TMP(RG)
            ACT(lambda h: h.activation(out=vg[0:tw, :], in_=psf[0:tw, bk, :], func=AF.Gelu), reads=pb, writes=[bvg])
            junk, bj = TMP(RG)
            ACT(lambda h: h.activation(out=junk[0:tw, :], in_=vg[0:tw, :], func=AF.Square, accum_out=sml[0:tw, 0:1]),
                reads=[bvg], writes=[bj, b_sml])
            rstd_from_ps(sml[0:tw, 0:1], 1, 1.0 / 512, 0.0, sml[0:tw, 1:2], [b_sml], [b_sml])
            vnf, b_vnf = TMP(RG)
            DVE(lambda h: h.scalar_tensor_tensor(out=vnf[0:tw, :], in0=vg[0:tw, :], scalar=sml[0:tw, 1:2], in1=SGBC[0:tw, :],
                                                 op0=ALU.mult, op1=ALU.mult), reads=[bvg, b_sml, b_lp], writes=[b_vnf])
            ACT(lambda h: h.copy(out=vnb[0:tw, :], in_=vnf[0:tw, :]), reads=[b_vnf], writes=[b_vnb])
            if smp:
                cx.dma("sp", o_sgu[l], vnf[0:LS, :], reads=[b_vnf])
            yield 1
            bk, pb = PS(1, RG)
            for g in range(4):
                mm(psf[:, bk, g * 128:g * 128 + tw], vnb[0:tw, g * 128:(g + 1) * 128], WmT[0:tw, g, 0:tw], True, True,
                   reads=[b_vnb, b_wm], writes=pb, sig=(g == 3))
            t0, bt0 = TMP(RG)
            t0v = t0.rearrange("p (g q) -> p g q", g=4)[:, :, 0:tw]
            DVE(lambda h: h.tensor_tensor(out=t0v, in0=pv(bk), in1=BSRB[:, :, 0:tw], op=ALU.add), reads=pb + [b_lp], writes=[bt0])
            DVE(lambda h: h.tensor_tensor(out=ABT[:, 0:4, 0:tw], in0=t0v, in1=uT[:, :, 0:tw], op=ALU.mult),
                reads=[bt0, b_uT], writes=[b_AB])
            yield 1
            yield from proj(3072, 1, M=8)
            bk, pb = pres["v"]
            ACT(lambda h: h.copy(out=absb[0:8, 0:tw], in_=psf[0:8, bk, 0:tw]), reads=pb, writes=[b_absb])
            bk2, pb2 = PS(2, RG)
            rbv = psf[:, bk2:bk2 + 2, :].rearrange("p b (r t) -> p (b r) t", r=4)
            for r0 in (0, 4):
                am_, bam = TMP(RG)
                am = am_.rearrange("p (r t) -> p r t", r=4)
                DVE(lambda h: h.tensor_tensor(out=am[0:8, :, 0:tw], in0=absb[0:8, 0:tw].unsqueeze(1).to_broadcast([8, 4, tw]),
                                              in1=ident_f[0:8, r0:r0 + 4].unsqueeze(2).to_broadcast([8, 4, tw]), op=ALU.mult),
                    reads=[b_absb, b_const], writes=[bam])
                for r in range(4):
                    mm(rbv[:, r0 + r, 0:tw], ones_f[0:8, :], am[0:8, r, 0:tw], True, True, reads=[bam, b_const], writes=pb2, sig=(r == 3))
            betaRB_, b_beta = TMP(RG)
            betaRB = betaRB_.rearrange("p (g q) -> p g q", g=4)
            spb_, b_spb = TMP(RG)
            spb = spb_.rearrange("p (g q) -> p g q", g=4)
            ACT(lambda h: h.activation(out=spb[:, :, 0:tw], in_=rbv[:, 0:4, 0:tw], func=AF.Exp, scale=-1.0), reads=pb2, writes=[b_spb])
            ACT(lambda h: h.activation(out=spb[:, :, 0:tw], in_=spb[:, :, 0:tw], func=AF.Ln, bias=1.0, scale=1.0), reads=[b_spb], writes=[b_spb])
            ACT(lambda h: h.activation(out=betaRB[:, :, 0:tw], in_=spb[:, :, 0:tw], func=AF.Exp, scale=-1.0), reads=[b_spb], writes=[b_beta])
            sp_, bsp = TMP(RG)
            spv = sp_.rearrange("p (g q) -> p g q", g=4)
            for hh in range(4):
                ACT(lambda h: h.activation(out=spv[:, hh, 0:tw], in_=rbv[:, 4 + hh, 0:tw], func=AF.Exp, bias=hd[:, hh:hh + 1], scale=1.0),
                    reads=pb2 + [b_lp], writes=[bsp])
            ACT(lambda h: h.activation(out=spv[:, :, 0:tw], in_=spv[:, :, 0:tw], func=AF.Ln, bias=1.0, scale=1.0), reads=[bsp], writes=[bsp])
            for hh in range(4):
                DVE(lambda h: h.tensor_tensor_scan(out=gcRB[:, hh, 0:tw], data0=ones_f[:, 0:tw], data1=spv[:, hh, 0:tw],
                                                   initial=0.0, op0=ALU.mult, op1=ALU.add), reads=[bsp, b_const], writes=[b_gc])
            POOL(lambda h: h.tensor_tensor(out=gcRB[:, :, 0:tw], in0=gcRB[:, :, 0:tw], in1=hd[:, 8:12].unsqueeze(2).to_broadcast([128, 4, tw]),
                                           op=ALU.mult), reads=[b_gc, b_lp], writes=[b_gc])
            POOL(lambda h: h.tensor_tensor(out=BMt[0:tw, :, 0:tw], in0=betaRB[0:tw, :, 0:tw], in1=bc4(cmask[0:tw, 5, 0:tw], 4, tw, tw), op=ALU.mult),
                 reads=[b_beta, b_const], writes=[b_BM])
            yield 1
            bk, pb = PS(1, RG)
            for hh in range(4):
                mm(psf[0:tw, bk, hh:hh + 1], gcRB[0:1, hh, 0:tw], ones_f[0:1, 0:1], True, True, reads=[b_gc, b_const], writes=pb, sig=False)
                mm(psf[0:tw, bk, 4 + hh:5 + hh], betaRB[0:1, hh, 0:tw], ones_f[0:1, 0:1], True, True, reads=[b_beta, b_const],
                   writes=pb, sig=(hh == 3))
            DVE(lambda h: h.tensor_copy(out=TM[0:tw, 0:8], in_=psf[0:tw, bk, 0:8]), reads=pb, writes=[b_TM])
            DVE(lambda h: h.tensor_scalar(out=TM2[0:tw, 0:8], in0=TM[0:tw, 0:8], scalar1=-1.0, scalar2=None, op0=ALU.mult),
                reads=[b_TM], writes=[b_TM2])
            ACT(lambda h: h.activation(out=TM2[0:tw, 8:12], in_=TM[0:tw, 0:4], func=AF.Exp), reads=[b_TM], writes=[b_TM2])
            DVE(lambda h: h.tensor_tensor(out=TM2[0:tw, 8:12], in0=TM2[0:tw, 8:12], in1=TM[0:tw, 4:8], op=ALU.mult),
                reads=[b_TM, b_TM2], writes=[b_TM2])
            for hh in range(4):
                ACT(lambda h: h.activation(out=TM2[0:tw, 12 + hh:13 + hh], in_=TM[0:tw, hh:hh + 1], func=AF.Exp,
                                           bias=gcRB[0:tw, hh, tw - 1:tw], scale=-1.0), reads=[b_TM, b_gc], writes=[b_TM2])
                ACT(lambda h: h.activation(out=TM2[:, 16 + hh:17 + hh], in_=gcRB[:, hh, tw - 1:tw], func=AF.Exp), reads=[b_gc], writes=[b_TM2])

            yield 1
            yield from proj(2560, 4)
            bk, pb = pres["v"]
            ACT(lambda h: h.activation(out=zs[:, :, 0:tw], in_=pv(bk), func=AF.Silu), reads=pb, writes=[b_zs])
            yield 1
            ACT(lambda h: h.activation(out=QA[:, :, 0:tw], in_=QA[:, :, 0:tw], func=AF.Silu), reads=[b_QA], writes=[b_QA])
            ACT(lambda h: h.copy(out=vTb[:, :, 0:tw], in_=QA[:, 8:12, 0:tw]), reads=[b_QA], writes=[b_vT])
            yield 1
            for half in range(2):
                sq, bsq = TMP(RG)
                sqv = sq.bitcast(BF16)[:, 0:512].rearrange("p (c t) -> p c t", c=4)
                ACT(lambda h: h.activation(out=sqv[:, :, 0:tw], in_=QA[:, half * 4:half * 4 + 4, 0:tw], func=AF.Square),
                    reads=[b_QA], writes=[bsq])
                bk, pb = PS(1, RG)
                for c in range(4):
                    mm(psf[:, bk, c * 128:c * 128 + tw], ones_b[:], sqv[:, c, 0:tw], True, True, reads=[bsq, b_const], writes=pb, sig=(c == 3))
                rr, brr = TMP(RG)
                rrv = rr.rearrange("p (c t) -> p c t", c=4)[:, :, 0:tw]
                rstd_from_ps(pv(bk), tw, 1.0, (-0.5 * float(np.log(128.0))) if half == 0 else 0.0, rrv, pb, [brr])
                POOL(lambda h: h.tensor_tensor(out=QKN[:, half * 4:half * 4 + 4, 0:tw], in0=QA[:, half * 4:half * 4 + 4, 0:tw], in1=rrv,
                                               op=ALU.mult), reads=[b_QA, brr], writes=[b_QKN])
            eg, beg = TMP(RG)
            egv = eg.rearrange("p (c t) -> p c t", c=4)[:, :, 0:tw]
            ACT(lambda h: h.activation(out=egv, in_=gcRB[:, :, 0:tw], func=AF.Exp), reads=[b_gc], writes=[beg])
            POOL(lambda h: h.tensor_tensor(out=qgT[:, :, 0:tw], in0=QKN[:, 0:4, 0:tw], in1=egv, op=ALU.mult),
                 reads=[b_QKN, beg], writes=[b_qg])

            yield 1
            bkK, pbK = PS(1, RG); bkQ, pbQ = PS(1, RG)
            for hh in range(4):
                mm(psf[0:tw, bkK, hh * 128:hh * 128 + tw], QKN[:, 4 + hh, 0:tw], QKN[:, 4 + hh, 0:tw], True, True,
                   reads=[b_QKN], writes=pbK, sig=False)
                mm(psf[0:tw, bkQ, hh * 128:hh * 128 + tw], QKN[:, 4 + hh, 0:tw], QKN[:, hh, 0:tw], True, True,
                   reads=[b_QKN], writes=pbQ, sig=(hh == 3))
            KKv = psf[0:tw, bkK, :].rearrange("p (c t) -> p c t", c=4)[:, :, 0:tw]
            QKv = psf[0:tw, bkQ, :].rearrange("p (c t) -> p c t", c=4)[:, :, 0:tw]

            def etile(src, b_src, mask, sgn, bias_col0):
                e, be = TMP(RG)
                ev = e.rearrange("p (c t) -> p c t", c=4)[0:tw, :, 0:tw]
                POOL(lambda h: h.tensor_tensor(out=ev, in0=src[0:tw, :, 0:tw], in1=bc4(mask[0:tw, 0:tw], 4, tw, tw), op=ALU.add),
                     reads=[b_src, b_const], writes=[be])
                for hh in range(4):
                    bias = (TM if bias_col0 < 100 else TM2)[0:tw, (bias_col0 % 100) + hh:(bias_col0 % 100) + hh + 1]
                    ACT(lambda h: h.activation(out=ev[:, hh, :], in_=ev[:, hh, :], func=AF.Exp, bias=bias, scale=sgn),
                        reads=[be, b_TM, b_TM2], writes=[be])
                return ev, be

            eN, beN = etile(gcRB, b_gc, LSB, -1.0, 0)
            POOL(lambda h: h.tensor_tensor(out=eN, in0=eN, in1=TM2[0:tw, 4:8].unsqueeze(2).to_broadcast([tw, 4, tw]), op=ALU.mult),
                 reads=[beN, b_TM2], writes=[beN])
            DVE(lambda h: h.tensor_tensor(out=NM[0:tw, :, 0, 0:tw], in0=eN, in1=KKv, op=ALU.mult), reads=[beN] + pbK, writes=[b_NM])
            yield 1
            eA, beA = etile(gcRB, b_gc, NUI, 1.0, 100)
            DVE(lambda h: h.tensor_tensor(out=attnT[0:tw, :, 0:tw], in0=eA, in1=QKv, op=ALU.mult), reads=[beA] + pbQ, writes=[b_attn])
            eM_, beM = TMP(RG)
            eM = eM_.rearrange("p (c t) -> p c t", c=4)[0:tw, :, 0:tw]
            POOL(lambda h: h.tensor_tensor(out=eM, in0=eA, in1=BMt[0:tw, :, 0:tw], op=ALU.mult), reads=[beA, b_BM], writes=[beM])
            DVE(lambda h: h.scalar_tensor_tensor(out=NM[0:tw, :, 1, 0:tw], in0=eM, scalar=-1.0, in1=KKv, op0=ALU.mult, op1=ALU.mult),
                reads=[beM] + pbK, writes=[b_NM])
            pbv = psb[:].rearrange("p (a c t) -> p a c t", a=2, c=4)
            for hh in range(4):
                tr(pbv[0:tw, 0, hh, :], QKN[:, 4 + hh, 0:tw], 128, reads=[b_QKN], writes=[b_psb], sig=False, dt=BF16)
                tr(pbv[0:tw, 1, hh, :], vTb[:, hh, 0:tw], 128, reads=[b_vT], writes=[b_psb], sig=(hh == 3), dt=BF16)

            def bcol(ap_cols):
                return ap_cols.unsqueeze(2).to_broadcast([tw, 4, 128])
            DVE(lambda h: h.tensor_tensor(out=kbg[0:tw], in0=pbv[0:tw, 0], in1=bcol(TM2[0:tw, 8:12]), op=ALU.mult),
                reads=[b_psb, b_TM2], writes=[b_kbg])
            DVE(lambda h: h.tensor_tensor(out=kg[0:tw], in0=pbv[0:tw, 0], in1=bcol(TM2[0:tw, 12:16]), op=ALU.mult),
                reads=[b_psb, b_TM2], writes=[b_kg])
            DVE(lambda h: h.tensor_tensor(out=vb[0:tw], in0=pbv[0:tw, 1], in1=bcol(TM[0:tw, 4:8]), op=ALU.mult),
                reads=[b_psb, b_TM], writes=[b_vb])
            RG = "B"
            yield "F_DONE"
            d8 = bc4(cmask[0:tw, 0, 0:tw], 4, tw, tw)
            DVE(lambda h: h.tensor_tensor(out=PQ[0:tw, :, 0, 0:tw], in0=NM[0:tw, :, 1, 0:tw], in1=d8, op=ALU.mult),
                reads=[b_NM, b_const], writes=[b_PQ])
            DVE(lambda h: h.tensor_tensor(out=PTQ[0:tw, :, 0, 0:tw], in0=NM[0:tw, :, 0, 0:tw], in1=d8, op=ALU.mult),
                reads=[b_NM, b_const], writes=[b_PTQ])
            idb = bc4(ident_f[0:tw, 0:tw], 4, tw, tw)
            DVE(lambda h: h.tensor_tensor(out=PQ[0:tw, :, 1, 0:tw], in0=PQ[0:tw, :, 0, 0:tw], in1=idb, op=ALU.add),
                reads=[b_PQ, b_const], writes=[b_PQ])
            DVE(lambda h: h.tensor_tensor(out=PTQ[0:tw, :, 1, 0:tw], in0=PTQ[0:tw, :, 0, 0:tw], in1=idb, op=ALU.add),
                reads=[b_PTQ, b_const], writes=[b_PTQ])
            for lev in (1, 2, 3):
                yield 2
                last = lev == 3
                bkA, pbA = PS(2, RG); bkB, pbB = PS(2, RG)
                Av = psf[0:tw, bkA:bkA + 2, :].rearrange("p b (h x) -> p (b h) x", h=2)
                Bv = psf[0:tw, bkB:bkB + 2, :].rearrange("p b (h x) -> p (b h) x", h=2)
                rd = [b_PQ, b_PTQ]
                for hh in range(4):
                    Ph = PQ[0:tw, hh, 0, 0:tw]; Qh = PQ[0:tw, hh, 1, 0:tw]
                    PTh = PTQ[0:tw, hh, 0, 0:tw]; Qnh = PTQ[0:tw, hh, 1, 0:tw]
                    if not last:
                        mm(Av[:, hh, 0:tw], PTh, Ph, True, True, reads=rd, writes=pbA, sig=False)
                        mm(Bv[:, hh, 0:tw], Ph, PTh, True, True, reads=rd, writes=pbA + pbB, sig=(hh == 3 and lev == 1))
                    if lev > 1:
                        mm(Av[:, hh, 128:128 + tw], PTh, Qh, True, True, reads=rd, writes=pbA, sig=False)
                        mm(Bv[:, hh, 128:128 + tw], Ph, Qnh, True, True, reads=rd, writes=pbA + pbB, sig=(hh == 3))
                if lev > 1:
                    DVE(lambda h: h.tensor_tensor(out=PQ[0:tw, :, 1, 0:tw], in0=PQ[0:tw, :, 1, 0:tw], in1=Av[:, :, 128:128 + tw], op=ALU.add),
                        reads=pbA + [b_PQ], writes=[b_PQ])
                    DVE(lambda h: h.tensor_tensor(out=PTQ[0:tw, :, 1, 0:tw], in0=PTQ[0:tw, :, 1, 0:tw], in1=Bv[:, :, 128:128 + tw], op=ALU.add),
                        reads=pbB + [b_PTQ], writes=[b_PTQ])
                if not last:
                    ACT(lambda h: h.copy(out=PQ[0:tw, :, 0, 0:tw], in_=Av[:, :, 0:tw]), reads=pbA, writes=[b_PQ])
                    ACT(lambda h: h.copy(out=PTQ[0:tw, :, 0, 0:tw], in_=Bv[:, :, 0:tw]), reads=pbB, writes=[b_PTQ])
            li = 0
            for bsz in (8, 16, 32, 64):
                li += 1
                if bsz >= tw:
                    break
                cb = bc4(cmask[0:tw, li, 0:tw], 4, tw, tw)
                yield 2
                bk1, pb1 = PS(1, RG); bk2_, pb2_ = PS(1, RG)
                Y1 = psf[0:tw, bk1, :].rearrange("p (c t) -> p c t", c=4)[:, :, 0:tw]
                Y2 = psf[0:tw, bk2_, :].rearrange("p (c t) -> p c t", c=4)[:, :, 0:tw]
                for hh in range(4):
                    mm(Y1[:, hh, :], NM[0:tw, hh, 0, 0:tw], PQ[0:tw, hh, 1, 0:tw], True, True, reads=[b_NM, b_PQ], writes=pb1, sig=False)
                    mm(Y2[:, hh, :], NM[0:tw, hh, 1, 0:tw], PTQ[0:tw, hh, 1, 0:tw], True, True, reads=[b_NM, b_PTQ], writes=pb2_, sig=(hh == 3))
                DVE(lambda h: h.tensor_tensor(out=PQ[0:tw, :, 0, 0:tw], in0=Y1, in1=cb, op=ALU.mult), reads=pb1 + [b_const], writes=[b_PQ])
                DVE(lambda h: h.tensor_tensor(out=PTQ[0:tw, :, 0, 0:tw], in0=Y2, in1=cb, op=ALU.mult), reads=pb2_ + [b_const], writes=[b_PTQ])
                yield 2
                bk3, pb3 = PS(1, RG); bk4, pb4 = PS(1, RG)
                Z1 = psf[0:tw, bk3, :].rearrange("p (c t) -> p c t", c=4)[:, :, 0:tw]
                Z2 = psf[0:tw, bk4, :].rearrange("p (c t) -> p c t", c=4)[:, :, 0:tw]
                for hh in range(4):
                    mm(Z1[:, hh, :], PTQ[0:tw, hh, 1, 0:tw], PQ[0:tw, hh, 0, 0:tw], True, True, reads=[b_PQ, b_PTQ], writes=pb3, sig=False)
                    mm(Z2[:, hh, :], PQ[0:tw, hh, 1, 0:tw], PTQ[0:tw, hh, 0, 0:tw], True, True, reads=[b_PQ, b_PTQ], writes=pb4, sig=(hh == 3))
                DVE(lambda h: h.tensor_tensor(out=PQ[0:tw, :, 1, 0:tw], in0=PQ[0:tw, :, 1, 0:tw], in1=Z1, op=ALU.add),
                    reads=pb3 + pb4 + [b_PQ], writes=[b_PQ])
                DVE(lambda h: h.tensor_tensor(out=PTQ[0:tw, :, 1, 0:tw], in0=PTQ[0:tw, :, 1, 0:tw], in1=Z2, op=ALU.add),
                    reads=pb3 + pb4 + [b_PTQ], writes=[b_PTQ])
            yield 2
            bkW, pbW = PS(1, RG)
            for hh in range(4):
                mm(psf[:, bkW, hh * 128:hh * 128 + tw], kbg[0:tw, hh, :], PQ[0:tw, hh, 1, 0:tw], True, True,
                   reads=[b_PQ, b_kbg], writes=pbW, sig=(hh == 3))
            ACT(lambda h: h.mul(out=wTb[:, :, 0:tw], in_=pv(bkW), mul=-1.0), reads=pbW, writes=[b_wT])
            if smp:
                cx.dma("sp", S[:], st_delta[l].rearrange("h k v -> k h v"), writes=[b_S])
                ACT(lambda h: h.copy(out=S_bf[:], in_=S[:]), reads=[b_S], writes=[b_Sbf])
            elif bi == 0:
                DVE(lambda h: h.memset(S[:], 0.0), writes=[b_S])
                DVE(lambda h: h.memset(S_bf[:], 0.0), writes=[b_Sbf])
            yield 2
            bk, pb = PS(1, RG)
            for hh in range(4):
                mm(psf[0:tw, bk, hh * 128:(hh + 1) * 128], PQ[0:tw, hh, 1, 0:tw], vb[0:tw, hh, :], True, False,
                   reads=[b_PQ, b_vb], writes=pb, sig=False)
                mm(psf[0:tw, bk, hh * 128:(hh + 1) * 128], wTb[:, hh, 0:tw], S_bf[:, hh, :], False, True, reads=[b_wT, b_Sbf], writes=pb,
                   sig=(hh == 3))
            DVE(lambda h: h.tensor_copy(out=vnew[0:tw], in_=psf[0:tw, bk, :].rearrange("p (c t) -> p c t", c=4)), reads=pb, writes=[b_vn])
            yield 2
            bkO, pbO = PS(1, RG)
            for hh in range(4):
                mm(psf[:, bkO, hh * 128:hh * 128 + tw], S_bf[:, hh, :], qgT[:, hh, 0:tw], True, False, reads=[b_Sbf, b_qg], writes=pbO, sig=False)
                mm(psf[:, bkO, hh * 128:hh * 128 + tw], vnew[0:tw, hh, :], attnT[0:tw, hh, 0:tw], False, True, reads=[b_vn, b_attn],
                   writes=pbO, sig=(hh == 3))
            bk, pb = PS(1, RG)
            for hh in range(4):
                mm(psf[:, bk, hh * 128:(hh + 1) * 128], kg[0:tw, hh, :], vnew[0:tw, hh, :], True, True, reads=[b_kg, b_vn], writes=pb,
                   sig=(hh == 3))
            for hh in range(4):
                DVE(lambda h: h.scalar_tensor_tensor(out=S[:, hh, :], in0=S[:, hh, :], scalar=TM2[:, 16 + hh:17 + hh],
                                                     in1=psf[:, bk, hh * 128:(hh + 1) * 128], op0=ALU.mult, op1=ALU.add),
                    reads=pb + [b_S, b_TM2], writes=[b_S])
            ACT(lambda h: h.copy(out=S_bf[:], in_=S[:]), reads=[b_S], writes=[b_Sbf])
            if smp or bi == NB - 1:
                cx.dma("sp", (o_sdelta if smp else o_pdelta)[l].rearrange("h k v -> k h v"), S[:], reads=[b_S])
            yield 2
            osq, bosq = TMP(RG)
            osqv = osq.bitcast(BF16)[:, 0:512].rearrange("p (c t) -> p c t", c=4)
            ACT(lambda h: h.activation(out=osqv[:, :, 0:tw], in_=pv(bkO), func=AF.Square), reads=pbO, writes=[bosq])
            bk, pb = PS(1, RG)
            for c in range(4):
                mm(psf[:, bk, c * 128:c * 128 + tw], ones_b[:], osqv[:, c, 0:tw], True, True, reads=[bosq, b_const], writes=pb, sig=(c == 3))
            ro, bro = TMP(RG)
            rov = ro.rearrange("p (c t) -> p c t", c=4)[:, :, 0:tw]
            rstd_from_ps(pv(bk), tw, 1.0 / 128, 0.0, rov, pb, [bro])
            t1, bt1 = TMP(RG)
            t1v = t1.rearrange("p (c t) -> p c t", c=4)[:, :, 0:tw]
            DVE(lambda h: h.scalar_tensor_tensor(out=t1v, in0=pv(bkO), scalar=pcols[:, 112:113], in1=rov, op0=ALU.mult, op1=ALU.mult),
                reads=pbO + [bro, b_pcols], writes=[bt1])
            DVE(lambda h: h.tensor_tensor(out=ABT[:, 4:8, 0:tw], in0=t1v, in1=zs[:, :, 0:tw], op=ALU.mult), reads=[bt1, b_zs], writes=[b_AB])
            if l == 0 and bi == 0:
                dump("ABT", ABT, [b_AB])
            yield 2
            bk2, pb2 = PS(2, RG)
            ov = psf[:, bk2:bk2 + 2, :].rearrange("p b (c t) -> p (b c) t", c=4)
            for m in range(8):
                for k in range(8):
                    mm(ov[:, m, 0:tw], wout[:, k, m * 128:(m + 1) * 128], ABT[:, k, 0:tw], k == 0, k == 7, reads=L_wout + [b_AB], writes=pb2,
                       sig=(k == 7 and m == 7))
            for m in range(8):
                DVE(lambda h: h.scalar_tensor_tensor(out=xT[:, m, col0:col0 + tw], in0=ov[:, m, 0:tw], scalar=lay[:, 2, s, m:m + 1],
                                                     in1=xT[:, m, col0:col0 + tw], op0=ALU.mult, op1=ALU.add),
                    reads=pb2 + [b_lay] + xb, writes=xb)

        tiles = [(t * 512, 512, 0, list(range(t * 4, t * 4 + 4))) for t in range(L // 512)]
        if L % 512:
            t0_ = (L // 512) * 512
            tiles.append((t0_, L - t0_, 0, list(range(t0_ // 128, NB))))
        tiles.append((L, LS, 1, [NB]))

        def load_eighth(l, e):
            sl = e % 4
            o = slot_off[sl]
            up = WA[:, o:o + 4096].rearrange("p (k n) -> p k n", k=8)
            dn = WA[:, o + 4096:o + 8192].rearrange("p (k n) -> p k n", k=4)
            cx.dma("pool", up, w_up[l].rearrange("(k p) n -> p k n", p=128)[:, :, e * 512:(e + 1) * 512], writes=slot_buf[sl])
            cx.dma("pool", dn, w_down[l][e * 512:(e + 1) * 512, :].rearrange("(k p) n -> p k n", p=128), writes=slot_buf[sl])
            return up, dn, slot_buf[sl]

        def phaseB(l, pend):
            nxt = l + 1 if l + 1 < DEPTH else None
            ada_next = list(range(24)) if nxt is not None else []
            hcnt = 0
            extra = [(hidf[0], b_hidf[0]), (hidf[1], b_hidf[1])]
            for i2 in range(2):
                hv = hidb[i2].rearrange("p a b -> p (a b)").bitcast(F32)
                extra.append((hv[:, 0:512], b_hidb[i2]))
            tstate["bpool"] = [(tmpFb[i], b_tmpb[i]) for i in range(4)] + extra
            cx.rec_begin()
            for (c0, tw, s, blks) in tiles:
                for sub in range(0, tw, 128):
                    w_ = min(128, tw - sub)
                    bi_ = (c0 + sub) // 128
                    norm_block(c0 + sub, w_, s, 3, hTall[:, :, c0 + sub:c0 + sub + w_], b_hall_l[bi_], [b_x[bi_]])
            cx.rec_flush()
            tstate["bpool"] = None
            cx.rec_begin()
            if nxt is not None:
                layer_params(nxt)
            for e in range(8):
                up, dn, wb = pend.pop(e)
                for (c0, tw, s, blks) in tiles:
                    xb = [b_x[i] for i in blks]
                    hb = hcnt % 2
                    hcnt += 1
                    for j in range(4):
                        bk, pb = PS(1)
                        for k in range(8):
                            mm(psf[:, bk, 0:tw], up[:, k, j * 128:(j + 1) * 128], hTall[:, k, c0:c0 + tw], k == 0, k == 7,
                               reads=wb + [b_hall_l[i] for i in blks], writes=pb, sig=(k == 7))
                        hf = hidf[j % 2]; bhf = b_hidf[j % 2]
                        ACT(lambda h: h.activation(out=hf[:, 0:tw], in_=psf[:, bk, 0:tw], func=AF.Relu), reads=pb, writes=[bhf])
                        POOL(lambda h: h.tensor_tensor(out=hidb[hb][:, j, 0:tw], in0=hf[:, 0:tw], in1=hf[:, 0:tw], op=ALU.mult),
                             reads=[bhf], writes=[b_hidb[hb]])
                    for m in range(8):
                        bk, pb = PS(1)
                        for k in range(4):
                            mm(psf[:, bk, 0:tw], dn[:, k, m * 128:(m + 1) * 128], hidb[hb][:, k, 0:tw], k == 0, k == 3,
                               reads=wb + [b_hidb[hb]], writes=pb, sig=(k == 3))
                        DVE(lambda h: h.scalar_tensor_tensor(out=xT[:, m, c0:c0 + tw], in0=psf[:, bk, 0:tw], scalar=lay[:, 5, s, m:m + 1],
                                                             in1=xT[:, m, c0:c0 + tw], op0=ALU.mult, op1=ALU.add),
                            reads=pb + [b_lay] + xb, writes=xb)
                    if ada_next and e >= 1:
                        ada_chunk(nxt, ada_next.pop(0), adaStB, b_adaStB)
                if e + 4 < 8:
                    pend[e + 4] = load_eighth(l, e + 4)
                elif nxt is not None:
                    if e == 4:
                        load_wout(nxt)
                    elif e == 5:
                        load_win_chunks(nxt, [0, 1])
                    elif e == 6:
                        load_win_chunks(nxt, [2, 3, 4])
                    elif e == 7:
                        load_win_chunks(nxt, [5, 6, 7])
            while ada_next:
                ada_chunk(nxt, ada_next.pop(0), adaStB, b_adaStB)
            cx.rec_flush()

        xstage2 = R[:, tmp_o + 4 * 512:tmp_o + 6 * 512]
        b_xst2 = [b_tmp[4], b_tmp[5]]
        b_smlF = [B("smlF0"), B("smlF1")]

        def final_out(bi):
            tw = 128 if bi < NB else LS
            col0 = bi * 128
            par = bi % 2
            xs, bxs = (xstage, b_xst) if par == 0 else (xstage2, b_xst2)
            sc0 = par * 4
            bsm = b_smlF[par]
            bk2, pb2 = PS(2)
            tv = psf[0:tw, bk2:bk2 + 2, :].rearrange("p b x -> p (b x)")
            for c in range(8):
                tr(tv[:, c * 128:(c + 1) * 128], xT[:, c, col0:col0 + tw], 128, reads=[b_x[bi]], writes=pb2, sig=(c == 7))
            for hf in range(2):
                ACT(lambda h: h.activation(out=tmpF[2 + hf][0:tw, :], in_=tv[:, hf * 512:(hf + 1) * 512], func=AF.Square,
                                           accum_out=sml[0:tw, sc0 + hf:sc0 + hf + 1]), reads=pb2, writes=[b_tmp[2 + hf], bsm])
            DVE(lambda h: h.tensor_tensor(out=sml[0:tw, sc0 + 2:sc0 + 3], in0=sml[0:tw, sc0:sc0 + 1], in1=sml[0:tw, sc0 + 1:sc0 + 2], op=ALU.add),
                reads=[bsm], writes=[bsm])
            rstd_from_ps(sml[0:tw, sc0 + 2:sc0 + 3], 1, 1.0 / D, 0.0, sml[0:tw, sc0 + 3:sc0 + 4], [bsm], [bsm])
            DVE(lambda h: h.scalar_tensor_tensor(out=xs[0:tw, :], in0=tv, scalar=sml[0:tw, sc0 + 3:sc0 + 4], in1=FING[0:tw, :], op0=ALU.mult, op1=ALU.mult),
                reads=pb2 + [bsm, b_fing], writes=bxs)
            dst = y_p[col0:col0 + tw, :] if bi < NB else y_s
            cx.dma("sp", dst, xs[0:tw, :], reads=bxs)

        for l in range(DEPTH):
            if l == 0:
                load_wA(l)
                cx.rec_begin()
                for bi in range(NB + 1):
                    load_x(bi)
                layer_params(l)
                ada(l)
                cx.rec_flush()
            layer_cols(l)
            cx.barrier()
            cx.rec_begin()
            gens = [phaseA_block(l, bi) for bi in range(NB + 1)]

            def step(g):
                try:
                    return next(g)
                except StopIteration:
                    return "END"
            while step(gens[0]) != "F_DONE":
                pass
            for bi in range(NB + 1):
                cur = gens[bi]
                nxt = gens[bi + 1] if bi + 1 <= NB else None
                cur_done = False
                nxt_done = nxt is None
                while not (cur_done and nxt_done):
                    if not cur_done:
                        if step(cur) == "END":
                            cur_done = True
                    for _ in range(FSTEPS):
                        if not nxt_done:
                            if step(nxt) == "F_DONE":
                                nxt_done = True
            pend = {e: load_eighth(l, e) for e in range(4)}
            cx.rec_flush()
            cx.barrier()
            tstate["mode"] = "b"
            phaseB(l, pend)
            tstate["mode"] = "a"
        cx.barrier()
        cx.dma("sp", FING, fin_g.partition_broadcast(128), writes=[b_fing])
        cx.rec_begin()
        for bi in range(NB + 1):
            final_out(bi)
        cx.rec_flush()
        cx.finish()
    return nc


def _consts():
    ident = np.eye(128, dtype=np.float32)
    p = np.arange(128)[:, None]
    f = np.arange(128)[None, :]
    masks = np.zeros((128, 3, 128), np.float32)
    masks[:, 0, :] = np.where(f >= p, 0.0, -BIG)
    masks[:, 1, :] = np.where(f > p, 0.0, -BIG)
    masks[:, 2, :] = np.where(f < p, 0.0, BIG)
    sgum = (p // 64 <= f // 64).astype(np.float32)
    sel = np.zeros((8, 8, 128), np.float32)
    for r in range(8):
        sel[r, r, :] = 1.0
    cm = np.zeros((128, 6, 128), np.float32)
    cm[:, 5, :] = (f > p)
    cm[:, 0, :] = (p // 8 == f // 8)
    for li, bsz in enumerate((8, 16, 32, 64)):
        cm[:, li + 1, :] = (p // (2 * bsz) == f // (2 * bsz)) & (p // bsz != f // bsz)
    return ident, masks, sgum, sel, cm.astype(ml_dtypes.bfloat16)


_NC_CACHE = {}


def make_in_maps(inputs, L):
    f = lambda a: np.ascontiguousarray(np.asarray(a, dtype=np.float32))
    ident, masks, sgum, sel, cm = _consts()
    x_prompt = f(inputs["x_prompt"]); x_sample = f(inputs["x_sample"])
    nb = x_prompt.shape[0]
    rowsA = np.concatenate([
        f(inputs["norm_mix_g"]).reshape(DEPTH, 8, 128), f(inputs["norm_ffn_g"]).reshape(DEPTH, 8, 128),
        f(inputs["ada_b"]).reshape(DEPTH, 48, 128), f(inputs["conv_w"]).reshape(DEPTH, 48, 128),
        f(inputs["dn_norm_g"]).reshape(DEPTH, 1, 128)], axis=1)
    shared = dict(
        ada_w=f(inputs["ada_w"]), rowsA=np.ascontiguousarray(rowsA), w_in=f(inputs["w_in"]),
        sgu_norm_g=f(inputs["sgu_norm_g"]), sgu_w=f(inputs["sgu_w"]), sgu_b=f(inputs["sgu_b"]).reshape(DEPTH, 512),
        dt_bias=f(inputs["dt_bias"]), a_log=f(inputs["a_log"]), w_out=f(inputs["w_out"]), w_up=f(inputs["w_up"]),
        w_down=f(inputs["w_down"]), fin_g=f(inputs["final_norm_g"]).reshape(1, D),
        k_ident=ident, k_masks=masks, k_sgum=sgum, k_cm=cm)
    in_maps = []
    for i in range(nb):
        m = dict(shared)
        m["x_p"] = np.ascontiguousarray(x_prompt[i, :L]); m["x_s"] = x_sample[i]
        m["c_ps"] = np.ascontiguousarray(np.concatenate([f(inputs["c_prompt"])[i].reshape(8, 128), f(inputs["c_sample"])[i].reshape(8, 128)], 0))
        m["st_conv"] = np.ascontiguousarray(f(inputs["state_conv"])[:, i].reshape(DEPTH, 36, 128))
        m["st_delta"] = np.ascontiguousarray(f(inputs["state_delta"])[:, i])
        in_maps.append(m)
    return in_maps


def run(inputs, L=2048, dbg=None, core_ids=None):
    key = (L, tuple(sorted((dbg or {}).items())))
    if key not in _NC_CACHE:
        _NC_CACHE[key] = build(L, dbg)
    nc = _NC_CACHE[key]
    in_maps = make_in_maps(inputs, L)
    if core_ids is not None:
        in_maps = [in_maps[i] for i in core_ids]
    res = run_bass_kernel_spmd(nc, in_maps, core_ids=list(range(len(in_maps))))
    return res.results


def kernel(**inputs):
    r = run(inputs, 2048)
    st = lambda k: np.stack([np.asarray(x[k], dtype=np.float32) for x in r], axis=0)
    y_prompt = st("y_p"); y_sample = st("y_s")
    pconv = np.ascontiguousarray(st("o_pconv").transpose(1, 0, 2, 3))
    pdelta = np.ascontiguousarray(st("o_pdelta").transpose(1, 0, 2, 3, 4))
    sconv = np.ascontiguousarray(st("o_sconv").transpose(1, 0, 2, 3))
    sdelta = np.ascontiguousarray(st("o_sdelta").transpose(1, 0, 2, 3, 4))
    sgu = np.ascontiguousarray(st("o_sgu").transpose(1, 0, 2, 3))
    return (y_prompt, y_sample, pconv, pdelta, sconv, sdelta, sgu)
```

```python
import os
import numpy as np
import ml_dtypes
from contextlib import ExitStack
import concourse.bass as bass
import concourse.mybir as mybir
from concourse.bass_utils import run_bass_kernel_spmd

F32, BF16 = mybir.dt.float32, mybir.dt.bfloat16
AF = mybir.ActivationFunctionType
ALU = mybir.AluOpType

D = 1024
LS = 32
DEPTH = 2
INW = 3080
DFF = 4096
EPS = 1e-6
BIG = 1.0e30
KDMA = 6
SAME_ENGINE_SYNC = True
FSTEPS = 1
NO_REORDER = False


class Buf:
    __slots__ = ("name", "w", "r", "rw", "rr")

    def __init__(self, name):
        self.name = name
        self.w = {}
        self.r = {}
        self.rw = []
        self.rr = []


class _Proxy:
    def __init__(self):
        self.call = None

    def __getattr__(self, name):
        def f(*a, **kw):
            self.call = (name, a, kw)
            return self
        return f


def _free_size(call):
    name, a, kw = call
    ap = kw.get("out", a[0] if a else None)
    try:
        shp = ap.shape
        n = 1
        for d in shp[1:]:
            n *= int(d)
        return max(n, 1)
    except Exception:
        return 128


def _cost(E, call):
    n = _free_size(call)
    if E == "pe":
        return (50.0 + 0.4 * n) * float(os.environ.get("SC_PE", 1.0))
    if E == "dve":
        return (n + 151) / 0.96 * float(os.environ.get("SC_DVE", 1.0))
    if E == "act":
        return (130.0 + 0.75 * n) * float(os.environ.get("SC_ACT", 1.0))
    if E == "pool":
        return 290.0 + 1.23 * n
    return 100.0


class Ctx:
    def __init__(self, nc, es):
        self.nc = nc
        self.es = es
        self.sems = {}
        self.eng = {}
        for name, h in (("pe", nc.tensor), ("act", nc.scalar), ("dve", nc.vector),
                        ("pool", nc.gpsimd), ("sp", nc.sync)):
            self.sems["e_" + name] = es.enter_context(nc.semaphore("se_" + name))
            self.eng[name] = dict(h=h, cnt=0, seen={})
        self.dq = {}
        for q in ("sp", "pool"):
            for i in range(KDMA):
                self.sems[f"d_{q}{i}"] = es.enter_context(nc.semaphore(f"sd_{q}{i}"))
            self.dq[q] = dict(cnts=[0] * KDMA, n=0)
        self.nbuf = 0
        self.recording = False
        self.rec = []

    def rec_begin(self):
        self.recording = True
        self.rec = []

    def rec_flush(self):
        rec = self.rec
        self.recording = False
        n = len(rec)
        if n == 0:
            return
        unit_of = [0] * n
        units = []
        cur = None
        for i, nd in enumerate(rec):
            if nd["E"] == "pe" and nd["kind"] == "op":
                if cur is None:
                    cur = len(units)
                    units.append(dict(E="pe", ops=[], dur=0.0, dma=False))
                units[cur]["ops"].append(i)
                units[cur]["dur"] += nd["dur"]
                unit_of[i] = cur
                if nd["sig"]:
                    cur = None
            else:
                u = len(units)
                units.append(dict(E=nd["E"], ops=[i], dur=nd["dur"], dma=(nd["kind"] == "dma")))
                unit_of[i] = u
        assert cur is None, "segment ends inside an unsignaled PE group"
        nu = len(units)
        udeps = [set() for _ in range(nu)]
        for i, nd in enumerate(rec):
            u = unit_of[i]
            for d in nd["deps"]:
                du = unit_of[d]
                if du != u:
                    udeps[u].add(du)
        succ = [[] for _ in range(nu)]
        indeg = [0] * nu
        for u in range(nu):
            indeg[u] = len(udeps[u])
            for d in udeps[u]:
                succ[d].append(u)
        prio = [0.0] * nu
        for u in range(nu - 1, -1, -1):
            m = 0.0
            for v in succ[u]:
                if prio[v] > m:
                    m = prio[v]
            prio[u] = units[u]["dur"] + m
        LAT, SLAT, WIN = 250.0, 100.0, 250.0
        etime = {k: 0.0 for k in self.eng}
        fin = [0.0] * nu
        ready = [u for u in range(nu) if indeg[u] == 0]
        order = []
        while ready:
            best = None
            ests = []
            mn = None
            for u in ready:
                E = units[u]["E"]
                st = etime[E]
                for d in udeps[u]:
                    t = fin[d] + (SLAT if units[d]["E"] == E else LAT)
                    if t > st:
                        st = t
                ests.append((u, st))
                if mn is None or st < mn:
                    mn = st
            for u, st in ests:
                if st <= mn + WIN:
                    key = (prio[u], -u)
                    if best is None or key > best[0]:
                        best = (key, u, st)
            _, u, st = best
            E = units[u]["E"]
            if units[u]["dma"]:
                etime[E] = st + units[u]["dur"]
                fin[u] = st + 2000.0
            else:
                etime[E] = st + units[u]["dur"]
                fin[u] = etime[E]
            order.append(u)
            ready.remove(u)
            for v in succ[u]:
                indeg[v] -= 1
                if indeg[v] == 0:
                    ready.append(v)
        assert len(order) == nu
        if NO_REORDER:
            order = list(range(nu))
        self.sim_time = max(fin) if fin else 0.0
        ticket = [None] * n
        for u in order:
            for i in units[u]["ops"]:
                nd = rec[i]
                E = nd["E"]
                e = self.eng[E]
                deps = self._deps(nd["reads"], nd["writes"])
                for d in nd["deps"]:
                    deps.append(ticket[d])
                if nd["kind"] == "dma":
                    q = self.dq[E]
                    idx = q["n"] % KDMA
                    q["n"] += 1
                    key = f"d_{E}{idx}"
                    if q["cnts"][idx] > 0:
                        deps.append((key, q["cnts"][idx]))
                    self._wait(E, deps)
                    out, in_ = nd["call"]
                    ins = e["h"].dma_start(out=out, in_=in_)
                    q["cnts"][idx] += 16
                    ins.then_inc(self.sems[key], 16)
                    ticket[i] = (key, q["cnts"][idx])
                else:
                    if E == "pe" or not SAME_ENGINE_SYNC:
                        deps = [d for d in deps if d[0] != "e_" + E]
                    self._wait(E, deps)
                    name, a, kw = nd["call"]
                    ins = getattr(e["h"], name)(*a, **kw)
                    ticket[i] = ("e_" + E, e["cnt"] + 1)
                    if nd["sig"]:
                        ins.then_inc(self.sems["e_" + E], 1)
                        e["cnt"] += 1
        for i, nd in enumerate(rec):
            self._reg(ticket[i], nd["reads"], nd["writes"])
            for b in nd["reads"]:
                b.rw = []
                b.rr = []
            for b in nd["writes"]:
                b.rw = []
                b.rr = []
        self.rec = []

    def buf(self, name=None):
        self.nbuf += 1
        return Buf(name or f"b{self.nbuf}")

    def sb(self, name, shape, dt):
        t = self.es.enter_context(self.nc.sbuf_tensor(name, list(shape), dt))
        return t

    def _wait(self, E, deps):
        e = self.eng[E]
        need = {}
        for k, v in deps:
            if v > need.get(k, 0):
                need[k] = v
        for k, v in need.items():
            if e["seen"].get(k, 0) >= v:
                continue
            e["h"].wait_ge(self.sems[k], v)
            e["seen"][k] = v

    @staticmethod
    def _deps(reads, writes):
        deps = []
        for b in reads:
            deps.extend(b.w.items())
        for b in writes:
            deps.extend(b.w.items())
            deps.extend(b.r.items())
        return deps

    @staticmethod
    def _reg(tk, reads, writes):
        k, v = tk
        for b in reads:
            if b.r.get(k, 0) < v:
                b.r[k] = v
        for b in writes:
            if b.w.get(k, 0) < v:
                b.w[k] = v

    def _rec_add(self, node, reads, writes):
        i = len(self.rec)
        deps = set()
        for b in reads:
            deps.update(b.rw)
        nw = os.environ.get("SIM_NOWAR", "")
        for b in writes:
            deps.update(b.rw)
            if not (nw == "1" or (nw and any(b.name.startswith(x) for x in nw.split(",") if x))):
                deps.update(b.rr)
        node["deps"] = deps
        node["reads"] = list(reads)
        node["writes"] = list(writes)
        self.rec.append(node)
        for b in reads:
            b.rr.append(i)
        for b in writes:
            b.rw = [i]
            b.rr = []

    def op(self, E, fn, reads=(), writes=(), sig=True):
        if self.recording:
            px = _Proxy()
            fn(px)
            self._rec_add(dict(E=E, kind="op", call=px.call, sig=sig, dur=_cost(E, px.call)), reads, writes)
            return None
        e = self.eng[E]
        deps = self._deps(reads, writes)
        if E == "pe" or not SAME_ENGINE_SYNC:
            deps = [d for d in deps if d[0] != "e_" + E]
        self._wait(E, deps)
        ins = fn(e["h"])
        tk = ("e_" + E, e["cnt"] + 1)
        if sig:
            ins.then_inc(self.sems["e_" + E], 1)
            e["cnt"] += 1
        self._reg(tk, reads, writes)
        return ins

    def dma(self, Q, out, in_, reads=(), writes=()):
        if self.recording:
            self._rec_add(dict(E=Q, kind="dma", call=(out, in_), sig=True, dur=100.0), reads, writes)
            return
        q = self.dq[Q]
        e = self.eng[Q]
        idx = q["n"] % KDMA
        q["n"] += 1
        key = f"d_{Q}{idx}"
        deps = self._deps(reads, writes)
        if q["cnts"][idx] > 0:
            deps.append((key, q["cnts"][idx]))
        self._wait(Q, deps)
        ins = e["h"].dma_start(out=out, in_=in_)
        q["cnts"][idx] += 16
        ins.then_inc(self.sems[key], 16)
        self._reg((key, q["cnts"][idx]), reads, writes)

    def barrier(self):
        deps = []
        for i in range(KDMA):
            c = self.dq["sp"]["cnts"][i]
            if c:
                deps.append((f"d_sp{i}", c))
        for name in ("pe", "act", "dve", "pool", "sp"):
            c = self.eng[name]["cnt"]
            if c:
                deps.append(("e_" + name, c))
        for name in ("pe", "act", "dve", "pool", "sp"):
            self._wait(name, [d for d in deps if d[0] != "e_" + name])

    def finish(self):
        deps = []
        for q in ("sp", "pool"):
            for i in range(KDMA):
                c = self.dq[q]["cnts"][i]
                if c:
                    deps.append((f"d_{q}{i}", c))
        for name in ("pe", "act", "dve", "pool"):
            c = self.eng[name]["cnt"]
            if c:
                deps.append(("e_" + name, c))
        self._wait("sp", deps)


def build(L=2048, dbg=None):
    NTOK = L + LS
    NB = L // 128
    nc = bass.Bass("TRN2", target_bir_lowering=False)
    dbg = dbg or {}

    def din(name, shape, dt=F32):
        return nc.dram_tensor(name, list(shape), dt, kind="ExternalInput").ap()

    def dout(name, shape, dt=F32):
        return nc.dram_tensor(name, list(shape), dt, kind="ExternalOutput").ap()

    x_p = din("x_p", [L, D]); x_s = din("x_s", [LS, D])
    c_ps = din("c_ps", [16, 128])
    st_conv = din("st_conv", [DEPTH, 36, 128])
    st_delta = din("st_delta", [DEPTH, 4, 128, 128])
    ada_w = din("ada_w", [DEPTH, D, 6 * D])
    rowsA = din("rowsA", [DEPTH, 113, 128])
    w_in = din("w_in", [DEPTH, D, INW])
    sgu_norm_g = din("sgu_norm_g", [DEPTH, 512])
    sgu_w = din("sgu_w", [DEPTH, 4, 128, 128])
    sgu_b = din("sgu_b", [DEPTH, 512])
    dt_bias = din("dt_bias", [DEPTH, 4]); a_log = din("a_log", [DEPTH, 4])
    w_out = din("w_out", [DEPTH, D, D]); w_up = din("w_up", [DEPTH, D, DFF]); w_down = din("w_down", [DEPTH, DFF, D])
    fin_g = din("fin_g", [1, D])
    k_ident = din("k_ident", [128, 128]); k_masks = din("k_masks", [128, 3, 128])
    k_sgum = din("k_sgum", [128, 128])
    k_cm = din("k_cm", [128, 6, 128], BF16)

    y_p = dout("y_p", [L, D]); y_s = dout("y_s", [LS, D])
    o_pconv = dout("o_pconv", [DEPTH, 3, 1536]); o_pdelta = dout("o_pdelta", [DEPTH, 4, 128, 128])
    o_sconv = dout("o_sconv", [DEPTH, 3, 1536]); o_sdelta = dout("o_sdelta", [DEPTH, 4, 128, 128])
    o_sgu = dout("o_sgu", [DEPTH, LS, 512])
    dbg_outs = {k: dout("dbg_" + k, shp) for k, shp in dbg.items()}

    es = ExitStack()
    with es:
        cx = Ctx(nc, es)
        sb = cx.sb
        B = cx.buf

        psf = es.enter_context(nc.psum_tensor("psf", [128, 7, 512], F32))
        psb = es.enter_context(nc.psum_tensor("psb", [128, 1024], BF16))
        pbank = [B(f"psbank{i}") for i in range(7)]
        b_psb = B("psb")
        pstate = dict(n=0)

        rings = {"X": (0, 7), "F": (0, 3), "B": (3, 4)}
        rpos_ps = {"X": 0, "F": 0, "B": 0}

        def PS(nb=1, ring="X"):
            base, size = rings[ring]
            i = rpos_ps[ring] % size
            if i + nb > size:
                i = 0
            rpos_ps[ring] = i + nb
            return base + i, pbank[base + i:base + i + nb]

        ident_f = sb("ident_f", [128, 128], F32); b_ident = B()
        ident_b = sb("ident_b", [128, 128], BF16)
        masks = sb("masks", [128, 3, 128], F32); b_masks = B()
        sgum = sb("sgum", [128, 128], F32)
        cmask = sb("cmask", [128, 6, 128], BF16)
        ones_b = sb("ones_b", [128, 128], BF16)
        ones_f = sb("ones_f", [128, 128], F32)
        b_const = B("const")
        cx.dma("sp", ident_f[:], k_ident, writes=[b_const])
        cx.dma("sp", masks[:], k_masks, writes=[b_const])
        cx.dma("sp", sgum[:], k_sgum, writes=[b_const])
        cx.dma("sp", cmask[:], k_cm, writes=[b_const])
        cx.op("dve", lambda h: h.memset(ones_b[:], 1.0), writes=[b_const])
        cx.op("dve", lambda h: h.memset(ones_f[:], 1.0), writes=[b_const])
        cx.op("dve", lambda h: h.tensor_copy(out=ident_b[:], in_=ident_f[:]), reads=[b_const], writes=[b_const])
        NUI = masks[:, 0, :]; NUS = masks[:, 1, :]; LSB = masks[:, 2, :]

        def bc4(ap2d, n=4, w=128, p=128):
            return ap2d.unsqueeze(1).to_broadcast([p, n, w])

        xT = sb("xT", [128, 8, NTOK], F32)
        b_x = [B(f"x{i}") for i in range(NB + 1)]
        WA = sb("WA", [128, 8 * INW + 8 * D], BF16)
        win = WA[:, 0:8 * INW].rearrange("p (k n) -> p k n", k=8)
        wout = WA[:, 8 * INW:8 * INW + 8 * D].rearrange("p (k n) -> p k n", k=8)
        SLOT = 8192
        b_slot = [B(f"slot{i}") for i in range(4)]
        slot_off = [8 * INW, 0, SLOT, 2 * SLOT]
        slot_buf = [[b_slot[0]], [b_slot[1]], [b_slot[2]], [b_slot[3]]]
        L_win = [b_slot[1], b_slot[2], b_slot[3]]; L_wout = [b_slot[0]]

        RBYTES = int(os.environ.get('RKB', 61)) * 1024
        R = sb("R", [128, RBYTES // 4], F32)
        rpos = dict(a=0, b=0)

        def RA(name, free_shape, dt, which="a"):
            n = int(np.prod(free_shape))
            words = (n * (2 if dt == BF16 else 4) + 3) // 4
            words = (words + 7) // 8 * 8
            o = rpos[which]
            rpos[which] = o + words
            assert rpos[which] * 4 <= RBYTES, (name, rpos[which] * 4)
            ap = R[:, o:o + words]
            if dt == BF16:
                ap = ap.bitcast(BF16)[:, 0:n]
            else:
                ap = ap[:, 0:n]
            if len(free_shape) == 2:
                ap = ap.rearrange("p (a b) -> p a b", a=free_shape[0])
            elif len(free_shape) == 3:
                ap = ap.rearrange("p (a b c) -> p a b c", a=free_shape[0], b=free_shape[1])
            return ap

        pcols = sb("pcols", [128, 113], F32); b_pcols = B("pcols")
        rowsA_sb = sb("rowsA_sb", [113, 128], F32); b_rowsA = B()
        rowsB_sb = sb("rowsB_sb", [52, 128], F32); b_rowsB = B()
        colsB = sb("colsB", [128, 52], F32); b_colsB = B("colsB")
        csT = sb("csT", [128, 2, 8], BF16); b_cs = B("cs")
        modT = sb("modT", [128, 48, 2], F32); b_mod = B("mod")
        lay = sb("lay", [128, 6, 2, 8], F32); b_lay = B("lay")
        WmT = sb("WmT", [128, 4, 128], BF16); b_wm = B("wm")
        BSRB = sb("BSRB", [128, 4, 128], BF16)
        SGBC = sb("SGBC", [128, 512], F32)
        hd = sb("hd", [128, 16], F32)
        b_lp = B("layerparams")
        FING = WA[:, 0:2 * D].bitcast(F32); b_fing = B()
        S = sb("S", [128, 4, 128], F32); S_bf = sb("S_bf", [128, 4, 128], BF16); b_S = B("S"); b_Sbf = B("Sbf")
        XP = sb("XP", [128, 12, 131], BF16); b_XP = B("XP")

        ACT = lambda fn, **kw: cx.op("act", fn, **kw)
        DVE = lambda fn, **kw: cx.op("dve", fn, **kw)
        POOL = lambda fn, **kw: cx.op("pool", fn, **kw)
        PE = lambda fn, **kw: cx.op("pe", fn, **kw)

        def mm(out, lhsT, rhs, start, stop, reads, writes, sig):
            PE(lambda h: h.matmul(out, lhsT=lhsT, rhs=rhs, start=start, stop=stop),
               reads=reads, writes=writes, sig=sig)

        def tr(out, in_, K, reads, writes, sig, dt=F32):
            idn = (ident_f if dt == F32 else ident_b)[0:K, 0:K]
            PE(lambda h: h.transpose(out, in_, idn), reads=list(reads) + [b_const], writes=writes, sig=sig)

        def bmerge(dst, src):
            for k_, v_ in src.w.items():
                if dst.w.get(k_, 0) < v_:
                    dst.w[k_] = v_
            for k_, v_ in src.r.items():
                if dst.r.get(k_, 0) < v_:
                    dst.r[k_] = v_
            dst.rw = list(set(dst.rw) | set(src.rw))
            dst.rr = list(set(dst.rr) | set(src.rr))

        def dump(name, ap, reads):
            if name in dbg_outs:
                cx.dma("sp", dbg_outs[name], ap, reads=reads)

        hT = RA("hT", [8, 128], BF16); b_hT = B("hT")
        QA = RA("QA", [12, 128], F32); b_QA = B("QA")
        b_QAg = [B("QAg0"), B("QAg1"), B("QAg2")]
        QKN = RA("QKN", [8, 128], BF16); b_QKN = B("QKN")
        vTb = RA("vTb", [4, 128], BF16); b_vT = B("vT")
        gcRB = RA("gcRB", [4, 128], F32); b_gc = B("gcRB")
        BMt = RA("BMt", [4, 128], BF16); b_BM = B("BMt")
        qgT_l = [RA("qgT" + str(i), [4, 128], BF16) for i in range(2)]; b_qg_l = [B("qgT" + str(i)) for i in range(2)]
        TM = RA("TM", [16], F32); b_TM = B("TM")
        TM2_l = [RA("TM2" + str(i), [24], F32) for i in range(2)]; b_TM2_l = [B("TM2" + str(i)) for i in range(2)]
        absb = RA("absb", [128], F32); b_absb = B("absb")
        PQ = RA("PQ", [4, 2, 128], BF16); b_PQ = B("PQ")
        PTQ = RA("PTQ", [4, 2, 128], BF16); b_PTQ = B("PTQ")
        NM_l = [RA("NM" + str(i), [4, 2, 128], BF16) for i in range(2)]; b_NM_l = [B("NM" + str(i)) for i in range(2)]
        attnT_l = [RA("attnT" + str(i), [4, 128], BF16) for i in range(2)]; b_attn_l = [B("attnT" + str(i)) for i in range(2)]
        kbg_l = [RA("kbg" + str(i), [4, 128], BF16) for i in range(2)]; b_kbg_l = [B("kbg" + str(i)) for i in range(2)]
        kg_l = [RA("kg" + str(i), [4, 128], BF16) for i in range(2)]; b_kg_l = [B("kg" + str(i)) for i in range(2)]
        vb_l = [RA("vb" + str(i), [4, 128], BF16) for i in range(2)]; b_vb_l = [B("vb" + str(i)) for i in range(2)]
        wTb = RA("wTb", [4, 128], BF16); b_wT = B("wT")
        vnew = RA("vnew", [4, 128], BF16); b_vn = B("vnew")
        zs_l = [RA("zs" + str(i), [4, 128], BF16) for i in range(2)]; b_zs_l = [B("zs" + str(i)) for i in range(2)]
        uT = RA("uT", [4, 128], BF16); b_uT = B("uT")
        vnb = RA("vnb", [512], BF16); b_vnb = B("vnb")
        ABT_l = [RA("ABT" + str(i), [8, 128], BF16) for i in range(2)]; b_AB_l = [B("ABT" + str(i)) for i in range(2)]
        sml = RA("sml", [8], F32); b_sml = B("sml")
        NTF = int(os.environ.get('NTF', 7)); NTB = int(os.environ.get('NTB', 2)); NTMP = NTF + NTB
        tmp_o = rpos["a"]
        tmpF = [RA(f"tmpF{i}", [512], F32) for i in range(NTMP)]
        b_tmp = [B(f"tmp{i}") for i in range(NTMP)]
        xstage = R[:, tmp_o:tmp_o + 1024]
        adaSt = [R[:, i * 1024:(i + 1) * 1024].bitcast(BF16).rearrange("p (k n) -> p k n", k=8) for i in range(2)]
        b_adaSt = [B("adaSt0"), B("adaSt1")]
        adaSt8 = [R[:, i * 1024:(i + 1) * 1024].bitcast(BF16).rearrange("p (k n) -> p k n", k=8) for i in range(8)]
        b_adaSt8 = [B(f"adaSt8_{i}") for i in range(8)]
        assert 8 * 1024 <= tmp_o
        tstate = dict(n=0)
        b_xst = [b_tmp[0], b_tmp[1]]

        tpos = {"F": 0, "B": 0, "b": 0}

        def TMP(pool="F"):
            if tstate.get("mode", "a") == "b":
                pool_b = tstate.get("bpool")
                if pool_b:
                    i = tpos["b"] % len(pool_b)
                    tpos["b"] += 1
                    return pool_b[i]
                i = tpos["b"] % 4
                tpos["b"] += 1
                return tmpFb[i], b_tmpb[i]
            if pool == "F":
                i = tpos["F"] % NTF
            else:
                i = NTF + tpos["B"] % NTB
            tpos[pool] += 1
            return tmpF[i], b_tmp[i]

        hTall = RA("hTall", [8, NTOK], BF16, "b"); b_hall_l = [B(f"hTall{i}") for i in range(NB + 1)]
        hidf = [RA(f"hidf{i}", [512], F32, "b") for i in range(2)]; b_hidf = [B(), B()]
        hidb = [RA(f"hidb{i}", [4, 512], BF16, "b") for i in range(2)]; b_hidb = [B(), B()]
        adaStB = RA("adaStB", [8, 256], BF16, "b"); b_adaStB = B("adaStB")
        tmpFb = [RA(f"tmpFb{i}", [512], F32, "b") for i in range(4)]
        b_tmpb = [B(f"tmpb{i}") for i in range(4)]

        def load_x(bi):
            tw = 128 if bi < NB else LS
            src = x_p[bi * 128:(bi + 1) * 128, :] if bi < NB else x_s
            xs_, bxs_ = (xstage, b_xst) if bi % 2 == 0 else (xstage2, b_xst2)
            cx.dma("sp", xs_[0:tw, :], src, writes=bxs_)
            for half in range(2):
                bk, pb = PS(1)
                for c in range(4):
                    cc = half * 4 + c
                    tr(psf[:, bk, c * 128:c * 128 + tw], xs_[0:tw, cc * 128:(cc + 1) * 128], tw,
                       reads=bxs_, writes=pb, sig=(c == 3))
                o = psf[:, bk, :].rearrange("p (c t) -> p c t", c=4)[:, :, 0:tw]
                ACT(lambda h: h.copy(out=xT[:, half * 4:half * 4 + 4, bi * 128:bi * 128 + tw], in_=o),
                    reads=pb, writes=[b_x[bi]])

        cx.dma("sp", rowsB_sb[36:52, :], c_ps, writes=[b_rowsB])

        def layer_params(l):
            cx.dma("sp", rowsA_sb[:], rowsA[l], writes=[b_rowsA])
            cx.dma("sp", rowsB_sb[0:36, :], st_conv[l], writes=[b_rowsB])
            bk, pb = PS(1)
            tr(psf[:, bk, 0:113], rowsA_sb[:], 113, reads=[b_rowsA], writes=pb, sig=True)
            DVE(lambda h: h.tensor_copy(out=pcols[:], in_=psf[:, bk, 0:113]), reads=pb, writes=[b_pcols])
            bk, pb = PS(1)
            tr(psf[:, bk, 0:52], rowsB_sb[:], 52, reads=[b_rowsB], writes=pb, sig=True)
            DVE(lambda h: h.tensor_copy(out=colsB[:], in_=psf[:, bk, 0:52]), reads=pb, writes=[b_colsB])
            if l == 0:
                ACT(lambda h: h.activation(out=csT[:].rearrange("p s k -> p (s k)"), in_=colsB[:, 36:52], func=AF.Silu),
                    reads=[b_colsB], writes=[b_cs])
            cx.dma("sp", hd[:, 0:4], dt_bias[l:l + 1, :].partition_broadcast(128), writes=[b_lp])
            cx.dma("sp", hd[:, 4:8], a_log[l:l + 1, :].partition_broadcast(128), writes=[b_lp])
            ACT(lambda h: h.activation(out=hd[:, 8:12], in_=hd[:, 4:8], func=AF.Exp), reads=[b_lp], writes=[b_lp])
            DVE(lambda h: h.tensor_scalar(out=hd[:, 8:12], in0=hd[:, 8:12], scalar1=-1.0, scalar2=None, op0=ALU.mult),
                reads=[b_lp], writes=[b_lp])
            cx.dma("sp", SGBC[:], sgu_norm_g[l:l + 1, :].partition_broadcast(128), writes=[b_lp])
            cx.dma("pool", BSRB[:].rearrange("p g q -> p (g q)"), sgu_b[l:l + 1, :].partition_broadcast(128), writes=[b_lp])
            t0, bt0 = TMP()
            cx.dma("sp", t0.rearrange("p (g q) -> p g q", g=4), sgu_w[l].rearrange("g p q -> p g q"), writes=[bt0])
            bk, pb = PS(1)
            for g in range(4):
                tr(psf[:, bk, g * 128:(g + 1) * 128], t0[:, g * 128:(g + 1) * 128], 128, reads=[bt0], writes=pb, sig=(g == 3))
            DVE(lambda h: h.tensor_tensor(out=WmT[:], in0=psf[:, bk, :].rearrange("p (g q) -> p g q", g=4),
                                          in1=bc4(sgum[:]), op=ALU.mult), reads=pb + [b_const], writes=[b_wm])

        def ada_chunk(l, ch, st, bst):
            cx.dma("pool", st, ada_w[l].rearrange("(k p) n -> p k n", p=128)[:, :, ch * 256:(ch + 1) * 256], writes=[bst])
            bk, pb = PS(1)
            for j in range(2):
                for k in range(8):
                    mm(psf[:, bk, j * 2:j * 2 + 2], st[:, k, j * 128:(j + 1) * 128], csT[:, :, k],
                       k == 0, k == 7, reads=[bst, b_cs], writes=pb, sig=(k == 7 and j == 1))
            DVE(lambda h: h.tensor_tensor(out=modT[:, ch * 2:ch * 2 + 2, :],
                                          in0=psf[:, bk, 0:4].rearrange("p (j s) -> p j s", s=2),
                                          in1=pcols[:, 16 + ch * 2:16 + ch * 2 + 2].unsqueeze(2).to_broadcast([128, 2, 2]),
                                          op=ALU.add), reads=pb + [b_pcols], writes=[b_mod])

        def ada(l):
            for ch in range(24):
                ada_chunk(l, ch, adaSt8[ch % 8], b_adaSt8[ch % 8])

        def layer_cols(l):
            for s in range(2):
                for (dst, jsc, jsh, jgt, gof) in ((0, 8, 0, 16, 0), (3, 32, 24, 40, 8)):
                    DVE(lambda h: h.scalar_tensor_tensor(out=lay[:, dst, s, :], in0=modT[:, jsc:jsc + 8, s], scalar=1.0,
                                                         in1=pcols[:, gof:gof + 8], op0=ALU.add, op1=ALU.mult),
                        reads=[b_mod, b_pcols], writes=[b_lay])
                    DVE(lambda h: h.tensor_copy(out=lay[:, dst + 1, s, :], in_=modT[:, jsh:jsh + 8, s]), reads=[b_mod], writes=[b_lay])
                    DVE(lambda h: h.tensor_copy(out=lay[:, dst + 2, s, :], in_=modT[:, jgt:jgt + 8, s]), reads=[b_mod], writes=[b_lay])

        def win_chunk_slots(k):
            lo, hi = k * INW, (k + 1) * INW
            sl = []
            for j in (1, 2, 3):
                a0, a1 = (j - 1) * SLOT, j * SLOT
                if lo < a1 and hi > a0:
                    sl.append(j)
            if hi > 3 * SLOT and 3 not in sl:
                sl.append(3)
            return sl

        def load_win_chunks(l, ks):
            for k in ks:
                cx.dma("pool", win[:, k:k + 1, :], w_in[l].rearrange("(k p) n -> p k n", p=128)[:, k:k + 1, :],
                       writes=[b_slot[j] for j in win_chunk_slots(k)])

        def load_wout(l):
            for k0 in range(0, 8, 4):
                cx.dma("pool", wout[:, k0:k0 + 4, :], w_out[l].rearrange("(k p) n -> p k n", p=128)[:, k0:k0 + 4, :],
                       writes=L_wout)

        def load_wA(l):
            load_win_chunks(l, range(8))
            load_wout(l)

        def rstd_from_ps(ps_ap, width, scale, bias_ln, dst, rd, wr):
            ACT(lambda h: h.activation(out=dst, in_=ps_ap, func=AF.Ln, bias=EPS, scale=scale), reads=rd, writes=wr)
            ACT(lambda h: h.activation(out=dst, in_=dst, func=AF.Exp, scale=-0.5, bias=bias_ln), reads=wr, writes=wr)

        def norm_block(col0, tw, s, a_idx, dst, b_dst, xbufs, ring="X"):
            sq, bsq = TMP(); sq2, bsq2 = TMP()
            sqv = sq.bitcast(BF16).rearrange("p (c t) -> p c t", c=8)
            ACT(lambda h: h.activation(out=sqv[:, :, 0:tw], in_=xT[:, :, col0:col0 + tw], func=AF.Square),
                reads=xbufs, writes=[bsq])
            bk, pb = PS(1, ring)
            for c in range(8):
                mm(psf[:, bk, 0:tw], ones_b[:], sqv[:, c, 0:tw], c == 0, c == 7, reads=[bsq, b_const], writes=pb, sig=(c == 7))
            rstd_from_ps(psf[:, bk, 0:tw], tw, 1.0 / D, 0.0, sq2[:, 0:tw], pb, [bsq2])
            t3, bt3 = TMP(); t4, bt4 = TMP()
            rbc = sq2[:, 0:tw].unsqueeze(1).to_broadcast([128, 4, tw])
            for hf, (tq, bq) in enumerate(((t3, bt3), (t4, bt4))):
                tqv = tq.rearrange("p (c t) -> p c t", c=4)[:, :, 0:tw]
                DVE(lambda h: h.tensor_tensor(out=tqv, in0=xT[:, hf * 4:hf * 4 + 4, col0:col0 + tw], in1=rbc, op=ALU.mult),
                    reads=xbufs + [bsq2], writes=[bq])
            for c in range(8):
                tt = (t3 if c < 4 else t4)[:, (c % 4) * 128:(c % 4) * 128 + tw]
                bb = bt3 if c < 4 else bt4
                if c % 2 == 0:
                    ACT(lambda h: h.activation(out=dst[:, c, 0:tw], in_=tt, func=AF.Identity, bias=lay[:, a_idx + 1, s, c:c + 1],
                                               scale=lay[:, a_idx, s, c:c + 1]), reads=[bb, b_lay], writes=[b_dst])
                else:
                    DVE(lambda h: h.tensor_scalar(out=dst[:, c, 0:tw], in0=tt, scalar1=lay[:, a_idx, s, c:c + 1],
                                                  scalar2=lay[:, a_idx + 1, s, c:c + 1], op0=ALU.mult, op1=ALU.add),
                        reads=[bb, b_lay], writes=[b_dst])

        def phaseA_block(l, bi):
            smp = bi == NB
            tw = LS if smp else 128
            s = 1 if smp else 0
            col0 = bi * 128
            xb = [b_x[bi]]
            rdw = L_win + [b_hT]
            par = bi % 2
            NM, b_NM = NM_l[par], b_NM_l[par]; attnT, b_attn = attnT_l[par], b_attn_l[par]
            kbg, b_kbg = kbg_l[par], b_kbg_l[par]; kg, b_kg = kg_l[par], b_kg_l[par]; vb, b_vb = vb_l[par], b_vb_l[par]
            qgT, b_qg = qgT_l[par], b_qg_l[par]; zs, b_zs = zs_l[par], b_zs_l[par]; ABT, b_AB = ABT_l[par], b_AB_l[par]
            TM2, b_TM2 = TM2_l[par], b_TM2_l[par]
            RG = "F"
            pres = {}

            norm_block(col0, tw, s, 0, hT, b_hT, xb, ring="F")
            yield 1
            if l == 0 and bi == 0:
                dump("hT", hT, [b_hT])

            def proj(c0, nch, M=128):
                bk, pb = PS(1, RG)
                for j in range(nch):
                    for k in range(8):
                        mm(psf[0:M, bk, j * 128:j * 128 + tw], win[:, k, c0 + j * 128:c0 + j * 128 + M], hT[:, k, 0:tw],
                           k == 0, k == 7, reads=rdw, writes=pb, sig=(k == 7))
                    if j < nch - 1:
                        yield 1
                pres["v"] = (bk, pb)

            def pv(bk, nch=4):
                return psf[:, bk, :].rearrange("p (c t) -> p c t", c=4)[:, 0:nch, 0:tw]

            if smp:
                DVE(lambda h: h.tensor_copy(out=XP[:, :, 0:3], in_=colsB[:, 0:36].rearrange("p (i c) -> p c i", i=3)),
                    reads=[b_colsB], writes=[b_XP])
            elif bi == 0:
                DVE(lambda h: h.memset(XP[:, :, 0:3], 0.0), writes=[b_XP])
            for g3 in range(3):
                yield from proj(1024 + g3 * 512, 4)
                bk, pb = pres["v"]
                yield 1
                eng = ACT if g3 != 1 else DVE
                if eng is ACT:
                    ACT(lambda h: h.copy(out=XP[:, g3 * 4:g3 * 4 + 4, 3:3 + tw], in_=pv(bk)), reads=pb, writes=[b_XP])
                else:
                    DVE(lambda h: h.tensor_copy(out=XP[:, g3 * 4:g3 * 4 + 4, 3:3 + tw], in_=pv(bk)), reads=pb, writes=[b_XP])
            if smp or bi == NB - 1:
                dst = o_sconv if smp else o_pconv
                for g3 in range(3):
                    for c in range(4):
                        tr(psb[0:3, c * 128:(c + 1) * 128], XP[:, g3 * 4 + c, tw:tw + 3], 128, reads=[b_XP], writes=[b_psb], sig=(c == 3), dt=BF16)
                    t0, bt0 = TMP(RG)
                    DVE(lambda h: h.tensor_copy(out=t0[0:3, :], in_=psb[0:3, 0:512]), reads=[b_psb], writes=[bt0])
                    cx.dma("sp", dst[l][:, g3 * 512:(g3 + 1) * 512], t0[0:3, :], reads=[bt0])
            yield 1
            def cwb(g, i):
                return pcols[:, 64 + i * 12 + g * 4:64 + i * 12 + g * 4 + 4].unsqueeze(2).to_broadcast([128, 4, tw])
            def CE(g):
                return DVE if g == 2 else POOL
            for g in range(3):
                bmerge(b_QAg[g], b_QA)
            for g in range(3):
                CE(g)(lambda h: h.tensor_tensor(out=QA[:, g * 4:g * 4 + 4, 0:tw], in0=XP[:, g * 4:g * 4 + 4, 0:tw], in1=cwb(g, 0), op=ALU.mult),
                      reads=[b_XP, b_pcols], writes=[b_QAg[g]])
            for i in range(1, 4):
                tl = []
                for g in range(3):
                    t_, bt_ = TMP(RG)
                    tv_ = t_.rearrange("p (c t) -> p c t", c=4)[:, :, 0:tw]
                    CE(g)(lambda h: h.tensor_tensor(out=tv_, in0=XP[:, g * 4:g * 4 + 4, i:i + tw], in1=cwb(g, i), op=ALU.mult),
                          reads=[b_XP, b_pcols], writes=[bt_])
                    tl.append((tv_, bt_))
                for g in range(3):
                    tv_, bt_ = tl[g]
                    CE(g)(lambda h: h.tensor_tensor(out=QA[:, g * 4:g * 4 + 4, 0:tw], in0=QA[:, g * 4:g * 4 + 4, 0:tw], in1=tv_, op=ALU.add),
                          reads=[bt_, b_QAg[g]], writes=[b_QAg[g]])
                yield 1
            for g in range(3):
                bmerge(b_QA, b_QAg[g])
            if not smp:
                POOL(lambda h: h.tensor_copy(out=XP[:, :, 0:3], in_=XP[:, :, 128:131]), reads=[b_XP, b_QA], writes=[b_XP])
            yield 1
            yield from proj(0, 4)
            bk, pb = pres["v"]
            ACT(lambda h: h.activation(out=uT[:, :, 0:tw], in_=pv(bk), func=AF.Gelu), reads=pb, writes=[b_uT])
            yield 1
            bk, pb = PS(1, RG)
            for k in range(8):
                mm(psf[0:tw, bk, :], hT[:, k, 0:tw], win[:, k, 512:1024], k == 0, k == 7, reads=rdw, writes=pb, sig=(k == 7))
            vg, bvg = TMP(RG)
            ACT(lambda h: h.activation(out=vg[0:tw, :], in_=psf[0:tw, bk, :], func=AF.Gelu), reads=pb, writes=[bvg])
            junk, bj = TMP(RG)
            ACT(lambda h: h.activation(out=junk[0:tw, :], in_=vg[0:tw, :], func=AF.Square, accum_out=sml[0:tw, 0:1]),
                reads=[bvg], writes=[bj, b_sml])
            rstd_from_ps(sml[0:tw, 0:1], 1, 1.0 / 512, 0.0, sml[0:tw, 1:2], [b_sml], [b_sml])
            vnf, b_vnf = TMP(RG)
            DVE(lambda h: h.scalar_tensor_tensor(out=vnf[0:tw, :], in0=vg[0:tw, :], scalar=sml[0:tw, 1:2], in1=SGBC[0:tw, :],
                                                 op0=ALU.mult, op1=ALU.mult), reads=[bvg, b_sml, b_lp], writes=[b_vnf])
            ACT(lambda h: h.copy(out=vnb[0:tw, :], in_=vnf[0:tw, :]), reads=[b_vnf], writes=[b_vnb])
            if smp:
                cx.dma("sp", o_sgu[l], vnf[0:LS, :], reads=[b_vnf])
            yield 1
            bk, pb = PS(1, RG)
            for g in range(4):
                mm(psf[:, bk, g * 128:g * 128 + tw], vnb[0:tw, g * 128:(g + 1) * 128], WmT[0:tw, g, 0:tw], True, True,
                   reads=[b_vnb, b_wm], writes=pb, sig=(g == 3))
            t0, bt0 = TMP(RG)
            t0v = t0.rearrange("p (g q) -> p g q", g=4)[:, :, 0:tw]
            DVE(lambda h: h.tensor_tensor(out=t0v, in0=pv(bk), in1=BSRB[:, :, 0:tw], op=ALU.add), reads=pb + [b_lp], writes=[bt0])
            DVE(lambda h: h.tensor_tensor(out=ABT[:, 0:4, 0:tw], in0=t0v, in1=uT[:, :, 0:tw], op=ALU.mult),
                reads=[bt0, b_uT], writes=[b_AB])
            yield 1
            yield from proj(3072, 1, M=8)
            bk, pb = pres["v"]
            ACT(lambda h: h.copy(out=absb[0:8, 0:tw], in_=psf[0:8, bk, 0:tw]), reads=pb, writes=[b_absb])
            bk2, pb2 = PS(2, RG)
            rbv = psf[:, bk2:bk2 + 2, :].rearrange("p b (r t) -> p (b r) t", r=4)
            for r0 in (0, 4):
                am_, bam = TMP(RG)
                am = am_.rearrange("p (r t) -> p r t", r=4)
                DVE(lambda h: h.tensor_tensor(out=am[0:8, :, 0:tw], in0=absb[0:8, 0:tw].unsqueeze(1).to_broadcast([8, 4, tw]),
                                              in1=ident_f[0:8, r0:r0 + 4].unsqueeze(2).to_broadcast([8, 4, tw]), op=ALU.mult),
                    reads=[b_absb, b_const], writes=[bam])
                for r in range(4):
                    mm(rbv[:, r0 + r, 0:tw], ones_f[0:8, :], am[0:8, r, 0:tw], True, True, reads=[bam, b_const], writes=pb2, sig=(r == 3))
            betaRB_, b_beta = TMP(RG)
            betaRB = betaRB_.rearrange("p (g q) -> p g q", g=4)
            spb_, b_spb = TMP(RG)
            spb = spb_.rearrange("p (g q) -> p g q", g=4)
            ACT(lambda h: h.activation(out=spb[:, :, 0:tw], in_=rbv[:, 0:4, 0:tw], func=AF.Exp, scale=-1.0), reads=pb2, writes=[b_spb])
            ACT(lambda h: h.activation(out=spb[:, :, 0:tw], in_=spb[:, :, 0:tw], func=AF.Ln, bias=1.0, scale=1.0), reads=[b_spb], writes=[b_spb])
            ACT(lambda h: h.activation(out=betaRB[:, :, 0:tw], in_=spb[:, :, 0:tw], func=AF.Exp, scale=-1.0), reads=[b_spb], writes=[b_beta])
            sp_, bsp = TMP(RG)
            spv = sp_.rearrange("p (g q) -> p g q", g=4)
            for hh in range(4):
                ACT(lambda h: h.activation(out=spv[:, hh, 0:tw], in_=rbv[:, 4 + hh, 0:tw], func=AF.Exp, bias=hd[:, hh:hh + 1], scale=1.0),
                    reads=pb2 + [b_lp], writes=[bsp])
            ACT(lambda h: h.activation(out=spv[:, :, 0:tw], in_=spv[:, :, 0:tw], func=AF.Ln, bias=1.0, scale=1.0), reads=[bsp], writes=[bsp])
            for hh in range(4):
                DVE(lambda h: h.tensor_tensor_scan(out=gcRB[:, hh, 0:tw], data0=ones_f[:, 0:tw], data1=spv[:, hh, 0:tw],
                                                   initial=0.0, op0=ALU.mult, op1=ALU.add), reads=[bsp, b_const], writes=[b_gc])
            POOL(lambda h: h.tensor_tensor(out=gcRB[:, :, 0:tw], in0=gcRB[:, :, 0:tw], in1=hd[:, 8:12].unsqueeze(2).to_broadcast([128, 4, tw]),
                                           op=ALU.mult), reads=[b_gc, b_lp], writes=[b_gc])
            POOL(lambda h: h.tensor_tensor(out=BMt[0:tw, :, 0:tw], in0=betaRB[0:tw, :, 0:tw], in1=bc4(cmask[0:tw, 5, 0:tw], 4, tw, tw), op=ALU.mult),
                 reads=[b_beta, b_const], writes=[b_BM])
            yield 1
            bk, pb = PS(1, RG)
            for hh in range(4):
                mm(psf[0:tw, bk, hh:hh + 1], gcRB[0:1, hh, 0:tw], ones_f[0:1, 0:1], True, True, reads=[b_gc, b_const], writes=pb, sig=False)
                mm(psf[0:tw, bk, 4 + hh:5 + hh], betaRB[0:1, hh, 0:tw], ones_f[0:1, 0:1], True, True, reads=[b_beta, b_const],
                   writes=pb, sig=(hh == 3))
            DVE(lambda h: h.tensor_copy(out=TM[0:tw, 0:8], in_=psf[0:tw, bk, 0:8]), reads=pb, writes=[b_TM])
            DVE(lambda h: h.tensor_scalar(out=TM2[0:tw, 0:8], in0=TM[0:tw, 0:8], scalar1=-1.0, scalar2=None, op0=ALU.mult),
                reads=[b_TM], writes=[b_TM2])
            ACT(lambda h: h.activation(out=TM2[0:tw, 8:12], in_=TM[0:tw, 0:4], func=AF.Exp), reads=[b_TM], writes=[b_TM2])
            DVE(lambda h: h.tensor_tensor(out=TM2[0:tw, 8:12], in0=TM2[0:tw, 8:12], in1=TM[0:tw, 4:8], op=ALU.mult),
                reads=[b_TM, b_TM2], writes=[b_TM2])
            for hh in range(4):
                ACT(lambda h: h.activation(out=TM2[0:tw, 12 + hh:13 + hh], in_=TM[0:tw, hh:hh + 1], func=AF.Exp,
                                           bias=gcRB[0:tw, hh, tw - 1:tw], scale=-1.0), reads=[b_TM, b_gc], writes=[b_TM2])
                ACT(lambda h: h.activation(out=TM2[:, 16 + hh:17 + hh], in_=gcRB[:, hh, tw - 1:tw], func=AF.Exp), reads=[b_gc], writes=[b_TM2])

            yield 1
            yield from proj(2560, 4)
            bk, pb = pres["v"]
            ACT(lambda h: h.activation(out=zs[:, :, 0:tw], in_=pv(bk), func=AF.Silu), reads=pb, writes=[b_zs])
            yield 1
            ACT(lambda h: h.activation(out=QA[:, :, 0:tw], in_=QA[:, :, 0:tw], func=AF.Silu), reads=[b_QA], writes=[b_QA])
            ACT(lambda h: h.copy(out=vTb[:, :, 0:tw], in_=QA[:, 8:12, 0:tw]), reads=[b_QA], writes=[b_vT])
            yield 1
            for half in range(2):
                sq, bsq = TMP(RG)
                sqv = sq.bitcast(BF16)[:, 0:512].rearrange("p (c t) -> p c t", c=4)
                ACT(lambda h: h.activation(out=sqv[:, :, 0:tw], in_=QA[:, half * 4:half * 4 + 4, 0:tw], func=AF.Square),
                    reads=[b_QA], writes=[bsq])
                bk, pb = PS(1, RG)
                for c in range(4):
                    mm(psf[:, bk, c * 128:c * 128 + tw], ones_b[:], sqv[:, c, 0:tw], True, True, reads=[bsq, b_const], writes=pb, sig=(c == 3))
                rr, brr = TMP(RG)
                rrv = rr.rearrange("p (c t) -> p c t", c=4)[:, :, 0:tw]
                rstd_from_ps(pv(bk), tw, 1.0, (-0.5 * float(np.log(128.0))) if half == 0 else 0.0, rrv, pb, [brr])
                POOL(lambda h: h.tensor_tensor(out=QKN[:, half * 4:half * 4 + 4, 0:tw], in0=QA[:, half * 4:half * 4 + 4, 0:tw], in1=rrv,
                                               op=ALU.mult), reads=[b_QA, brr], writes=[b_QKN])
            eg, beg = TMP(RG)
            egv = eg.rearrange("p (c t) -> p c t", c=4)[:, :, 0:tw]
            ACT(lambda h: h.activation(out=egv, in_=gcRB[:, :, 0:tw], func=AF.Exp), reads=[b_gc], writes=[beg])
            POOL(lambda h: h.tensor_tensor(out=qgT[:, :, 0:tw], in0=QKN[:, 0:4, 0:tw], in1=egv, op=ALU.mult),
                 reads=[b_QKN, beg], writes=[b_qg])

            yield 1
            bkK, pbK = PS(1, RG); bkQ, pbQ = PS(1, RG)
            for hh in range(4):
                mm(psf[0:tw, bkK, hh * 128:hh * 128 + tw], QKN[:, 4 + hh, 0:tw], QKN[:, 4 + hh, 0:tw], True, True,
                   reads=[b_QKN], writes=pbK, sig=False)
                mm(psf[0:tw, bkQ, hh * 128:hh * 128 + tw], QKN[:, 4 + hh, 0:tw], QKN[:, hh, 0:tw], True, True,
                   reads=[b_QKN], writes=pbQ, sig=(hh == 3))
            KKv = psf[0:tw, bkK, :].rearrange("p (c t) -> p c t", c=4)[:, :, 0:tw]
            QKv = psf[0:tw, bkQ, :].rearrange("p (c t) -> p c t", c=4)[:, :, 0:tw]

            def etile(src, b_src, mask, sgn, bias_col0):
                e, be = TMP(RG)
                ev = e.rearrange("p (c t) -> p c t", c=4)[0:tw, :, 0:tw]
                POOL(lambda h: h.tensor_tensor(out=ev, in0=src[0:tw, :, 0:tw], in1=bc4(mask[0:tw, 0:tw], 4, tw, tw), op=ALU.add),
                     reads=[b_src, b_const], writes=[be])
                for hh in range(4):
                    bias = (TM if bias_col0 < 100 else TM2)[0:tw, (bias_col0 % 100) + hh:(bias_col0 % 100) + hh + 1]
                    ACT(lambda h: h.activation(out=ev[:, hh, :], in_=ev[:, hh, :], func=AF.Exp, bias=bias, scale=sgn),
                        reads=[be, b_TM, b_TM2], writes=[be])
                return ev, be

            eN, beN = etile(gcRB, b_gc, LSB, -1.0, 0)
            POOL(lambda h: h.tensor_tensor(out=eN, in0=eN, in1=TM2[0:tw, 4:8].unsqueeze(2).to_broadcast([tw, 4, tw]), op=ALU.mult),
                 reads=[beN, b_TM2], writes=[beN])
            DVE(lambda h: h.tensor_tensor(out=NM[0:tw, :, 0, 0:tw], in0=eN, in1=KKv, op=ALU.mult), reads=[beN] + pbK, writes=[b_NM])
            yield 1
            eA, beA = etile(gcRB, b_gc, NUI, 1.0, 100)
            DVE(lambda h: h.tensor_tensor(out=attnT[0:tw, :, 0:tw], in0=eA, in1=QKv, op=ALU.mult), reads=[beA] + pbQ, writes=[b_attn])
            eM_, beM = TMP(RG)
            eM = eM_.rearrange("p (c t) -> p c t", c=4)[0:tw, :, 0:tw]
            POOL(lambda h: h.tensor_tensor(out=eM, in0=eA, in1=BMt[0:tw, :, 0:tw], op=ALU.mult), reads=[beA, b_BM], writes=[beM])
            DVE(lambda h: h.scalar_tensor_tensor(out=NM[0:tw, :, 1, 0:tw], in0=eM, scalar=-1.0, in1=KKv, op0=ALU.mult, op1=ALU.mult),
                reads=[beM] + pbK, writes=[b_NM])
            pbv = psb[:].rearrange("p (a c t) -> p a c t", a=2, c=4)
            for hh in range(4):
                tr(pbv[0:tw, 0, hh, :], QKN[:, 4 + hh, 0:tw], 128, reads=[b_QKN], writes=[b_psb], sig=False, dt=BF16)
                tr(pbv[0:tw, 1, hh, :], vTb[:, hh, 0:tw], 128, reads=[b_vT], writes=[b_psb], sig=(hh == 3), dt=BF16)

            def bcol(ap_cols):
                return ap_cols.unsqueeze(2).to_broadcast([tw, 4, 128])
            DVE(lambda h: h.tensor_tensor(out=kbg[0:tw], in0=pbv[0:tw, 0], in1=bcol(TM2[0:tw, 8:12]), op=ALU.mult),
                reads=[b_psb, b_TM2], writes=[b_kbg])
            DVE(lambda h: h.tensor_tensor(out=kg[0:tw], in0=pbv[0:tw, 0], in1=bcol(TM2[0:tw, 12:16]), op=ALU.mult),
                reads=[b_psb, b_TM2], writes=[b_kg])
            DVE(lambda h: h.tensor_tensor(out=vb[0:tw], in0=pbv[0:tw, 1], in1=bcol(TM[0:tw, 4:8]), op=ALU.mult),
                reads=[b_psb, b_TM], writes=[b_vb])
            RG = "B"
            yield "F_DONE"
            d8 = bc4(cmask[0:tw, 0, 0:tw], 4, tw, tw)
            DVE(lambda h: h.tensor_tensor(out=PQ[0:tw, :, 0, 0:tw], in0=NM[0:tw, :, 1, 0:tw], in1=d8, op=ALU.mult),
                reads=[b_NM, b_const], writes=[b_PQ])
            DVE(lambda h: h.tensor_tensor(out=PTQ[0:tw, :, 0, 0:tw], in0=NM[0:tw, :, 0, 0:tw], in1=d8, op=ALU.mult),
                reads=[b_NM, b_const], writes=[b_PTQ])
            idb = bc4(ident_f[0:tw, 0:tw], 4, tw, tw)
            DVE(lambda h: h.tensor_tensor(out=PQ[0:tw, :, 1, 0:tw], in0=PQ[0:tw, :, 0, 0:tw], in1=idb, op=ALU.add),
                reads=[b_PQ, b_const], writes=[b_PQ])
            DVE(lambda h: h.tensor_tensor(out=PTQ[0:tw, :, 1, 0:tw], in0=PTQ[0:tw, :, 0, 0:tw], in1=idb, op=ALU.add),
                reads=[b_PTQ, b_const], writes=[b_PTQ])
            for lev in (1, 2, 3):
                yield 2
                last = lev == 3
                bkA, pbA = PS(2, RG); bkB, pbB = PS(2, RG)
                Av = psf[0:tw, bkA:bkA + 2, :].rearrange("p b (h x) -> p (b h) x", h=2)
                Bv = psf[0:tw, bkB:bkB + 2, :].rearrange("p b (h x) -> p (b h) x", h=2)
                rd = [b_PQ, b_PTQ]
                for hh in range(4):
                    Ph = PQ[0:tw, hh, 0, 0:tw]; Qh = PQ[0:tw, hh, 1, 0:tw]
                    PTh = PTQ[0:tw, hh, 0, 0:tw]; Qnh = PTQ[0:tw, hh, 1, 0:tw]
                    if not last:
                        mm(Av[:, hh, 0:tw], PTh, Ph, True, True, reads=rd, writes=pbA, sig=False)
                        mm(Bv[:, hh, 0:tw], Ph, PTh, True, True, reads=rd, writes=pbA + pbB, sig=(hh == 3 and lev == 1))
                    if lev > 1:
                        mm(Av[:, hh, 128:128 + tw], PTh, Qh, True, True, reads=rd, writes=pbA, sig=False)
                        mm(Bv[:, hh, 128:128 + tw], Ph, Qnh, True, True, reads=rd, writes=pbA + pbB, sig=(hh == 3))
                if lev > 1:
                    DVE(lambda h: h.tensor_tensor(out=PQ[0:tw, :, 1, 0:tw], in0=PQ[0:tw, :, 1, 0:tw], in1=Av[:, :, 128:128 + tw], op=ALU.add),
                        reads=pbA + [b_PQ], writes=[b_PQ])
                    DVE(lambda h: h.tensor_tensor(out=PTQ[0:tw, :, 1, 0:tw], in0=PTQ[0:tw, :, 1, 0:tw], in1=Bv[:, :, 128:128 + tw], op=ALU.add),
                        reads=pbB + [b_PTQ], writes=[b_PTQ])
                if not last:
                    ACT(lambda h: h.copy(out=PQ[0:tw, :, 0, 0:tw], in_=Av[:, :, 0:tw]), reads=pbA, writes=[b_PQ])
                    ACT(lambda h: h.copy(out=PTQ[0:tw, :, 0, 0:tw], in_=Bv[:, :, 0:tw]), reads=pbB, writes=[b_PTQ])
            li = 0
            for bsz in (8, 16, 32, 64):
                li += 1
                if bsz >= tw:
                    break
                cb = bc4(cmask[0:tw, li, 0:tw], 4, tw, tw)
                yield 2
                bk1, pb1 = PS(1, RG); bk2_, pb2_ = PS(1, RG)
                Y1 = psf[0:tw, bk1, :].rearrange("p (c t) -> p c t", c=4)[:, :, 0:tw]
                Y2 = psf[0:tw, bk2_, :].rearrange("p (c t) -> p c t", c=4)[:, :, 0:tw]
                for hh in range(4):
                    mm(Y1[:, hh, :], NM[0:tw, hh, 0, 0:tw], PQ[0:tw, hh, 1, 0:tw], True, True, reads=[b_NM, b_PQ], writes=pb1, sig=False)
                    mm(Y2[:, hh, :], NM[0:tw, hh, 1, 0:tw], PTQ[0:tw, hh, 1, 0:tw], True, True, reads=[b_NM, b_PTQ], writes=pb2_, sig=(hh == 3))
                DVE(lambda h: h.tensor_tensor(out=PQ[0:tw, :, 0, 0:tw], in0=Y1, in1=cb, op=ALU.mult), reads=pb1 + [b_const], writes=[b_PQ])
                DVE(lambda h: h.tensor_tensor(out=PTQ[0:tw, :, 0, 0:tw], in0=Y2, in1=cb, op=ALU.mult), reads=pb2_ + [b_const], writes=[b_PTQ])
                yield 2
                bk3, pb3 = PS(1, RG); bk4, pb4 = PS(1, RG)
                Z1 = psf[0:tw, bk3, :].rearrange("p (c t) -> p c t", c=4)[:, :, 0:tw]
                Z2 = psf[0:tw, bk4, :].rearrange("p (c t) -> p c t", c=4)[:, :, 0:tw]
                for hh in range(4):
                    mm(Z1[:, hh, :], PTQ[0:tw, hh, 1, 0:tw], PQ[0:tw, hh, 0, 0:tw], True, True, reads=[b_PQ, b_PTQ], writes=pb3, sig=False)
                    mm(Z2[:, hh, :], PQ[0:tw, hh, 1, 0:tw], PTQ[0:tw, hh, 0, 0:tw], True, True, reads=[b_PQ, b_PTQ], writes=pb4, sig=(hh == 3))
                DVE(lambda h: h.tensor_tensor(out=PQ[0:tw, :, 1, 0:tw], in0=PQ[0:tw, :, 1, 0:tw], in1=Z1, op=ALU.add),
                    reads=pb3 + pb4 + [b_PQ], writes=[b_PQ])
                DVE(lambda h: h.tensor_tensor(out=PTQ[0:tw, :, 1, 0:tw], in0=PTQ[0:tw, :, 1, 0:tw], in1=Z2, op=ALU.add),
                    reads=pb3 + pb4 + [b_PTQ], writes=[b_PTQ])
            yield 2
            bkW, pbW = PS(1, RG)
            for hh in range(4):
                mm(psf[:, bkW, hh * 128:hh * 128 + tw], kbg[0:tw, hh, :], PQ[0:tw, hh, 1, 0:tw], True, True,
                   reads=[b_PQ, b_kbg], writes=pbW, sig=(hh == 3))
            ACT(lambda h: h.mul(out=wTb[:, :, 0:tw], in_=pv(bkW), mul=-1.0), reads=pbW, writes=[b_wT])
            if smp:
                cx.dma("sp", S[:], st_delta[l].rearrange("h k v -> k h v"), writes=[b_S])
                ACT(lambda h: h.copy(out=S_bf[:], in_=S[:]), reads=[b_S], writes=[b_Sbf])
            elif bi == 0:
                DVE(lambda h: h.memset(S[:], 0.0), writes=[b_S])
                DVE(lambda h: h.memset(S_bf[:], 0.0), writes=[b_Sbf])
            yield 2
            bk, pb = PS(1, RG)
            for hh in range(4):
                mm(psf[0:tw, bk, hh * 128:(hh + 1) * 128], PQ[0:tw, hh, 1, 0:tw], vb[0:tw, hh, :], True, False,
                   reads=[b_PQ, b_vb], writes=pb, sig=False)
                mm(psf[0:tw, bk, hh * 128:(hh + 1) * 128], wTb[:, hh, 0:tw], S_bf[:, hh, :], False, True, reads=[b_wT, b_Sbf], writes=pb,
                   sig=(hh == 3))
            DVE(lambda h: h.tensor_copy(out=vnew[0:tw], in_=psf[0:tw, bk, :].rearrange("p (c t) -> p c t", c=4)), reads=pb, writes=[b_vn])
            yield 2
            bkO, pbO = PS(1, RG)
            for hh in range(4):
                mm(psf[:, bkO, hh * 128:hh * 128 + tw], S_bf[:, hh, :], qgT[:, hh, 0:tw], True, False, reads=[b_Sbf, b_qg], writes=pbO, sig=False)
                mm(psf[:, bkO, hh * 128:hh * 128 + tw], vnew[0:tw, hh, :], attnT[0:tw, hh, 0:tw], False, True, reads=[b_vn, b_attn],
                   writes=pbO, sig=(hh == 3))
            bk, pb = PS(1, RG)
            for hh in range(4):
                mm(psf[:, bk, hh * 128:(hh + 1) * 128], kg[0:tw, hh, :], vnew[0:tw, hh, :], True, True, reads=[b_kg, b_vn], writes=pb,
                   sig=(hh == 3))
            for hh in range(4):
                DVE(lambda h: h.scalar_tensor_tensor(out=S[:, hh, :], in0=S[:, hh, :], scalar=TM2[:, 16 + hh:17 + hh],
                                                     in1=psf[:, bk, hh * 128:(hh + 1) * 128], op0=ALU.mult, op1=ALU.add),
                    reads=pb + [b_S, b_TM2], writes=[b_S])
            ACT(lambda h: h.copy(out=S_bf[:], in_=S[:]), reads=[b_S], writes=[b_Sbf])
            if smp or bi == NB - 1:
                cx.dma("sp", (o_sdelta if smp else o_pdelta)[l].rearrange("h k v -> k h v"), S[:], reads=[b_S])
            yield 2
            osq, bosq = TMP(RG)
            osqv = osq.bitcast(BF16)[:, 0:512].rearrange("p (c t) -> p c t", c=4)
            ACT(lambda h: h.activation(out=osqv[:, :, 0:tw], in_=pv(bkO), func=AF.Square), reads=pbO, writes=[bosq])
            bk, pb = PS(1, RG)
            for c in range(4):
                mm(psf[:, bk, c * 128:c * 128 + tw], ones_b[:], osqv[:, c, 0:tw], True, True, reads=[bosq, b_const], writes=pb, sig=(c == 3))
            ro, bro = TMP(RG)
            rov = ro.rearrange("p (c t) -> p c t", c=4)[:, :, 0:tw]
            rstd_from_ps(pv(bk), tw, 1.0 / 128, 0.0, rov, pb, [bro])
            t1, bt1 = TMP(RG)
            t1v = t1.rearrange("p (c t) -> p c t", c=4)[:, :, 0:tw]
            DVE(lambda h: h.scalar_tensor_tensor(out=t1v, in0=pv(bkO), scalar=pcols[:, 112:113], in1=rov, op0=ALU.mult, op1=ALU.mult),
                reads=pbO + [bro, b_pcols], writes=[bt1])
            DVE(lambda h: h.tensor_tensor(out=ABT[:, 4:8, 0:tw], in0=t1v, in1=zs[:, :, 0:tw], op=ALU.mult), reads=[bt1, b_zs], writes=[b_AB])
            if l == 0 and bi == 0:
                dump("ABT", ABT, [b_AB])
            yield 2
            bk2, pb2 = PS(2, RG)
            ov = psf[:, bk2:bk2 + 2, :].rearrange("p b (c t) -> p (b c) t", c=4)
            for m in range(8):
                for k in range(8):
                    mm(ov[:, m, 0:tw], wout[:, k, m * 128:(m + 1) * 128], ABT[:, k, 0:tw], k == 0, k == 7, reads=L_wout + [b_AB], writes=pb2,
                       sig=(k == 7 and m == 7))
            for m in range(8):
                DVE(lambda h: h.scalar_tensor_tensor(out=xT[:, m, col0:col0 + tw], in0=ov[:, m, 0:tw], scalar=lay[:, 2, s, m:m + 1],
                                                     in1=xT[:, m, col0:col0 + tw], op0=ALU.mult, op1=ALU.add),
                    reads=pb2 + [b_lay] + xb, writes=xb)

        tiles = [(t * 512, 512, 0, list(range(t * 4, t * 4 + 4))) for t in range(L // 512)]
        if L % 512:
            t0_ = (L // 512) * 512
            tiles.append((t0_, L - t0_, 0, list(range(t0_ // 128, NB))))
        tiles.append((L, LS, 1, [NB]))

        def load_eighth(l, e):
            sl = e % 4
            o = slot_off[sl]
            up = WA[:, o:o + 4096].rearrange("p (k n) -> p k n", k=8)
            dn = WA[:, o + 4096:o + 8192].rearrange("p (k n) -> p k n", k=4)
            cx.dma("pool", up, w_up[l].rearrange("(k p) n -> p k n", p=128)[:, :, e * 512:(e + 1) * 512], writes=slot_buf[sl])
            cx.dma("pool", dn, w_down[l][e * 512:(e + 1) * 512, :].rearrange("(k p) n -> p k n", p=128), writes=slot_buf[sl])
            return up, dn, slot_buf[sl]

        def phaseB(l, pend):
            nxt = l + 1 if l + 1 < DEPTH else None
            ada_next = list(range(24)) if nxt is not None else []
            hcnt = 0
            extra = [(hidf[0], b_hidf[0]), (hidf[1], b_hidf[1])]
            for i2 in range(2):
                hv = hidb[i2].rearrange("p a b -> p (a b)").bitcast(F32)
                extra.append((hv[:, 0:512], b_hidb[i2]))
            tstate["bpool"] = [(tmpFb[i], b_tmpb[i]) for i in range(4)] + extra
            cx.rec_begin()
            for (c0, tw, s, blks) in tiles:
                for sub in range(0, tw, 128):
                    w_ = min(128, tw - sub)
                    bi_ = (c0 + sub) // 128
                    norm_block(c0 + sub, w_, s, 3, hTall[:, :, c0 + sub:c0 + sub + w_], b_hall_l[bi_], [b_x[bi_]])
            cx.rec_flush()
            tstate["bpool"] = None
            cx.rec_begin()
            if nxt is not None:
                layer_params(nxt)
            for e in range(8):
                up, dn, wb = pend.pop(e)
                for (c0, tw, s, blks) in tiles:
                    xb = [b_x[i] for i in blks]
                    hb = hcnt % 2
                    hcnt += 1
                    for j in range(4):
                        bk, pb = PS(1)
                        for k in range(8):
                            mm(psf[:, bk, 0:tw], up[:, k, j * 128:(j + 1) * 128], hTall[:, k, c0:c0 + tw], k == 0, k == 7,
                               reads=wb + [b_hall_l[i] for i in blks], writes=pb, sig=(k == 7))
                        hf = hidf[j % 2]; bhf = b_hidf[j % 2]
                        ACT(lambda h: h.activation(out=hf[:, 0:tw], in_=psf[:, bk, 0:tw], func=AF.Relu), reads=pb, writes=[bhf])
                        POOL(lambda h: h.tensor_tensor(out=hidb[hb][:, j, 0:tw], in0=hf[:, 0:tw], in1=hf[:, 0:tw], op=ALU.mult),
                             reads=[bhf], writes=[b_hidb[hb]])
                    for m in range(8):
                        bk, pb = PS(1)
                        for k in range(4):
                            mm(psf[:, bk, 0:tw], dn[:, k, m * 128:(m + 1) * 128], hidb[hb][:, k, 0:tw], k == 0, k == 3,
                               reads=wb + [b_hidb[hb]], writes=pb, sig=(k == 3))
                        DVE(lambda h: h.scalar_tensor_tensor(out=xT[:, m, c0:c0 + tw], in0=psf[:, bk, 0:tw], scalar=lay[:, 5, s, m:m + 1],
                                                             in1=xT[:, m, c0:c0 + tw], op0=ALU.mult, op1=ALU.add),
                            reads=pb + [b_lay] + xb, writes=xb)
                    if ada_next and e >= 1:
                        ada_chunk(nxt, ada_next.pop(0), adaStB, b_adaStB)
                if e + 4 < 8:
                    pend[e + 4] = load_eighth(l, e + 4)
                elif nxt is not None:
                    if e == 4:
                        load_wout(nxt)
                    elif e == 5:
                        load_win_chunks(nxt, [0, 1])
                    elif e == 6:
                        load_win_chunks(nxt, [2, 3, 4])
                    elif e == 7:
                        load_win_chunks(nxt, [5, 6, 7])
            while ada_next:
                ada_chunk(nxt, ada_next.pop(0), adaStB, b_adaStB)
            cx.rec_flush()

        xstage2 = R[:, tmp_o + 4 * 512:tmp_o + 6 * 512]
        b_xst2 = [b_tmp[4], b_tmp[5]]
        b_smlF = [B("smlF0"), B("smlF1")]

        def final_out(bi):
            tw = 128 if bi < NB else LS
            col0 = bi * 128
            par = bi % 2
            xs, bxs = (xstage, b_xst) if par == 0 else (xstage2, b_xst2)
            sc0 = par * 4
            bsm = b_smlF[par]
            bk2, pb2 = PS(2)
            tv = psf[0:tw, bk2:bk2 + 2, :].rearrange("p b x -> p (b x)")
            for c in range(8):
                tr(tv[:, c * 128:(c + 1) * 128], xT[:, c, col0:col0 + tw], 128, reads=[b_x[bi]], writes=pb2, sig=(c == 7))
            for hf in range(2):
                ACT(lambda h: h.activation(out=tmpF[2 + hf][0:tw, :], in_=tv[:, hf * 512:(hf + 1) * 512], func=AF.Square,
                                           accum_out=sml[0:tw, sc0 + hf:sc0 + hf + 1]), reads=pb2, writes=[b_tmp[2 + hf], bsm])
            DVE(lambda h: h.tensor_tensor(out=sml[0:tw, sc0 + 2:sc0 + 3], in0=sml[0:tw, sc0:sc0 + 1], in1=sml[0:tw, sc0 + 1:sc0 + 2], op=ALU.add),
                reads=[bsm], writes=[bsm])
            rstd_from_ps(sml[0:tw, sc0 + 2:sc0 + 3], 1, 1.0 / D, 0.0, sml[0:tw, sc0 + 3:sc0 + 4], [bsm], [bsm])
            DVE(lambda h: h.scalar_tensor_tensor(out=xs[0:tw, :], in0=tv, scalar=sml[0:tw, sc0 + 3:sc0 + 4], in1=FING[0:tw, :], op0=ALU.mult, op1=ALU.mult),
                reads=pb2 + [bsm, b_fing], writes=bxs)
            dst = y_p[col0:col0 + tw, :] if bi < NB else y_s
            cx.dma("sp", dst, xs[0:tw, :], reads=bxs)

        for l in range(DEPTH):
            if l == 0:
                load_wA(l)
                cx.rec_begin()
                for bi in range(NB + 1):
                    load_x(bi)
                layer_params(l)
                ada(l)
                cx.rec_flush()
            else:
                cx.barrier()
            layer_cols(l)
            cx.barrier()
            cx.rec_begin()
            gens = [phaseA_block(l, bi) for bi in range(NB + 1)]

            def step(g):
                try:
                    return next(g)
                except StopIteration:
                    return "END"
            while step(gens[0]) != "F_DONE":
                pass
            for bi in range(NB + 1):
                cur = gens[bi]
                nxt = gens[bi + 1] if bi + 1 <= NB else None
                cur_done = False
                nxt_done = nxt is None
                while not (cur_done and nxt_done):
                    if not cur_done:
                        if step(cur) == "END":
                            cur_done = True
                    for _ in range(FSTEPS):
                        if not nxt_done:
                            if step(nxt) == "F_DONE":
                                nxt_done = True
            pend = {e: load_eighth(l, e) for e in range(4)}
            cx.rec_flush()
            cx.barrier()
            tstate["mode"] = "b"
            phaseB(l, pend)
            tstate["mode"] = "a"
        cx.barrier()
        cx.dma("sp", FING, fin_g.partition_broadcast(128), writes=[b_fing])
        cx.rec_begin()
        for bi in range(NB + 1):
            final_out(bi)
        cx.rec_flush()
        cx.finish()
    return nc


def _consts():
    ident = np.eye(128, dtype=np.float32)
    p = np.arange(128)[:, None]
    f = np.arange(128)[None, :]
    masks = np.zeros((128, 3, 128), np.float32)
    masks[:, 0, :] = np.where(f >= p, 0.0, -BIG)
    masks[:, 1, :] = np.where(f > p, 0.0, -BIG)
    masks[:, 2, :] = np.where(f < p, 0.0, BIG)
    sgum = (p // 64 <= f // 64).astype(np.float32)
    sel = np.zeros((8, 8, 128), np.float32)
    for r in range(8):
        sel[r, r, :] = 1.0
    cm = np.zeros((128, 6, 128), np.float32)
    cm[:, 5, :] = (f > p)
    cm[:, 0, :] = (p // 8 == f // 8)
    for li, bsz in enumerate((8, 16, 32, 64)):
        cm[:, li + 1, :] = (p // (2 * bsz) == f // (2 * bsz)) & (p // bsz != f // bsz)
    return ident, masks, sgum, sel, cm.astype(ml_dtypes.bfloat16)


_NC_CACHE = {}


def make_in_maps(inputs, L):
    f = lambda a: np.ascontiguousarray(np.asarray(a, dtype=np.float32))
    ident, masks, sgum, sel, cm = _consts()
    x_prompt = f(inputs["x_prompt"]); x_sample = f(inputs["x_sample"])
    nb = x_prompt.shape[0]
    rowsA = np.concatenate([
        f(inputs["norm_mix_g"]).reshape(DEPTH, 8, 128), f(inputs["norm_ffn_g"]).reshape(DEPTH, 8, 128),
        f(inputs["ada_b"]).reshape(DEPTH, 48, 128), f(inputs["conv_w"]).reshape(DEPTH, 48, 128),
        f(inputs["dn_norm_g"]).reshape(DEPTH, 1, 128)], axis=1)
    shared = dict(
        ada_w=f(inputs["ada_w"]), rowsA=np.ascontiguousarray(rowsA), w_in=f(inputs["w_in"]),
        sgu_norm_g=f(inputs["sgu_norm_g"]), sgu_w=f(inputs["sgu_w"]), sgu_b=f(inputs["sgu_b"]).reshape(DEPTH, 512),
        dt_bias=f(inputs["dt_bias"]), a_log=f(inputs["a_log"]), w_out=f(inputs["w_out"]), w_up=f(inputs["w_up"]),
        w_down=f(inputs["w_down"]), fin_g=f(inputs["final_norm_g"]).reshape(1, D),
        k_ident=ident, k_masks=masks, k_sgum=sgum, k_cm=cm)
    in_maps = []
    for i in range(nb):
        m = dict(shared)
        m["x_p"] = np.ascontiguousarray(x_prompt[i, :L]); m["x_s"] = x_sample[i]
        m["c_ps"] = np.ascontiguousarray(np.concatenate([f(inputs["c_prompt"])[i].reshape(8, 128), f(inputs["c_sample"])[i].reshape(8, 128)], 0))
        m["st_conv"] = np.ascontiguousarray(f(inputs["state_conv"])[:, i].reshape(DEPTH, 36, 128))
        m["st_delta"] = np.ascontiguousarray(f(inputs["state_delta"])[:, i])
        in_maps.append(m)
    return in_maps


def run(inputs, L=2048, dbg=None, core_ids=None):
    key = (L, tuple(sorted((dbg or {}).items())))
    if key not in _NC_CACHE:
        _NC_CACHE[key] = build(L, dbg)
    nc = _NC_CACHE[key]
    in_maps = make_in_maps(inputs, L)
    if core_ids is not None:
        in_maps = [in_maps[i] for i in core_ids]
    res = run_bass_kernel_spmd(nc, in_maps, core_ids=list(range(len(in_maps))))
    return res.results


def kernel(**inputs):
    r = run(inputs, 2048)
    st = lambda k: np.stack([np.asarray(x[k], dtype=np.float32) for x in r], axis=0)
    y_prompt = st("y_p"); y_sample = st("y_s")
    pconv = np.ascontiguousarray(st("o_pconv").transpose(1, 0, 2, 3))
    pdelta = np.ascontiguousarray(st("o_pdelta").transpose(1, 0, 2, 3, 4))
    sconv = np.ascontiguousarray(st("o_sconv").transpose(1, 0, 2, 3))
    sdelta = np.ascontiguousarray(st("o_sdelta").transpose(1, 0, 2, 3, 4))
    sgu = np.ascontiguousarray(st("o_sgu").transpose(1, 0, 2, 3))
    return (y_prompt, y_sample, pconv, pdelta, sconv, sdelta, sgu)
```

```python
import os
import numpy as np
import ml_dtypes
from contextlib import ExitStack
import concourse.bass as bass
import concourse.mybir as mybir
from concourse.bass_utils import run_bass_kernel_spmd

F32, BF16 = mybir.dt.float32, mybir.dt.bfloat16
AF = mybir.ActivationFunctionType
ALU = mybir.AluOpType

D = 1024
LS = 32
DEPTH = 2
INW = 3080
DFF = 4096
EPS = 1e-6
BIG = 1.0e30
KDMA = 6
SAME_ENGINE_SYNC = True
FSTEPS = 1
NO_REORDER = False


class Buf:
    __slots__ = ("name", "w", "r", "rw", "rr")

    def __init__(self, name):
        self.name = name
        self.w = {}
        self.r = {}
        self.rw = []
        self.rr = []


class _Proxy:
    def __init__(self):
        self.call = None

    def __getattr__(self, name):
        def f(*a, **kw):
            self.call = (name, a, kw)
            return self
        return f


def _free_size(call):
    name, a, kw = call
    ap = kw.get("out", a[0] if a else None)
    try:
        shp = ap.shape
        n = 1
        for d in shp[1:]:
            n *= int(d)
        return max(n, 1)
    except Exception:
        return 128


def _cost(E, call):
    n = _free_size(call)
    if E == "pe":
        return (50.0 + 0.4 * n) * float(os.environ.get("SC_PE", 1.0))
    if E == "dve":
        return (n + 151) / 0.96 * float(os.environ.get("SC_DVE", 1.0))
    if E == "act":
        return (130.0 + 0.75 * n) * float(os.environ.get("SC_ACT", 1.0))
    if E == "pool":
        return 290.0 + 1.23 * n
    return 100.0


class Ctx:
    def __init__(self, nc, es):
        self.nc = nc
        self.es = es
        self.sems = {}
        self.eng = {}
        for name, h in (("pe", nc.tensor), ("act", nc.scalar), ("dve", nc.vector),
                        ("pool", nc.gpsimd), ("sp", nc.sync)):
            self.sems["e_" + name] = es.enter_context(nc.semaphore("se_" + name))
            self.eng[name] = dict(h=h, cnt=0, seen={})
        self.dq = {}
        for q in ("sp", "pool"):
            for i in range(KDMA):
                self.sems[f"d_{q}{i}"] = es.enter_context(nc.semaphore(f"sd_{q}{i}"))
            self.dq[q] = dict(cnts=[0] * KDMA, n=0)
        self.nbuf = 0
        self.recording = False
        self.rec = []

    def rec_begin(self):
        self.recording = True
        self.rec = []

    def rec_flush(self):
        rec = self.rec
        self.recording = False
        n = len(rec)
        if n == 0:
            return
        unit_of = [0] * n
        units = []
        cur = None
        for i, nd in enumerate(rec):
            if nd["E"] == "pe" and nd["kind"] == "op":
                if cur is None:
                    cur = len(units)
                    units.append(dict(E="pe", ops=[], dur=0.0, dma=False))
                units[cur]["ops"].append(i)
                units[cur]["dur"] += nd["dur"]
                unit_of[i] = cur
                if nd["sig"]:
                    cur = None
            else:
                u = len(units)
                units.append(dict(E=nd["E"], ops=[i], dur=nd["dur"], dma=(nd["kind"] == "dma")))
                unit_of[i] = u
        assert cur is None, "segment ends inside an unsignaled PE group"
        nu = len(units)
        udeps = [set() for _ in range(nu)]
        for i, nd in enumerate(rec):
            u = unit_of[i]
            for d in nd["deps"]:
                du = unit_of[d]
                if du != u:
                    udeps[u].add(du)
        succ = [[] for _ in range(nu)]
        indeg = [0] * nu
        for u in range(nu):
            indeg[u] = len(udeps[u])
            for d in udeps[u]:
                succ[d].append(u)
        prio = [0.0] * nu
        for u in range(nu - 1, -1, -1):
            m = 0.0
            for v in succ[u]:
                if prio[v] > m:
                    m = prio[v]
            prio[u] = units[u]["dur"] + m
        LAT, SLAT, WIN = 350.0, 150.0, 250.0
        etime = {k: 0.0 for k in self.eng}
        fin = [0.0] * nu
        ready = [u for u in range(nu) if indeg[u] == 0]
        order = []
        while ready:
            best = None
            ests = []
            mn = None
            for u in ready:
                E = units[u]["E"]
                st = etime[E]
                for d in udeps[u]:
                    t = fin[d] + (SLAT if units[d]["E"] == E else LAT)
                    if t > st:
                        st = t
                ests.append((u, st))
                if mn is None or st < mn:
                    mn = st
            for u, st in ests:
                if st <= mn + WIN:
                    key = (prio[u], -u)
                    if best is None or key > best[0]:
                        best = (key, u, st)
            _, u, st = best
            E = units[u]["E"]
            if units[u]["dma"]:
                etime[E] = st + units[u]["dur"]
                fin[u] = st + 2000.0
            else:
                etime[E] = st + units[u]["dur"]
                fin[u] = etime[E]
            order.append(u)
            ready.remove(u)
            for v in succ[u]:
                indeg[v] -= 1
                if indeg[v] == 0:
                    ready.append(v)
        assert len(order) == nu
        if NO_REORDER:
            order = list(range(nu))
        self.sim_time = max(fin) if fin else 0.0
        ticket = [None] * n
        for u in order:
            for i in units[u]["ops"]:
                nd = rec[i]
                E = nd["E"]
                e = self.eng[E]
                deps = self._deps(nd["reads"], nd["writes"])
                for d in nd["deps"]:
                    deps.append(ticket[d])
                if nd["kind"] == "dma":
                    q = self.dq[E]
                    idx = q["n"] % KDMA
                    q["n"] += 1
                    key = f"d_{E}{idx}"
                    if q["cnts"][idx] > 0:
                        deps.append((key, q["cnts"][idx]))
                    self._wait(E, deps)
                    out, in_ = nd["call"]
                    ins = e["h"].dma_start(out=out, in_=in_)
                    q["cnts"][idx] += 16
                    ins.then_inc(self.sems[key], 16)
                    ticket[i] = (key, q["cnts"][idx])
                else:
                    if E == "pe" or not SAME_ENGINE_SYNC:
                        deps = [d for d in deps if d[0] != "e_" + E]
                    self._wait(E, deps)
                    name, a, kw = nd["call"]
                    ins = getattr(e["h"], name)(*a, **kw)
                    ticket[i] = ("e_" + E, e["cnt"] + 1)
                    if nd["sig"]:
                        ins.then_inc(self.sems["e_" + E], 1)
                        e["cnt"] += 1
        for i, nd in enumerate(rec):
            self._reg(ticket[i], nd["reads"], nd["writes"])
            for b in nd["reads"]:
                b.rw = []
                b.rr = []
            for b in nd["writes"]:
                b.rw = []
                b.rr = []
        self.rec = []

    def buf(self, name=None):
        self.nbuf += 1
        return Buf(name or f"b{self.nbuf}")

    def sb(self, name, shape, dt):
        t = self.es.enter_context(self.nc.sbuf_tensor(name, list(shape), dt))
        return t

    def _wait(self, E, deps):
        e = self.eng[E]
        need = {}
        for k, v in deps:
            if v > need.get(k, 0):
                need[k] = v
        for k, v in need.items():
            if e["seen"].get(k, 0) >= v:
                continue
            e["h"].wait_ge(self.sems[k], v)
            e["seen"][k] = v

    @staticmethod
    def _deps(reads, writes):
        deps = []
        for b in reads:
            deps.extend(b.w.items())
        for b in writes:
            deps.extend(b.w.items())
            deps.extend(b.r.items())
        return deps

    @staticmethod
    def _reg(tk, reads, writes):
        k, v = tk
        for b in reads:
            if b.r.get(k, 0) < v:
                b.r[k] = v
        for b in writes:
            if b.w.get(k, 0) < v:
                b.w[k] = v

    def _rec_add(self, node, reads, writes):
        i = len(self.rec)
        deps = set()
        for b in reads:
            deps.update(b.rw)
        nw = os.environ.get("SIM_NOWAR", "")
        for b in writes:
            deps.update(b.rw)
            if not (nw == "1" or (nw and any(b.name.startswith(x) for x in nw.split(",") if x))):
                deps.update(b.rr)
        node["deps"] = deps
        node["reads"] = list(reads)
        node["writes"] = list(writes)
        self.rec.append(node)
        for b in reads:
            b.rr.append(i)
        for b in writes:
            b.rw = [i]
            b.rr = []

    def op(self, E, fn, reads=(), writes=(), sig=True):
        if self.recording:
            px = _Proxy()
            fn(px)
            self._rec_add(dict(E=E, kind="op", call=px.call, sig=sig, dur=_cost(E, px.call)), reads, writes)
            return None
        e = self.eng[E]
        deps = self._deps(reads, writes)
        if E == "pe" or not SAME_ENGINE_SYNC:
            deps = [d for d in deps if d[0] != "e_" + E]
        self._wait(E, deps)
        ins = fn(e["h"])
        tk = ("e_" + E, e["cnt"] + 1)
        if sig:
            ins.then_inc(self.sems["e_" + E], 1)
            e["cnt"] += 1
        self._reg(tk, reads, writes)
        return ins

    def dma(self, Q, out, in_, reads=(), writes=()):
        if self.recording:
            self._rec_add(dict(E=Q, kind="dma", call=(out, in_), sig=True, dur=100.0), reads, writes)
            return
        q = self.dq[Q]
        e = self.eng[Q]
        idx = q["n"] % KDMA
        q["n"] += 1
        key = f"d_{Q}{idx}"
        deps = self._deps(reads, writes)
        if q["cnts"][idx] > 0:
            deps.append((key, q["cnts"][idx]))
        self._wait(Q, deps)
        ins = e["h"].dma_start(out=out, in_=in_)
        q["cnts"][idx] += 16
        ins.then_inc(self.sems[key], 16)
        self._reg((key, q["cnts"][idx]), reads, writes)

    def barrier(self):
        deps = []
        for i in range(KDMA):
            c = self.dq["sp"]["cnts"][i]
            if c:
                deps.append((f"d_sp{i}", c))
        for name in ("pe", "act", "dve", "pool", "sp"):
            c = self.eng[name]["cnt"]
            if c:
                deps.append(("e_" + name, c))
        for name in ("pe", "act", "dve", "pool", "sp"):
            self._wait(name, [d for d in deps if d[0] != "e_" + name])

    def finish(self):
        deps = []
        for q in ("sp", "pool"):
            for i in range(KDMA):
                c = self.dq[q]["cnts"][i]
                if c:
                    deps.append((f"d_{q}{i}", c))
        for name in ("pe", "act", "dve", "pool"):
            c = self.eng[name]["cnt"]
            if c:
                deps.append(("e_" + name, c))
        self._wait("sp", deps)


def build(L=2048, dbg=None):
    NTOK = L + LS
    NB = L // 128
    nc = bass.Bass("TRN2", target_bir_lowering=False)
    dbg = dbg or {}

    def din(name, shape, dt=F32):
        return nc.dram_tensor(name, list(shape), dt, kind="ExternalInput").ap()

    def dout(name, shape, dt=F32):
        return nc.dram_tensor(name, list(shape), dt, kind="ExternalOutput").ap()

    x_p = din("x_p", [L, D]); x_s = din("x_s", [LS, D])
    c_ps = din("c_ps", [16, 128])
    st_conv = din("st_conv", [DEPTH, 36, 128])
    st_delta = din("st_delta", [DEPTH, 4, 128, 128])
    ada_w = din("ada_w", [DEPTH, D, 6 * D])
    rowsA = din("rowsA", [DEPTH, 113, 128])
    w_in = din("w_in", [DEPTH, D, INW])
    sgu_norm_g = din("sgu_norm_g", [DEPTH, 512])
    sgu_w = din("sgu_w", [DEPTH, 4, 128, 128])
    sgu_b = din("sgu_b", [DEPTH, 512])
    dt_bias = din("dt_bias", [DEPTH, 4]); a_log = din("a_log", [DEPTH, 4])
    w_out = din("w_out", [DEPTH, D, D]); w_up = din("w_up", [DEPTH, D, DFF]); w_down = din("w_down", [DEPTH, DFF, D])
    fin_g = din("fin_g", [1, D])
    k_ident = din("k_ident", [128, 128]); k_masks = din("k_masks", [128, 3, 128])
    k_sgum = din("k_sgum", [128, 128])
    k_cm = din("k_cm", [128, 6, 128], BF16)

    y_p = dout("y_p", [L, D]); y_s = dout("y_s", [LS, D])
    o_pconv = dout("o_pconv", [DEPTH, 3, 1536]); o_pdelta = dout("o_pdelta", [DEPTH, 4, 128, 128])
    o_sconv = dout("o_sconv", [DEPTH, 3, 1536]); o_sdelta = dout("o_sdelta", [DEPTH, 4, 128, 128])
    o_sgu = dout("o_sgu", [DEPTH, LS, 512])
    dbg_outs = {k: dout("dbg_" + k, shp) for k, shp in dbg.items()}

    es = ExitStack()
    with es:
        cx = Ctx(nc, es)
        sb = cx.sb
        B = cx.buf

        psf = es.enter_context(nc.psum_tensor("psf", [128, 7, 512], F32))
        psb = es.enter_context(nc.psum_tensor("psb", [128, 1024], BF16))
        pbank = [B(f"psbank{i}") for i in range(7)]
        b_psb = B("psb")
        pstate = dict(n=0)

        rings = {"X": (0, 7), "F": (0, 3), "B": (3, 4)}
        rpos_ps = {"X": 0, "F": 0, "B": 0}

        def PS(nb=1, ring="X"):
            base, size = rings[ring]
            i = rpos_ps[ring] % size
            if i + nb > size:
                i = 0
            rpos_ps[ring] = i + nb
            return base + i, pbank[base + i:base + i + nb]

        ident_f = sb("ident_f", [128, 128], F32); b_ident = B()
        ident_b = sb("ident_b", [128, 128], BF16)
        masks = sb("masks", [128, 3, 128], F32); b_masks = B()
        sgum = sb("sgum", [128, 128], F32)
        cmask = sb("cmask", [128, 6, 128], BF16)
        ones_b = sb("ones_b", [128, 128], BF16)
        ones_f = sb("ones_f", [128, 128], F32)
        b_const = B("const")
        cx.dma("sp", ident_f[:], k_ident, writes=[b_const])
        cx.dma("sp", masks[:], k_masks, writes=[b_const])
        cx.dma("sp", sgum[:], k_sgum, writes=[b_const])
        cx.dma("sp", cmask[:], k_cm, writes=[b_const])
        cx.op("dve", lambda h: h.memset(ones_b[:], 1.0), writes=[b_const])
        cx.op("dve", lambda h: h.memset(ones_f[:], 1.0), writes=[b_const])
        cx.op("dve", lambda h: h.tensor_copy(out=ident_b[:], in_=ident_f[:]), reads=[b_const], writes=[b_const])
        NUI = masks[:, 0, :]; NUS = masks[:, 1, :]; LSB = masks[:, 2, :]

        def bc4(ap2d, n=4, w=128, p=128):
            return ap2d.unsqueeze(1).to_broadcast([p, n, w])

        xT = sb("xT", [128, 8, NTOK], F32)
        b_x = [B(f"x{i}") for i in range(NB + 1)]
        WA = sb("WA", [128, 8 * INW + 8 * D], BF16)
        win = WA[:, 0:8 * INW].rearrange("p (k n) -> p k n", k=8)
        wout = WA[:, 8 * INW:8 * INW + 8 * D].rearrange("p (k n) -> p k n", k=8)
        SLOT = 8192
        b_slot = [B(f"slot{i}") for i in range(4)]
        slot_off = [8 * INW, 0, SLOT, 2 * SLOT]
        slot_buf = [[b_slot[0]], [b_slot[1]], [b_slot[2]], [b_slot[3]]]
        L_win = [b_slot[1], b_slot[2], b_slot[3]]; L_wout = [b_slot[0]]

        RBYTES = int(os.environ.get('RKB', 61)) * 1024
        R = sb("R", [128, RBYTES // 4], F32)
        rpos = dict(a=0, b=0)

        def RA(name, free_shape, dt, which="a"):
            n = int(np.prod(free_shape))
            words = (n * (2 if dt == BF16 else 4) + 3) // 4
            words = (words + 7) // 8 * 8
            o = rpos[which]
            rpos[which] = o + words
            assert rpos[which] * 4 <= RBYTES, (name, rpos[which] * 4)
            ap = R[:, o:o + words]
            if dt == BF16:
                ap = ap.bitcast(BF16)[:, 0:n]
            else:
                ap = ap[:, 0:n]
            if len(free_shape) == 2:
                ap = ap.rearrange("p (a b) -> p a b", a=free_shape[0])
            elif len(free_shape) == 3:
                ap = ap.rearrange("p (a b c) -> p a b c", a=free_shape[0], b=free_shape[1])
            return ap

        pcols = sb("pcols", [128, 113], F32); b_pcols = B("pcols")
        rowsA_sb = sb("rowsA_sb", [113, 128], F32); b_rowsA = B()
        rowsB_sb = sb("rowsB_sb", [52, 128], F32); b_rowsB = B()
        colsB = sb("colsB", [128, 52], F32); b_colsB = B("colsB")
        csT = sb("csT", [128, 2, 8], BF16); b_cs = B("cs")
        modT = sb("modT", [128, 48, 2], F32); b_mod = B("mod")
        lay = sb("lay", [128, 6, 2, 8], F32); b_lay = B("lay")
        WmT = sb("WmT", [128, 4, 128], BF16); b_wm = B("wm")
        BSRB = sb("BSRB", [128, 4, 128], BF16)
        SGBC = sb("SGBC", [128, 512], F32)
        hd = sb("hd", [128, 16], F32)
        b_lp = B("layerparams")
        FING = WA[:, 0:2 * D].bitcast(F32); b_fing = B()
        S = sb("S", [128, 4, 128], F32); S_bf = sb("S_bf", [128, 4, 128], BF16); b_S = B("S"); b_Sbf = B("Sbf")
        XP = sb("XP", [128, 12, 131], BF16); b_XP = B("XP")

        ACT = lambda fn, **kw: cx.op("act", fn, **kw)
        DVE = lambda fn, **kw: cx.op("dve", fn, **kw)
        POOL = lambda fn, **kw: cx.op("pool", fn, **kw)
        PE = lambda fn, **kw: cx.op("pe", fn, **kw)

        def mm(out, lhsT, rhs, start, stop, reads, writes, sig):
            PE(lambda h: h.matmul(out, lhsT=lhsT, rhs=rhs, start=start, stop=stop),
               reads=reads, writes=writes, sig=sig)

        def tr(out, in_, K, reads, writes, sig, dt=F32):
            idn = (ident_f if dt == F32 else ident_b)[0:K, 0:K]
            PE(lambda h: h.transpose(out, in_, idn), reads=list(reads) + [b_const], writes=writes, sig=sig)

        def bmerge(dst, src):
            for k_, v_ in src.w.items():
                if dst.w.get(k_, 0) < v_:
                    dst.w[k_] = v_
            for k_, v_ in src.r.items():
                if dst.r.get(k_, 0) < v_:
                    dst.r[k_] = v_
            dst.rw = list(set(dst.rw) | set(src.rw))
            dst.rr = list(set(dst.rr) | set(src.rr))

        def dump(name, ap, reads):
            if name in dbg_outs:
                cx.dma("sp", dbg_outs[name], ap, reads=reads)

        hT = RA("hT", [8, 128], BF16); b_hT = B("hT")
        QA = RA("QA", [12, 128], F32); b_QA = B("QA")
        b_QAg = [B("QAg0"), B("QAg1"), B("QAg2")]
        QKN = RA("QKN", [8, 128], BF16); b_QKN = B("QKN")
        vTb = RA("vTb", [4, 128], BF16); b_vT = B("vT")
        gcRB = RA("gcRB", [4, 128], F32); b_gc = B("gcRB")
        BMt = RA("BMt", [4, 128], BF16); b_BM = B("BMt")
        qgT_l = [RA("qgT" + str(i), [4, 128], BF16) for i in range(2)]; b_qg_l = [B("qgT" + str(i)) for i in range(2)]
        TM = RA("TM", [16], F32); b_TM = B("TM")
        TM2_l = [RA("TM2" + str(i), [24], F32) for i in range(2)]; b_TM2_l = [B("TM2" + str(i)) for i in range(2)]
        absb = RA("absb", [128], F32); b_absb = B("absb")
        PQ = RA("PQ", [4, 2, 128], BF16); b_PQ = B("PQ")
        PTQ = RA("PTQ", [4, 2, 128], BF16); b_PTQ = B("PTQ")
        NM_l = [RA("NM" + str(i), [4, 2, 128], BF16) for i in range(2)]; b_NM_l = [B("NM" + str(i)) for i in range(2)]
        attnT_l = [RA("attnT" + str(i), [4, 128], BF16) for i in range(2)]; b_attn_l = [B("attnT" + str(i)) for i in range(2)]
        kbg_l = [RA("kbg" + str(i), [4, 128], BF16) for i in range(2)]; b_kbg_l = [B("kbg" + str(i)) for i in range(2)]
        kg_l = [RA("kg" + str(i), [4, 128], BF16) for i in range(2)]; b_kg_l = [B("kg" + str(i)) for i in range(2)]
        vb_l = [RA("vb" + str(i), [4, 128], BF16) for i in range(2)]; b_vb_l = [B("vb" + str(i)) for i in range(2)]
        wTb = RA("wTb", [4, 128], BF16); b_wT = B("wT")
        vnew = RA("vnew", [4, 128], BF16); b_vn = B("vnew")
        zs_l = [RA("zs" + str(i), [4, 128], BF16) for i in range(2)]; b_zs_l = [B("zs" + str(i)) for i in range(2)]
        uT = RA("uT", [4, 128], BF16); b_uT = B("uT")
        vnb = RA("vnb", [512], BF16); b_vnb = B("vnb")
        ABT_l = [RA("ABT" + str(i), [8, 128], BF16) for i in range(2)]; b_AB_l = [B("ABT" + str(i)) for i in range(2)]
        sml = RA("sml", [8], F32); b_sml = B("sml")
        NTF = int(os.environ.get('NTF', 7)); NTB = int(os.environ.get('NTB', 2)); NTMP = NTF + NTB
        tmp_o = rpos["a"]
        tmpF = [RA(f"tmpF{i}", [512], F32) for i in range(NTMP)]
        b_tmp = [B(f"tmp{i}") for i in range(NTMP)]
        xstage = R[:, tmp_o:tmp_o + 1024]
        adaSt = [R[:, i * 1024:(i + 1) * 1024].bitcast(BF16).rearrange("p (k n) -> p k n", k=8) for i in range(2)]
        b_adaSt = [B("adaSt0"), B("adaSt1")]
        adaSt8 = [R[:, i * 1024:(i + 1) * 1024].bitcast(BF16).rearrange("p (k n) -> p k n", k=8) for i in range(8)]
        b_adaSt8 = [B(f"adaSt8_{i}") for i in range(8)]
        assert 8 * 1024 <= tmp_o
        tstate = dict(n=0)
        b_xst = [b_tmp[0], b_tmp[1]]

        tpos = {"F": 0, "B": 0, "b": 0}

        def TMP(pool="F"):
            if tstate.get("mode", "a") == "b":
                pool_b = tstate.get("bpool")
                if pool_b:
                    i = tpos["b"] % len(pool_b)
                    tpos["b"] += 1
                    return pool_b[i]
                i = tpos["b"] % 4
                tpos["b"] += 1
                return tmpFb[i], b_tmpb[i]
            if pool == "F":
                i = tpos["F"] % NTF
            else:
                i = NTF + tpos["B"] % NTB
            tpos[pool] += 1
            return tmpF[i], b_tmp[i]

        hTall = RA("hTall", [8, NTOK], BF16, "b"); b_hall_l = [B(f"hTall{i}") for i in range(NB + 1)]
        hidf = [RA(f"hidf{i}", [512], F32, "b") for i in range(2)]; b_hidf = [B(), B()]
        hidb = [RA(f"hidb{i}", [4, 512], BF16, "b") for i in range(2)]; b_hidb = [B(), B()]
        adaStB = RA("adaStB", [8, 256], BF16, "b"); b_adaStB = B("adaStB")
        tmpFb = [RA(f"tmpFb{i}", [512], F32, "b") for i in range(4)]
        b_tmpb = [B(f"tmpb{i}") for i in range(4)]

        def load_x(bi):
            tw = 128 if bi < NB else LS
            src = x_p[bi * 128:(bi + 1) * 128, :] if bi < NB else x_s
            xs_, bxs_ = (xstage, b_xst) if bi % 2 == 0 else (xstage2, b_xst2)
            cx.dma("sp", xs_[0:tw, :], src, writes=bxs_)
            for half in range(2):
                bk, pb = PS(1)
                for c in range(4):
                    cc = half * 4 + c
                    tr(psf[:, bk, c * 128:c * 128 + tw], xs_[0:tw, cc * 128:(cc + 1) * 128], tw,
                       reads=bxs_, writes=pb, sig=(c == 3))
                o = psf[:, bk, :].rearrange("p (c t) -> p c t", c=4)[:, :, 0:tw]
                ACT(lambda h: h.copy(out=xT[:, half * 4:half * 4 + 4, bi * 128:bi * 128 + tw], in_=o),
                    reads=pb, writes=[b_x[bi]])

        cx.dma("sp", rowsB_sb[36:52, :], c_ps, writes=[b_rowsB])

        def layer_params(l):
            cx.dma("sp", rowsA_sb[:], rowsA[l], writes=[b_rowsA])
            cx.dma("sp", rowsB_sb[0:36, :], st_conv[l], writes=[b_rowsB])
            bk, pb = PS(1)
            tr(psf[:, bk, 0:113], rowsA_sb[:], 113, reads=[b_rowsA], writes=pb, sig=True)
            DVE(lambda h: h.tensor_copy(out=pcols[:], in_=psf[:, bk, 0:113]), reads=pb, writes=[b_pcols])
            bk, pb = PS(1)
            tr(psf[:, bk, 0:52], rowsB_sb[:], 52, reads=[b_rowsB], writes=pb, sig=True)
            DVE(lambda h: h.tensor_copy(out=colsB[:], in_=psf[:, bk, 0:52]), reads=pb, writes=[b_colsB])
            if l == 0:
                ACT(lambda h: h.activation(out=csT[:].rearrange("p s k -> p (s k)"), in_=colsB[:, 36:52], func=AF.Silu),
                    reads=[b_colsB], writes=[b_cs])
            cx.dma("sp", hd[:, 0:4], dt_bias[l:l + 1, :].partition_broadcast(128), writes=[b_lp])
            cx.dma("sp", hd[:, 4:8], a_log[l:l + 1, :].partition_broadcast(128), writes=[b_lp])
            ACT(lambda h: h.activation(out=hd[:, 8:12], in_=hd[:, 4:8], func=AF.Exp), reads=[b_lp], writes=[b_lp])
            DVE(lambda h: h.tensor_scalar(out=hd[:, 8:12], in0=hd[:, 8:12], scalar1=-1.0, scalar2=None, op0=ALU.mult),
                reads=[b_lp], writes=[b_lp])
            cx.dma("sp", SGBC[:], sgu_norm_g[l:l + 1, :].partition_broadcast(128), writes=[b_lp])
            cx.dma("pool", BSRB[:].rearrange("p g q -> p (g q)"), sgu_b[l:l + 1, :].partition_broadcast(128), writes=[b_lp])
            t0, bt0 = TMP()
            cx.dma("sp", t0.rearrange("p (g q) -> p g q", g=4), sgu_w[l].rearrange("g p q -> p g q"), writes=[bt0])
            bk, pb = PS(1)
            for g in range(4):
                tr(psf[:, bk, g * 128:(g + 1) * 128], t0[:, g * 128:(g + 1) * 128], 128, reads=[bt0], writes=pb, sig=(g == 3))
            DVE(lambda h: h.tensor_tensor(out=WmT[:], in0=psf[:, bk, :].rearrange("p (g q) -> p g q", g=4),
                                          in1=bc4(sgum[:]), op=ALU.mult), reads=pb + [b_const], writes=[b_wm])

        def ada_chunk(l, ch, st, bst):
            cx.dma("pool", st, ada_w[l].rearrange("(k p) n -> p k n", p=128)[:, :, ch * 256:(ch + 1) * 256], writes=[bst])
            bk, pb = PS(1)
            for j in range(2):
                for k in range(8):
                    mm(psf[:, bk, j * 2:j * 2 + 2], st[:, k, j * 128:(j + 1) * 128], csT[:, :, k],
                       k == 0, k == 7, reads=[bst, b_cs], writes=pb, sig=(k == 7 and j == 1))
            DVE(lambda h: h.tensor_tensor(out=modT[:, ch * 2:ch * 2 + 2, :],
                                          in0=psf[:, bk, 0:4].rearrange("p (j s) -> p j s", s=2),
                                          in1=pcols[:, 16 + ch * 2:16 + ch * 2 + 2].unsqueeze(2).to_broadcast([128, 2, 2]),
                                          op=ALU.add), reads=pb + [b_pcols], writes=[b_mod])

        def ada(l):
            for ch in range(24):
                ada_chunk(l, ch, adaSt8[ch % 8], b_adaSt8[ch % 8])

        def layer_cols(l):
            for s in range(2):
                for (dst, jsc, jsh, jgt, gof) in ((0, 8, 0, 16, 0), (3, 32, 24, 40, 8)):
                    DVE(lambda h: h.scalar_tensor_tensor(out=lay[:, dst, s, :], in0=modT[:, jsc:jsc + 8, s], scalar=1.0,
                                                         in1=pcols[:, gof:gof + 8], op0=ALU.add, op1=ALU.mult),
                        reads=[b_mod, b_pcols], writes=[b_lay])
                    DVE(lambda h: h.tensor_copy(out=lay[:, dst + 1, s, :], in_=modT[:, jsh:jsh + 8, s]), reads=[b_mod], writes=[b_lay])
                    DVE(lambda h: h.tensor_copy(out=lay[:, dst + 2, s, :], in_=modT[:, jgt:jgt + 8, s]), reads=[b_mod], writes=[b_lay])

        def win_chunk_slots(k):
            lo, hi = k * INW, (k + 1) * INW
            sl = []
            for j in (1, 2, 3):
                a0, a1 = (j - 1) * SLOT, j * SLOT
                if lo < a1 and hi > a0:
                    sl.append(j)
            if hi > 3 * SLOT and 3 not in sl:
                sl.append(3)
            return sl

        def load_win_chunks(l, ks):
            for k in ks:
                cx.dma("pool", win[:, k:k + 1, :], w_in[l].rearrange("(k p) n -> p k n", p=128)[:, k:k + 1, :],
                       writes=[b_slot[j] for j in win_chunk_slots(k)])

        def load_wout(l):
            for k0 in range(0, 8, 4):
                cx.dma("pool", wout[:, k0:k0 + 4, :], w_out[l].rearrange("(k p) n -> p k n", p=128)[:, k0:k0 + 4, :],
                       writes=L_wout)

        def load_wA(l):
            load_win_chunks(l, range(8))
            load_wout(l)

        def rstd_from_ps(ps_ap, width, scale, bias_ln, dst, rd, wr):
            ACT(lambda h: h.activation(out=dst, in_=ps_ap, func=AF.Ln, bias=EPS, scale=scale), reads=rd, writes=wr)
            ACT(lambda h: h.activation(out=dst, in_=dst, func=AF.Exp, scale=-0.5, bias=bias_ln), reads=wr, writes=wr)

        def norm_block(col0, tw, s, a_idx, dst, b_dst, xbufs, ring="X"):
            sq, bsq = TMP(); sq2, bsq2 = TMP()
            sqv = sq.bitcast(BF16).rearrange("p (c t) -> p c t", c=8)
            ACT(lambda h: h.activation(out=sqv[:, :, 0:tw], in_=xT[:, :, col0:col0 + tw], func=AF.Square),
                reads=xbufs, writes=[bsq])
            bk, pb = PS(1, ring)
            for c in range(8):
                mm(psf[:, bk, 0:tw], ones_b[:], sqv[:, c, 0:tw], c == 0, c == 7, reads=[bsq, b_const], writes=pb, sig=(c == 7))
            rstd_from_ps(psf[:, bk, 0:tw], tw, 1.0 / D, 0.0, sq2[:, 0:tw], pb, [bsq2])
            t3, bt3 = TMP(); t4, bt4 = TMP()
            rbc = sq2[:, 0:tw].unsqueeze(1).to_broadcast([128, 4, tw])
            for hf, (tq, bq) in enumerate(((t3, bt3), (t4, bt4))):
                tqv = tq.rearrange("p (c t) -> p c t", c=4)[:, :, 0:tw]
                DVE(lambda h: h.tensor_tensor(out=tqv, in0=xT[:, hf * 4:hf * 4 + 4, col0:col0 + tw], in1=rbc, op=ALU.mult),
                    reads=xbufs + [bsq2], writes=[bq])
            for c in range(8):
                tt = (t3 if c < 4 else t4)[:, (c % 4) * 128:(c % 4) * 128 + tw]
                bb = bt3 if c < 4 else bt4
                if c % 2 == 0:
                    ACT(lambda h: h.activation(out=dst[:, c, 0:tw], in_=tt, func=AF.Identity, bias=lay[:, a_idx + 1, s, c:c + 1],
                                               scale=lay[:, a_idx, s, c:c + 1]), reads=[bb, b_lay], writes=[b_dst])
                else:
                    DVE(lambda h: h.tensor_scalar(out=dst[:, c, 0:tw], in0=tt, scalar1=lay[:, a_idx, s, c:c + 1],
                                                  scalar2=lay[:, a_idx + 1, s, c:c + 1], op0=ALU.mult, op1=ALU.add),
                        reads=[bb, b_lay], writes=[b_dst])

        def phaseA_block(l, bi):
            smp = bi == NB
            tw = LS if smp else 128
            s = 1 if smp else 0
            col0 = bi * 128
            xb = [b_x[bi]]
            rdw = L_win + [b_hT]
            par = bi % 2
            NM, b_NM = NM_l[par], b_NM_l[par]; attnT, b_attn = attnT_l[par], b_attn_l[par]
            kbg, b_kbg = kbg_l[par], b_kbg_l[par]; kg, b_kg = kg_l[par], b_kg_l[par]; vb, b_vb = vb_l[par], b_vb_l[par]
            qgT, b_qg = qgT_l[par], b_qg_l[par]; zs, b_zs = zs_l[par], b_zs_l[par]; ABT, b_AB = ABT_l[par], b_AB_l[par]
            TM2, b_TM2 = TM2_l[par], b_TM2_l[par]
            RG = "F"
            pres = {}

            norm_block(col0, tw, s, 0, hT, b_hT, xb, ring="F")
            yield 1
            if l == 0 and bi == 0:
                dump("hT", hT, [b_hT])

            def proj(c0, nch, M=128):
                bk, pb = PS(1, RG)
                for j in range(nch):
                    for k in range(8):
                        mm(psf[0:M, bk, j * 128:j * 128 + tw], win[:, k, c0 + j * 128:c0 + j * 128 + M], hT[:, k, 0:tw],
                           k == 0, k == 7, reads=rdw, writes=pb, sig=(k == 7))
                    if j < nch - 1:
                        yield 1
                pres["v"] = (bk, pb)

            def pv(bk, nch=4):
                return psf[:, bk, :].rearrange("p (c t) -> p c t", c=4)[:, 0:nch, 0:tw]

            if smp:
                DVE(lambda h: h.tensor_copy(out=XP[:, :, 0:3], in_=colsB[:, 0:36].rearrange("p (i c) -> p c i", i=3)),
                    reads=[b_colsB], writes=[b_XP])
            elif bi == 0:
                DVE(lambda h: h.memset(XP[:, :, 0:3], 0.0), writes=[b_XP])
            for g3 in range(3):
                yield from proj(1024 + g3 * 512, 4)
                bk, pb = pres["v"]
                yield 1
                eng = ACT if g3 != 1 else DVE
                if eng is ACT:
                    ACT(lambda h: h.copy(out=XP[:, g3 * 4:g3 * 4 + 4, 3:3 + tw], in_=pv(bk)), reads=pb, writes=[b_XP])
                else:
                    DVE(lambda h: h.tensor_copy(out=XP[:, g3 * 4:g3 * 4 + 4, 3:3 + tw], in_=pv(bk)), reads=pb, writes=[b_XP])
            if smp or bi == NB - 1:
                dst = o_sconv if smp else o_pconv
                for g3 in range(3):
                    for c in range(4):
                        tr(psb[0:3, c * 128:(c + 1) * 128], XP[:, g3 * 4 + c, tw:tw + 3], 128, reads=[b_XP], writes=[b_psb], sig=(c == 3), dt=BF16)
                    t0, bt0 = TMP(RG)
                    DVE(lambda h: h.tensor_copy(out=t0[0:3, :], in_=psb[0:3, 0:512]), reads=[b_psb], writes=[bt0])
                    cx.dma("sp", dst[l][:, g3 * 512:(g3 + 1) * 512], t0[0:3, :], reads=[bt0])
            yield 1
            def cwb(g, i):
                return pcols[:, 64 + i * 12 + g * 4:64 + i * 12 + g * 4 + 4].unsqueeze(2).to_broadcast([128, 4, tw])
            def CE(g):
                return DVE if g == 2 else POOL
            for g in range(3):
                bmerge(b_QAg[g], b_QA)
            for g in range(3):
                CE(g)(lambda h: h.tensor_tensor(out=QA[:, g * 4:g * 4 + 4, 0:tw], in0=XP[:, g * 4:g * 4 + 4, 0:tw], in1=cwb(g, 0), op=ALU.mult),
                      reads=[b_XP, b_pcols], writes=[b_QAg[g]])
            for i in range(1, 4):
                tl = []
                for g in range(3):
                    t_, bt_ = TMP(RG)
                    tv_ = t_.rearrange("p (c t) -> p c t", c=4)[:, :, 0:tw]
                    CE(g)(lambda h: h.tensor_tensor(out=tv_, in0=XP[:, g * 4:g * 4 + 4, i:i + tw], in1=cwb(g, i), op=ALU.mult),
                          reads=[b_XP, b_pcols], writes=[bt_])
                    tl.append((tv_, bt_))
                for g in range(3):
                    tv_, bt_ = tl[g]
                    CE(g)(lambda h: h.tensor_tensor(out=QA[:, g * 4:g * 4 + 4, 0:tw], in0=QA[:, g * 4:g * 4 + 4, 0:tw], in1=tv_, op=ALU.add),
                          reads=[bt_, b_QAg[g]], writes=[b_QAg[g]])
                yield 1
            for g in range(3):
                bmerge(b_QA, b_QAg[g])
            if not smp:
                POOL(lambda h: h.tensor_copy(out=XP[:, :, 0:3], in_=XP[:, :, 128:131]), reads=[b_XP, b_QA], writes=[b_XP])
            yield 1
            yield from proj(0, 4)
            bk, pb = pres["v"]
            ACT(lambda h: h.activation(out=uT[:, :, 0:tw], in_=pv(bk), func=AF.Gelu), reads=pb, writes=[b_uT])
            yield 1
            bk, pb = PS(1, RG)
            for k in range(8):
                mm(psf[0:tw, bk, :], hT[:, k, 0:tw], win[:, k, 512:1024], k == 0, k == 7, reads=rdw, writes=pb, sig=(k == 7))
            vg, bvg = TMP(RG)
            ACT(lambda h: h.activation(out=vg[0:tw, :], in_=psf[0:tw, bk, :], func=AF.Gelu), reads=pb, writes=[bvg])
            junk, bj = TMP(RG)
            ACT(lambda h: h.activation(out=junk[0:tw, :], in_=vg[0:tw, :], func=AF.Square, accum_out=sml[0:tw, 0:1]),
                reads=[bvg], writes=[bj, b_sml])
            rstd_from_ps(sml[0:tw, 0:1], 1, 1.0 / 512, 0.0, sml[0:tw, 1:2], [b_sml], [b_sml])
            vnf, b_vnf = TMP(RG)
            DVE(lambda h: h.scalar_tensor_tensor(out=vnf[0:tw, :], in0=vg[0:tw, :], scalar=sml[0:tw, 1:2], in1=SGBC[0:tw, :],
                                                 op0=ALU.mult, op1=ALU.mult), reads=[bvg, b_sml, b_lp], writes=[b_vnf])
            ACT(lambda h: h.copy(out=vnb[0:tw, :], in_=vnf[0:tw, :]), reads=[b_vnf], writes=[b_vnb])
            if smp:
                cx.dma("sp", o_sgu[l], vnf[0:LS, :], reads=[b_vnf])
            yield 1
            bk, pb = PS(1, RG)
            for g in range(4):
                mm(psf[:, bk, g * 128:g * 128 + tw], vnb[0:tw, g * 128:(g + 1) * 128], WmT[0:tw, g, 0:tw], True, True,
                   reads=[b_vnb, b_wm], writes=pb, sig=(g == 3))
            t0, bt0 = TMP(RG)
            t0v = t0.rearrange("p (g q) -> p g q", g=4)[:, :, 0:tw]
            DVE(lambda h: h.tensor_tensor(out=t0v, in0=pv(bk), in1=BSRB[:, :, 0:tw], op=ALU.add), reads=pb + [b_lp], writes=[bt0])
            DVE(lambda h: h.tensor_tensor(out=ABT[:, 0:4, 0:tw], in0=t0v, in1=uT[:, :, 0:tw], op=ALU.mult),
                reads=[bt0, b_uT], writes=[b_AB])
            yield 1
            yield from proj(3072, 1, M=8)
            bk, pb = pres["v"]
            ACT(lambda h: h.copy(out=absb[0:8, 0:tw], in_=psf[0:8, bk, 0:tw]), reads=pb, writes=[b_absb])
            bk2, pb2 = PS(2, RG)
            rbv = psf[:, bk2:bk2 + 2, :].rearrange("p b (r t) -> p (b r) t", r=4)
            for r0 in (0, 4):
                am_, bam = TMP(RG)
                am = am_.rearrange("p (r t) -> p r t", r=4)
                DVE(lambda h: h.tensor_tensor(out=am[0:8, :, 0:tw], in0=absb[0:8, 0:tw].unsqueeze(1).to_broadcast([8, 4, tw]),
                                              in1=ident_f[0:8, r0:r0 + 4].unsqueeze(2).to_broadcast([8, 4, tw]), op=ALU.mult),
                    reads=[b_absb, b_const], writes=[bam])
                for r in range(4):
                    mm(rbv[:, r0 + r, 0:tw], ones_f[0:8, :], am[0:8, r, 0:tw], True, True, reads=[bam, b_const], writes=pb2, sig=(r == 3))
            betaRB_, b_beta = TMP(RG)
            betaRB = betaRB_.rearrange("p (g q) -> p g q", g=4)
            spb_, b_spb = TMP(RG)
            spb = spb_.rearrange("p (g q) -> p g q", g=4)
            ACT(lambda h: h.activation(out=spb[:, :, 0:tw], in_=rbv[:, 0:4, 0:tw], func=AF.Exp, scale=-1.0), reads=pb2, writes=[b_spb])
            ACT(lambda h: h.activation(out=spb[:, :, 0:tw], in_=spb[:, :, 0:tw], func=AF.Ln, bias=1.0, scale=1.0), reads=[b_spb], writes=[b_spb])
            ACT(lambda h: h.activation(out=betaRB[:, :, 0:tw], in_=spb[:, :, 0:tw], func=AF.Exp, scale=-1.0), reads=[b_spb], writes=[b_beta])
            sp_, bsp = TMP(RG)
            spv = sp_.rearrange("p (g q) -> p g q", g=4)
            for hh in range(4):
                ACT(lambda h: h.activation(out=spv[:, hh, 0:tw], in_=rbv[:, 4 + hh, 0:tw], func=AF.Exp, bias=hd[:, hh:hh + 1], scale=1.0),
                    reads=pb2 + [b_lp], writes=[bsp])
            ACT(lambda h: h.activation(out=spv[:, :, 0:tw], in_=spv[:, :, 0:tw], func=AF.Ln, bias=1.0, scale=1.0), reads=[bsp], writes=[bsp])
            for hh in range(4):
                DVE(lambda h: h.tensor_tensor_scan(out=gcRB[:, hh, 0:tw], data0=ones_f[:, 0:tw], data1=spv[:, hh, 0:tw],
                                                   initial=0.0, op0=ALU.mult, op1=ALU.add), reads=[bsp, b_const], writes=[b_gc])
            POOL(lambda h: h.tensor_tensor(out=gcRB[:, :, 0:tw], in0=gcRB[:, :, 0:tw], in1=hd[:, 8:12].unsqueeze(2).to_broadcast([128, 4, tw]),
                                           op=ALU.mult), reads=[b_gc, b_lp], writes=[b_gc])
            POOL(lambda h: h.tensor_tensor(out=BMt[0:tw, :, 0:tw], in0=betaRB[0:tw, :, 0:tw], in1=bc4(cmask[0:tw, 5, 0:tw], 4, tw, tw), op=ALU.mult),
                 reads=[b_beta, b_const], writes=[b_BM])
            yield 1
            bk, pb = PS(1, RG)
            for hh in range(4):
                mm(psf[0:tw, bk, hh:hh + 1], gcRB[0:1, hh, 0:tw], ones_f[0:1, 0:1], True, True, reads=[b_gc, b_const], writes=pb, sig=False)
                mm(psf[0:tw, bk, 4 + hh:5 + hh], betaRB[0:1, hh, 0:tw], ones_f[0:1, 0:1], True, True, reads=[b_beta, b_const],
                   writes=pb, sig=(hh == 3))
            DVE(lambda h: h.tensor_copy(out=TM[0:tw, 0:8], in_=psf[0:tw, bk, 0:8]), reads=pb, writes=[b_TM])
            DVE(lambda h: h.tensor_scalar(out=TM2[0:tw, 0:8], in0=TM[0:tw, 0:8], scalar1=-1.0, scalar2=None, op0=ALU.mult),
                reads=[b_TM], writes=[b_TM2])
            ACT(lambda h: h.activation(out=TM2[0:tw, 8:12], in_=TM[0:tw, 0:4], func=AF.Exp), reads=[b_TM], writes=[b_TM2])
            DVE(lambda h: h.tensor_tensor(out=TM2[0:tw, 8:12], in0=TM2[0:tw, 8:12], in1=TM[0:tw, 4:8], op=ALU.mult),
                reads=[b_TM, b_TM2], writes=[b_TM2])
            for hh in range(4):
                ACT(lambda h: h.activation(out=TM2[0:tw, 12 + hh:13 + hh], in_=TM[0:tw, hh:hh + 1], func=AF.Exp,
                                           bias=gcRB[0:tw, hh, tw - 1:tw], scale=-1.0), reads=[b_TM, b_gc], writes=[b_TM2])
                ACT(lambda h: h.activation(out=TM2[:, 16 + hh:17 + hh], in_=gcRB[:, hh, tw - 1:tw], func=AF.Exp), reads=[b_gc], writes=[b_TM2])

            yield 1
            yield from proj(2560, 4)
            bk, pb = pres["v"]
            ACT(lambda h: h.activation(out=zs[:, :, 0:tw], in_=pv(bk), func=AF.Silu), reads=pb, writes=[b_zs])
            yield 1
            ACT(lambda h: h.activation(out=QA[:, :, 0:tw], in_=QA[:, :, 0:tw], func=AF.Silu), reads=[b_QA], writes=[b_QA])
            ACT(lambda h: h.copy(out=vTb[:, :, 0:tw], in_=QA[:, 8:12, 0:tw]), reads=[b_QA], writes=[b_vT])
            yield 1
            for half in range(2):
                sq, bsq = TMP(RG)
                sqv = sq.bitcast(BF16)[:, 0:512].rearrange("p (c t) -> p c t", c=4)
                ACT(lambda h: h.activation(out=sqv[:, :, 0:tw], in_=QA[:, half * 4:half * 4 + 4, 0:tw], func=AF.Square),
                    reads=[b_QA], writes=[bsq])
                bk, pb = PS(1, RG)
                for c in range(4):
                    mm(psf[:, bk, c * 128:c * 128 + tw], ones_b[:], sqv[:, c, 0:tw], True, True, reads=[bsq, b_const], writes=pb, sig=(c == 3))
                rr, brr = TMP(RG)
                rrv = rr.rearrange("p (c t) -> p c t", c=4)[:, :, 0:tw]
                rstd_from_ps(pv(bk), tw, 1.0, (-0.5 * float(np.log(128.0))) if half == 0 else 0.0, rrv, pb, [brr])
                POOL(lambda h: h.tensor_tensor(out=QKN[:, half * 4:half * 4 + 4, 0:tw], in0=QA[:, half * 4:half * 4 + 4, 0:tw], in1=rrv,
                                               op=ALU.mult), reads=[b_QA, brr], writes=[b_QKN])
            eg, beg = TMP(RG)
            egv = eg.rearrange("p (c t) -> p c t", c=4)[:, :, 0:tw]
            ACT(lambda h: h.activation(out=egv, in_=gcRB[:, :, 0:tw], func=AF.Exp), reads=[b_gc], writes=[beg])
            POOL(lambda h: h.tensor_tensor(out=qgT[:, :, 0:tw], in0=QKN[:, 0:4, 0:tw], in1=egv, op=ALU.mult),
                 reads=[b_QKN, beg], writes=[b_qg])

            yield 1
            bkK, pbK = PS(1, RG); bkQ, pbQ = PS(1, RG)
            for hh in range(4):
                mm(psf[0:tw, bkK, hh * 128:hh * 128 + tw], QKN[:, 4 + hh, 0:tw], QKN[:, 4 + hh, 0:tw], True, True,
                   reads=[b_QKN], writes=pbK, sig=False)
                mm(psf[0:tw, bkQ, hh * 128:hh * 128 + tw], QKN[:, 4 + hh, 0:tw], QKN[:, hh, 0:tw], True, True,
                   reads=[b_QKN], writes=pbQ, sig=(hh == 3))
            KKv = psf[0:tw, bkK, :].rearrange("p (c t) -> p c t", c=4)[:, :, 0:tw]
            QKv = psf[0:tw, bkQ, :].rearrange("p (c t) -> p c t", c=4)[:, :, 0:tw]

            def etile(src, b_src, mask, sgn, bias_col0):
                e, be = TMP(RG)
                ev = e.rearrange("p (c t) -> p c t", c=4)[0:tw, :, 0:tw]
                POOL(lambda h: h.tensor_tensor(out=ev, in0=src[0:tw, :, 0:tw], in1=bc4(mask[0:tw, 0:tw], 4, tw, tw), op=ALU.add),
                     reads=[b_src, b_const], writes=[be])
                for hh in range(4):
                    bias = (TM if bias_col0 < 100 else TM2)[0:tw, (bias_col0 % 100) + hh:(bias_col0 % 100) + hh + 1]
                    ACT(lambda h: h.activation(out=ev[:, hh, :], in_=ev[:, hh, :], func=AF.Exp, bias=bias, scale=sgn),
                        reads=[be, b_TM, b_TM2], writes=[be])
                return ev, be

            eN, beN = etile(gcRB, b_gc, LSB, -1.0, 0)
            POOL(lambda h: h.tensor_tensor(out=eN, in0=eN, in1=TM2[0:tw, 4:8].unsqueeze(2).to_broadcast([tw, 4, tw]), op=ALU.mult),
                 reads=[beN, b_TM2], writes=[beN])
            DVE(lambda h: h.tensor_tensor(out=NM[0:tw, :, 0, 0:tw], in0=eN, in1=KKv, op=ALU.mult), reads=[beN] + pbK, writes=[b_NM])
            yield 1
            eA, beA = etile(gcRB, b_gc, NUI, 1.0, 100)
            DVE(lambda h: h.tensor_tensor(out=attnT[0:tw, :, 0:tw], in0=eA, in1=QKv, op=ALU.mult), reads=[beA] + pbQ, writes=[b_attn])
            eM_, beM = TMP(RG)
            eM = eM_.rearrange("p (c t) -> p c t", c=4)[0:tw, :, 0:tw]
            POOL(lambda h: h.tensor_tensor(out=eM, in0=eA, in1=BMt[0:tw, :, 0:tw], op=ALU.mult), reads=[beA, b_BM], writes=[beM])
            DVE(lambda h: h.scalar_tensor_tensor(out=NM[0:tw, :, 1, 0:tw], in0=eM, scalar=-1.0, in1=KKv, op0=ALU.mult, op1=ALU.mult),
                reads=[beM] + pbK, writes=[b_NM])
            pbv = psb[:].rearrange("p (a c t) -> p a c t", a=2, c=4)
            for hh in range(4):
                tr(pbv[0:tw, 0, hh, :], QKN[:, 4 + hh, 0:tw], 128, reads=[b_QKN], writes=[b_psb], sig=False, dt=BF16)
                tr(pbv[0:tw, 1, hh, :], vTb[:, hh, 0:tw], 128, reads=[b_vT], writes=[b_psb], sig=(hh == 3), dt=BF16)

            def bcol(ap_cols):
                return ap_cols.unsqueeze(2).to_broadcast([tw, 4, 128])
            DVE(lambda h: h.tensor_tensor(out=kbg[0:tw], in0=pbv[0:tw, 0], in1=bcol(TM2[0:tw, 8:12]), op=ALU.mult),
                reads=[b_psb, b_TM2], writes=[b_kbg])
            DVE(lambda h: h.tensor_tensor(out=kg[0:tw], in0=pbv[0:tw, 0], in1=bcol(TM2[0:tw, 12:16]), op=ALU.mult),
                reads=[b_psb, b_TM2], writes=[b_kg])
            DVE(lambda h: h.tensor_tensor(out=vb[0:tw], in0=pbv[0:tw, 1], in1=bcol(TM[0:tw, 4:8]), op=ALU.mult),
                reads=[b_psb, b_TM], writes=[b_vb])
            RG = "B"
            yield "F_DONE"
            d8 = bc4(cmask[0:tw, 0, 0:tw], 4, tw, tw)
            DVE(lambda h: h.tensor_tensor(out=PQ[0:tw, :, 0, 0:tw], in0=NM[0:tw, :, 1, 0:tw], in1=d8, op=ALU.mult),
                reads=[b_NM, b_const], writes=[b_PQ])
            DVE(lambda h: h.tensor_tensor(out=PTQ[0:tw, :, 0, 0:tw], in0=NM[0:tw, :, 0, 0:tw], in1=d8, op=ALU.mult),
                reads=[b_NM, b_const], writes=[b_PTQ])
            idb = bc4(ident_f[0:tw, 0:tw], 4, tw, tw)
            DVE(lambda h: h.tensor_tensor(out=PQ[0:tw, :, 1, 0:tw], in0=PQ[0:tw, :, 0, 0:tw], in1=idb, op=ALU.add),
                reads=[b_PQ, b_const], writes=[b_PQ])
            DVE(lambda h: h.tensor_tensor(out=PTQ[0:tw, :, 1, 0:tw], in0=PTQ[0:tw, :, 0, 0:tw], in1=idb, op=ALU.add),
                reads=[b_PTQ, b_const], writes=[b_PTQ])
            for lev in (1, 2, 3):
                yield 2
                last = lev == 3
                bkA, pbA = PS(2, RG); bkB, pbB = PS(2, RG)
                Av = psf[0:tw, bkA:bkA + 2, :].rearrange("p b (h x) -> p (b h) x", h=2)
                Bv = psf[0:tw, bkB:bkB + 2, :].rearrange("p b (h x) -> p (b h) x", h=2)
                rd = [b_PQ, b_PTQ]
                for hh in range(4):
                    Ph = PQ[0:tw, hh, 0, 0:tw]; Qh = PQ[0:tw, hh, 1, 0:tw]
                    PTh = PTQ[0:tw, hh, 0, 0:tw]; Qnh = PTQ[0:tw, hh, 1, 0:tw]
                    if not last:
                        mm(Av[:, hh, 0:tw], PTh, Ph, True, True, reads=rd, writes=pbA, sig=False)
                        mm(Bv[:, hh, 0:tw], Ph, PTh, True, True, reads=rd, writes=pbA + pbB, sig=(hh == 3 and lev == 1))
                    if lev > 1:
                        mm(Av[:, hh, 128:128 + tw], PTh, Qh, True, True, reads=rd, writes=pbA, sig=False)
                        mm(Bv[:, hh, 128:128 + tw], Ph, Qnh, True, True, reads=rd, writes=pbA + pbB, sig=(hh == 3))
                if lev > 1:
                    DVE(lambda h: h.tensor_tensor(out=PQ[0:tw, :, 1, 0:tw], in0=PQ[0:tw, :, 1, 0:tw], in1=Av[:, :, 128:128 + tw], op=ALU.add),
                        reads=pbA + [b_PQ], writes=[b_PQ])
                    DVE(lambda h: h.tensor_tensor(out=PTQ[0:tw, :, 1, 0:tw], in0=PTQ[0:tw, :, 1, 0:tw], in1=Bv[:, :, 128:128 + tw], op=ALU.add),
                        reads=pbB + [b_PTQ], writes=[b_PTQ])
                if not last:
                    ACT(lambda h: h.copy(out=PQ[0:tw, :, 0, 0:tw], in_=Av[:, :, 0:tw]), reads=pbA, writes=[b_PQ])
                    ACT(lambda h: h.copy(out=PTQ[0:tw, :, 0, 0:tw], in_=Bv[:, :, 0:tw]), reads=pbB, writes=[b_PTQ])
            li = 0
            for bsz in (8, 16, 32, 64):
                li += 1
                if bsz >= tw:
                    break
                cb = bc4(cmask[0:tw, li, 0:tw], 4, tw, tw)
                yield 2
                bk1, pb1 = PS(1, RG); bk2_, pb2_ = PS(1, RG)
                Y1 = psf[0:tw, bk1, :].rearrange("p (c t) -> p c t", c=4)[:, :, 0:tw]
                Y2 = psf[0:tw, bk2_, :].rearrange("p (c t) -> p c t", c=4)[:, :, 0:tw]
                for hh in range(4):
                    mm(Y1[:, hh, :], NM[0:tw, hh, 0, 0:tw], PQ[0:tw, hh, 1, 0:tw], True, True, reads=[b_NM, b_PQ], writes=pb1, sig=False)
                    mm(Y2[:, hh, :], NM[0:tw, hh, 1, 0:tw], PTQ[0:tw, hh, 1, 0:tw], True, True, reads=[b_NM, b_PTQ], writes=pb2_, sig=(hh == 3))
                DVE(lambda h: h.tensor_tensor(out=PQ[0:tw, :, 0, 0:tw], in0=Y1, in1=cb, op=ALU.mult), reads=pb1 + [b_const], writes=[b_PQ])
                DVE(lambda h: h.tensor_tensor(out=PTQ[0:tw, :, 0, 0:tw], in0=Y2, in1=cb, op=ALU.mult), reads=pb2_ + [b_const], writes=[b_PTQ])
                yield 2
                bk3, pb3 = PS(1, RG); bk4, pb4 = PS(1, RG)
                Z1 = psf[0:tw, bk3, :].rearrange("p (c t) -> p c t", c=4)[:, :, 0:tw]
                Z2 = psf[0:tw, bk4, :].rearrange("p (c t) -> p c t", c=4)[:, :, 0:tw]
                for hh in range(4):
                    mm(Z1[:, hh, :], PTQ[0:tw, hh, 1, 0:tw], PQ[0:tw, hh, 0, 0:tw], True, True, reads=[b_PQ, b_PTQ], writes=pb3, sig=False)
                    mm(Z2[:, hh, :], PQ[0:tw, hh, 1, 0:tw], PTQ[0:tw, hh, 0, 0:tw], True, True, reads=[b_PQ, b_PTQ], writes=pb4, sig=(hh == 3))
                DVE(lambda h: h.tensor_tensor(out=PQ[0:tw, :, 1, 0:tw], in0=PQ[0:tw, :, 1, 0:tw], in1=Z1, op=ALU.add),
                    reads=pb3 + pb4 + [b_PQ], writes=[b_PQ])
                DVE(lambda h: h.tensor_tensor(out=PTQ[0:tw, :, 1, 0:tw], in0=PTQ[0:tw, :, 1, 0:tw], in1=Z2, op=ALU.add),
                    reads=pb3 + pb4 + [b_PTQ], writes=[b_PTQ])
            yield 2
            bkW, pbW = PS(1, RG)
            for hh in range(4):
                mm(psf[:, bkW, hh * 128:hh * 128 + tw], kbg[0:tw, hh, :], PQ[0:tw, hh, 1, 0:tw], True, True,
                   reads=[b_PQ, b_kbg], writes=pbW, sig=(hh == 3))
            ACT(lambda h: h.mul(out=wTb[:, :, 0:tw], in_=pv(bkW), mul=-1.0), reads=pbW, writes=[b_wT])
            if smp:
                cx.dma("sp", S[:], st_delta[l].rearrange("h k v -> k h v"), writes=[b_S])
                ACT(lambda h: h.copy(out=S_bf[:], in_=S[:]), reads=[b_S], writes=[b_Sbf])
            elif bi == 0:
                DVE(lambda h: h.memset(S[:], 0.0), writes=[b_S])
                DVE(lambda h: h.memset(S_bf[:], 0.0), writes=[b_Sbf])
            yield 2
            bk, pb = PS(1, RG)
            for hh in range(4):
                mm(psf[0:tw, bk, hh * 128:(hh + 1) * 128], PQ[0:tw, hh, 1, 0:tw], vb[0:tw, hh, :], True, False,
                   reads=[b_PQ, b_vb], writes=pb, sig=False)
                mm(psf[0:tw, bk, hh * 128:(hh + 1) * 128], wTb[:, hh, 0:tw], S_bf[:, hh, :], False, True, reads=[b_wT, b_Sbf], writes=pb,
                   sig=(hh == 3))
            DVE(lambda h: h.tensor_copy(out=vnew[0:tw], in_=psf[0:tw, bk, :].rearrange("p (c t) -> p c t", c=4)), reads=pb, writes=[b_vn])
            yield 2
            bkO, pbO = PS(1, RG)
            for hh in range(4):
                mm(psf[:, bkO, hh * 128:hh * 128 + tw], S_bf[:, hh, :], qgT[:, hh, 0:tw], True, False, reads=[b_Sbf, b_qg], writes=pbO, sig=False)
                mm(psf[:, bkO, hh * 128:hh * 128 + tw], vnew[0:tw, hh, :], attnT[0:tw, hh, 0:tw], False, True, reads=[b_vn, b_attn],
                   writes=pbO, sig=(hh == 3))
            bk, pb = PS(1, RG)
            for hh in range(4):
                mm(psf[:, bk, hh * 128:(hh + 1) * 128], kg[0:tw, hh, :], vnew[0:tw, hh, :], True, True, reads=[b_kg, b_vn], writes=pb,
                   sig=(hh == 3))
            for hh in range(4):
                DVE(lambda h: h.scalar_tensor_tensor(out=S[:, hh, :], in0=S[:, hh, :], scalar=TM2[:, 16 + hh:17 + hh],
                                                     in1=psf[:, bk, hh * 128:(hh + 1) * 128], op0=ALU.mult, op1=ALU.add),
                    reads=pb + [b_S, b_TM2], writes=[b_S])
            ACT(lambda h: h.copy(out=S_bf[:], in_=S[:]), reads=[b_S], writes=[b_Sbf])
            if smp or bi == NB - 1:
                cx.dma("sp", (o_sdelta if smp else o_pdelta)[l].rearrange("h k v -> k h v"), S[:], reads=[b_S])
            yield 2
            osq, bosq = TMP(RG)
            osqv = osq.bitcast(BF16)[:, 0:512].rearrange("p (c t) -> p c t", c=4)
            ACT(lambda h: h.activation(out=osqv[:, :, 0:tw], in_=pv(bkO), func=AF.Square), reads=pbO, writes=[bosq])
            bk, pb = PS(1, RG)
            for c in range(4):
                mm(psf[:, bk, c * 128:c * 128 + tw], ones_b[:], osqv[:, c, 0:tw], True, True, reads=[bosq, b_const], writes=pb, sig=(c == 3))
            ro, bro = TMP(RG)
            rov = ro.rearrange("p (c t) -> p c t", c=4)[:, :, 0:tw]
            rstd_from_ps(pv(bk), tw, 1.0 / 128, 0.0, rov, pb, [bro])
            t1, bt1 = TMP(RG)
            t1v = t1.rearrange("p (c t) -> p c t", c=4)[:, :, 0:tw]
            DVE(lambda h: h.scalar_tensor_tensor(out=t1v, in0=pv(bkO), scalar=pcols[:, 112:113], in1=rov, op0=ALU.mult, op1=ALU.mult),
                reads=pbO + [bro, b_pcols], writes=[bt1])
            DVE(lambda h: h.tensor_tensor(out=ABT[:, 4:8, 0:tw], in0=t1v, in1=zs[:, :, 0:tw], op=ALU.mult), reads=[bt1, b_zs], writes=[b_AB])
            if l == 0 and bi == 0:
                dump("ABT", ABT, [b_AB])
            yield 2
            bk2, pb2 = PS(2, RG)
            ov = psf[:, bk2:bk2 + 2, :].rearrange("p b (c t) -> p (b c) t", c=4)
            for m in range(8):
                for k in range(8):
                    mm(ov[:, m, 0:tw], wout[:, k, m * 128:(m + 1) * 128], ABT[:, k, 0:tw], k == 0, k == 7, reads=L_wout + [b_AB], writes=pb2,
                       sig=(k == 7 and m == 7))
            for m in range(8):
                DVE(lambda h: h.scalar_tensor_tensor(out=xT[:, m, col0:col0 + tw], in0=ov[:, m, 0:tw], scalar=lay[:, 2, s, m:m + 1],
                                                     in1=xT[:, m, col0:col0 + tw], op0=ALU.mult, op1=ALU.add),
                    reads=pb2 + [b_lay] + xb, writes=xb)

        tiles = [(t * 512, 512, 0, list(range(t * 4, t * 4 + 4))) for t in range(L // 512)]
        if L % 512:
            t0_ = (L // 512) * 512
            tiles.append((t0_, L - t0_, 0, list(range(t0_ // 128, NB))))
        tiles.append((L, LS, 1, [NB]))

        def load_eighth(l, e):
            sl = e % 4
            o = slot_off[sl]
            up = WA[:, o:o + 4096].rearrange("p (k n) -> p k n", k=8)
            dn = WA[:, o + 4096:o + 8192].rearrange("p (k n) -> p k n", k=4)
            cx.dma("pool", up, w_up[l].rearrange("(k p) n -> p k n", p=128)[:, :, e * 512:(e + 1) * 512], writes=slot_buf[sl])
            cx.dma("pool", dn, w_down[l][e * 512:(e + 1) * 512, :].rearrange("(k p) n -> p k n", p=128), writes=slot_buf[sl])
            return up, dn, slot_buf[sl]

        def phaseB(l, pend):
            nxt = l + 1 if l + 1 < DEPTH else None
            ada_next = list(range(24)) if nxt is not None else []
            hcnt = 0
            extra = [(hidf[0], b_hidf[0]), (hidf[1], b_hidf[1])]
            for i2 in range(2):
                hv = hidb[i2].rearrange("p a b -> p (a b)").bitcast(F32)
                extra.append((hv[:, 0:512], b_hidb[i2]))
            tstate["bpool"] = [(tmpFb[i], b_tmpb[i]) for i in range(4)] + extra
            cx.rec_begin()
            for (c0, tw, s, blks) in tiles:
                for sub in range(0, tw, 128):
                    w_ = min(128, tw - sub)
                    bi_ = (c0 + sub) // 128
                    norm_block(c0 + sub, w_, s, 3, hTall[:, :, c0 + sub:c0 + sub + w_], b_hall_l[bi_], [b_x[bi_]])
            cx.rec_flush()
            tstate["bpool"] = None
            cx.rec_begin()
            if nxt is not None:
                layer_params(nxt)
            for e in range(8):
                up, dn, wb = pend.pop(e)
                for (c0, tw, s, blks) in tiles:
                    xb = [b_x[i] for i in blks]
                    hb = hcnt % 2
                    hcnt += 1
                    for j in range(4):
                        bk, pb = PS(1)
                        for k in range(8):
                            mm(psf[:, bk, 0:tw], up[:, k, j * 128:(j + 1) * 128], hTall[:, k, c0:c0 + tw], k == 0, k == 7,
                               reads=wb + [b_hall_l[i] for i in blks], writes=pb, sig=(k == 7))
                        hf = hidf[j % 2]; bhf = b_hidf[j % 2]
                        ACT(lambda h: h.activation(out=hf[:, 0:tw], in_=psf[:, bk, 0:tw], func=AF.Relu), reads=pb, writes=[bhf])
                        POOL(lambda h: h.tensor_tensor(out=hidb[hb][:, j, 0:tw], in0=hf[:, 0:tw], in1=hf[:, 0:tw], op=ALU.mult),
                             reads=[bhf], writes=[b_hidb[hb]])
                    for m in range(8):
                        bk, pb = PS(1)
                        for k in range(4):
                            mm(psf[:, bk, 0:tw], dn[:, k, m * 128:(m + 1) * 128], hidb[hb][:, k, 0:tw], k == 0, k == 3,
                               reads=wb + [b_hidb[hb]], writes=pb, sig=(k == 3))
                        DVE(lambda h: h.scalar_tensor_tensor(out=xT[:, m, c0:c0 + tw], in0=psf[:, bk, 0:tw], scalar=lay[:, 5, s, m:m + 1],
                                                             in1=xT[:, m, c0:c0 + tw], op0=ALU.mult, op1=ALU.add),
                            reads=pb + [b_lay] + xb, writes=xb)
                    if ada_next and e >= 1:
                        ada_chunk(nxt, ada_next.pop(0), adaStB, b_adaStB)
                if e + 4 < 8:
                    pend[e + 4] = load_eighth(l, e + 4)
                elif nxt is not None:
                    if e == 4:
                        load_wout(nxt)
                    elif e == 5:
                        load_win_chunks(nxt, [0, 1])
                    elif e == 6:
                        load_win_chunks(nxt, [2, 3, 4])
                    elif e == 7:
                        load_win_chunks(nxt, [5, 6, 7])
            while ada_next:
                ada_chunk(nxt, ada_next.pop(0), adaStB, b_adaStB)
            cx.rec_flush()

        xstage2 = R[:, tmp_o + 4 * 512:tmp_o + 6 * 512]
        b_xst2 = [b_tmp[4], b_tmp[5]]
        b_smlF = [B("smlF0"), B("smlF1")]

        def final_out(bi):
            tw = 128 if bi < NB else LS
            col0 = bi * 128
            par = bi % 2
            xs, bxs = (xstage, b_xst) if par == 0 else (xstage2, b_xst2)
            sc0 = par * 4
            bsm = b_smlF[par]
            bk2, pb2 = PS(2)
            tv = psf[0:tw, bk2:bk2 + 2, :].rearrange("p b x -> p (b x)")
            for c in range(8):
                tr(tv[:, c * 128:(c + 1) * 128], xT[:, c, col0:col0 + tw], 128, reads=[b_x[bi]], writes=pb2, sig=(c == 7))
            for hf in range(2):
                ACT(lambda h: h.activation(out=tmpF[2 + hf][0:tw, :], in_=tv[:, hf * 512:(hf + 1) * 512], func=AF.Square,
                                           accum_out=sml[0:tw, sc0 + hf:sc0 + hf + 1]), reads=pb2, writes=[b_tmp[2 + hf], bsm])
            DVE(lambda h: h.tensor_tensor(out=sml[0:tw, sc0 + 2:sc0 + 3], in0=sml[0:tw, sc0:sc0 + 1], in1=sml[0:tw, sc0 + 1:sc0 + 2], op=ALU.add),
                reads=[bsm], writes=[bsm])
            rstd_from_ps(sml[0:tw, sc0 + 2:sc0 + 3], 1, 1.0 / D, 0.0, sml[0:tw, sc0 + 3:sc0 + 4], [bsm], [bsm])
            DVE(lambda h: h.scalar_tensor_tensor(out=xs[0:tw, :], in0=tv, scalar=sml[0:tw, sc0 + 3:sc0 + 4], in1=FING[0:tw, :], op0=ALU.mult, op1=ALU.mult),
                reads=pb2 + [bsm, b_fing], writes=bxs)
            dst = y_p[col0:col0 + tw, :] if bi < NB else y_s
            cx.dma("sp", dst, xs[0:tw, :], reads=bxs)

        for l in range(DEPTH):
            if l == 0:
                load_wA(l)
                cx.rec_begin()
                for bi in range(NB + 1):
                    load_x(bi)
                layer_params(l)
                ada(l)
                cx.rec_flush()
            else:
                cx.barrier()
            layer_cols(l)
            cx.barrier()
            cx.rec_begin()
            gens = [phaseA_block(l, bi) for bi in range(NB + 1)]

            def step(g):
                try:
                    return next(g)
                except StopIteration:
                    return "END"
            while step(gens[0]) != "F_DONE":
                pass
            for bi in range(NB + 1):
                cur = gens[bi]
                nxt = gens[bi + 1] if bi + 1 <= NB else None
                cur_done = False
                nxt_done = nxt is None
                while not (cur_done and nxt_done):
                    if not cur_done:
                        if step(cur) == "END":
                            cur_done = True
                    for _ in range(FSTEPS):
                        if not nxt_done:
                            if step(nxt) == "F_DONE":
                                nxt_done = True
            pend = {e: load_eighth(l, e) for e in range(4)}
            cx.rec_flush()
            cx.barrier()
            tstate["mode"] = "b"
            phaseB(l, pend)
            tstate["mode"] = "a"
        cx.barrier()
        cx.dma("sp", FING, fin_g.partition_broadcast(128), writes=[b_fing])
        cx.rec_begin()
        for bi in range(NB + 1):
            final_out(bi)
        cx.rec_flush()
        cx.finish()
    return nc


def _consts():
    ident = np.eye(128, dtype=np.float32)
    p = np.arange(128)[:, None]
    f = np.arange(128)[None, :]
    masks = np.zeros((128, 3, 128), np.float32)
    masks[:, 0, :] = np.where(f >= p, 0.0, -BIG)
    masks[:, 1, :] = np.where(f > p, 0.0, -BIG)
    masks[:, 2, :] = np.where(f < p, 0.0, BIG)
    sgum = (p // 64 <= f // 64).astype(np.float32)
    sel = np.zeros((8, 8, 128), np.float32)
    for r in range(8):
        sel[r, r, :] = 1.0
    cm = np.zeros((128, 6, 128), np.float32)
    cm[:, 5, :] = (f > p)
    cm[:, 0, :] = (p // 8 == f // 8)
    for li, bsz in enumerate((8, 16, 32, 64)):
        cm[:, li + 1, :] = (p // (2 * bsz) == f // (2 * bsz)) & (p // bsz != f // bsz)
    return ident, masks, sgum, sel, cm.astype(ml_dtypes.bfloat16)


_NC_CACHE = {}


def make_in_maps(inputs, L):
    f = lambda a: np.ascontiguousarray(np.asarray(a, dtype=np.float32))
    ident, masks, sgum, sel, cm = _consts()
    x_prompt = f(inputs["x_prompt"]); x_sample = f(inputs["x_sample"])
    nb = x_prompt.shape[0]
    rowsA = np.concatenate([
        f(inputs["norm_mix_g"]).reshape(DEPTH, 8, 128), f(inputs["norm_ffn_g"]).reshape(DEPTH, 8, 128),
        f(inputs["ada_b"]).reshape(DEPTH, 48, 128), f(inputs["conv_w"]).reshape(DEPTH, 48, 128),
        f(inputs["dn_norm_g"]).reshape(DEPTH, 1, 128)], axis=1)
    shared = dict(
        ada_w=f(inputs["ada_w"]), rowsA=np.ascontiguousarray(rowsA), w_in=f(inputs["w_in"]),
        sgu_norm_g=f(inputs["sgu_norm_g"]), sgu_w=f(inputs["sgu_w"]), sgu_b=f(inputs["sgu_b"]).reshape(DEPTH, 512),
        dt_bias=f(inputs["dt_bias"]), a_log=f(inputs["a_log"]), w_out=f(inputs["w_out"]), w_up=f(inputs["w_up"]),
        w_down=f(inputs["w_down"]), fin_g=f(inputs["final_norm_g"]).reshape(1, D),
        k_ident=ident, k_masks=masks, k_sgum=sgum, k_cm=cm)
    in_maps = []
    for i in range(nb):
        m = dict(shared)
        m["x_p"] = np.ascontiguousarray(x_prompt[i, :L]); m["x_s"] = x_sample[i]
        m["c_ps"] = np.ascontiguousarray(np.concatenate([f(inputs["c_prompt"])[i].reshape(8, 128), f(inputs["c_sample"])[i].reshape(8, 128)], 0))
        m["st_conv"] = np.ascontiguousarray(f(inputs["state_conv"])[:, i].reshape(DEPTH, 36, 128))
        m["st_delta"] = np.ascontiguousarray(f(inputs["state_delta"])[:, i])
        in_maps.append(m)
    return in_maps


def run(inputs, L=2048, dbg=None, core_ids=None):
    key = (L, tuple(sorted((dbg or {}).items())))
    if key not in _NC_CACHE:
        _NC_CACHE[key] = build(L, dbg)
    nc = _NC_CACHE[key]
    in_maps = make_in_maps(inputs, L)
    if core_ids is not None:
        in_maps = [in_maps[i] for i in core_ids]
    res = run_bass_kernel_spmd(nc, in_maps, core_ids=list(range(len(in_maps))))
    return res.results


def kernel(**inputs):
    r = run(inputs, 2048)
    st = lambda k: np.stack([np.asarray(x[k], dtype=np.float32) for x in r], axis=0)
    y_prompt = st("y_p"); y_sample = st("y_s")
    pconv = np.ascontiguousarray(st("o_pconv").transpose(1, 0, 2, 3))
    pdelta = np.ascontiguousarray(st("o_pdelta").transpose(1, 0, 2, 3, 4))
    sconv = np.ascontiguousarray(st("o_sconv").transpose(1, 0, 2, 3))
    sdelta = np.ascontiguousarray(st("o_sdelta").transpose(1, 0, 2, 3, 4))
    sgu = np.ascontiguousarray(st("o_sgu").transpose(1, 0, 2, 3))
    return (y_prompt, y_sample, pconv, pdelta, sconv, sdelta, sgu)
```
